# Optimizing a Trainium2 kernel written in Bass

```python
import math
import jax, jax.numpy as jnp
from jax import lax
import numpy as np

D_MODEL = 4096
BATCH = 4
SEQ = 2048
DEPTH = 1

RET_HEADS = 8
RET_DIM = 256
RET_WIDTH = RET_HEADS * RET_DIM
RET_CHUNK = 128
NSA_HEADS = 16
NSA_KV_GROUPS = 4
NSA_REP = NSA_HEADS // NSA_KV_GROUPS
NSA_DIM = 128
NSA_WIDTH = NSA_HEADS * NSA_DIM
NSA_KV_WIDTH = NSA_KV_GROUPS * NSA_DIM
CMP_LEN = 32
CMP_STRIDE = 16
CMP_HIDDEN = 128
SLC_LEN = 64
SLC_TOPK = 16
SLC_QBLOCK = 32
WIN_LEN = 512
WIN_QBLOCK = 128
MIX_WIDTH = RET_WIDTH + NSA_WIDTH
IN_SIZES = (RET_WIDTH, RET_WIDTH, RET_WIDTH, RET_WIDTH,
            NSA_WIDTH, NSA_KV_WIDTH, NSA_KV_WIDTH, NSA_KV_WIDTH,
            NSA_KV_WIDTH, NSA_KV_WIDTH, NSA_KV_WIDTH, 3 * NSA_HEADS)
IN_WIDTH = sum(IN_SIZES)
MEM_LEN = 256
XA_HEADS = 4
XA_DIM = 128
XA_WIDTH = XA_HEADS * XA_DIM
D_FF = 4 * D_MODEL
LN_EPS = 1e-5
DN_ALPHA = (2.0 * DEPTH) ** 0.25
DN_BETA = (8.0 * DEPTH) ** -0.25
NEG_INF = -1e30
FORCE = 1e9

kernel_name = "hybrid_retention_nsa_deepnorm"


def layer_norm(x, g, b):
    xf = x.astype(jnp.float32)
    mu = jnp.mean(xf, axis=-1, keepdims=True)
    var = jnp.mean(jnp.square(xf - mu), axis=-1, keepdims=True)
    return ((xf - mu) * lax.rsqrt(var + LN_EPS) * g + b).astype(x.dtype)


def alibi_slopes(n):
    return 2.0 ** (-8.0 * jnp.arange(1, n + 1, dtype=jnp.float32) / n)


def retention(q, k, v):
    f32 = jnp.float32
    b, t, _ = q.shape
    n = t // RET_CHUNK

    def split(z):
        return z.astype(f32).reshape(b, n, RET_CHUNK, RET_HEADS, RET_DIM).transpose(0, 3, 1, 2, 4)

    q, k, v = split(q), split(k) * RET_DIM ** -0.5, split(v)
    log_g = jnp.log1p(-(2.0 ** (-5.0 - jnp.arange(RET_HEADS, dtype=f32))))
    pos = jnp.arange(RET_CHUNK, dtype=f32)
    diff = pos[:, None] - pos[None, :]
    decay = jnp.where(diff >= 0, jnp.exp(log_g[:, None, None] * jnp.maximum(diff, 0.0)), 0.0)
    scores = jnp.einsum('bhncd,bhnsd->bhncs', q, k) * decay[None, :, None]
    o_inner = jnp.einsum('bhncs,bhnse->bhnce', scores, v)
    k_tail = k * jnp.exp(log_g[:, None] * (RET_CHUNK - 1 - pos))[None, :, None, :, None]
    kv = jnp.einsum('bhncd,bhnce->nbhde', k_tail, v)
    chunk_decay = jnp.exp(log_g * RET_CHUNK)[None, :, None, None]

    def step(state, kv_n):
        return chunk_decay * state + kv_n, state

    _, states = lax.scan(step, jnp.zeros(kv.shape[1:], f32), kv)
    q_head = q * jnp.exp(log_g[:, None] * (pos + 1.0))[None, :, None, :, None]
    o = o_inner + jnp.einsum('bhncd,nbhde->bhnce', q_head, states)
    mu = jnp.mean(o, axis=-1, keepdims=True)
    var = jnp.mean(jnp.square(o - mu), axis=-1, keepdims=True)
    o = (o - mu) * lax.rsqrt(var + LN_EPS)
    return o.transpose(0, 2, 3, 1, 4).reshape(b, t, RET_WIDTH)


def compress(z, pos_emb, w1, w2):
    b, g, t, d = z.shape
    n_sub = CMP_LEN // CMP_STRIDE
    nc = t // CMP_STRIDE - n_sub + 1
    sub = z.reshape(b, g, t // CMP_STRIDE, CMP_STRIDE, d)
    blocks = jnp.concatenate([sub[:, :, i:i + nc] for i in range(n_sub)], axis=3) + pos_emb
    hid = jax.nn.gelu(blocks.reshape(b, g, nc, CMP_LEN * d) @ w1)
    return (hid @ w2).astype(jnp.float32)


def nsa(q, k_cmp, v_cmp, k_slc, v_slc, k_win, v_win, gates, cmp_pos, cmp_w1, cmp_w2):
    f32 = jnp.float32
    b, t, _ = q.shape
    G, R, dh = NSA_KV_GROUPS, NSA_REP, NSA_DIM
    q = q.astype(f32).reshape(b, t, G, R, dh).transpose(0, 2, 3, 1, 4) * dh ** -0.5

    def kv_heads(z):
        return z.astype(f32).reshape(b, t, G, dh).transpose(0, 2, 1, 3)

    slopes = alibi_slopes(NSA_HEADS).reshape(G, R)
    tpos = jnp.arange(t)

    kc = compress(kv_heads(k_cmp), cmp_pos[0], cmp_w1[0], cmp_w2[0])
    vc = compress(kv_heads(v_cmp), cmp_pos[1], cmp_w1[1], cmp_w2[1])
    nc = kc.shape[2]
    c_start = jnp.arange(nc) * CMP_STRIDE
    c_dist = (tpos[:, None] - (c_start + CMP_LEN - 1)[None, :]).astype(f32)
    c_mask = c_dist >= 0
    s = jnp.einsum('bgrtd,bgcd->bgrtc', q, kc) - slopes[:, :, None, None] * c_dist
    p_cmp = jnp.where(c_mask, jax.nn.softmax(jnp.where(c_mask, s, NEG_INF), axis=-1), 0.0)
    o_cmp = jnp.einsum('bgrtc,bgcd->bgrtd', p_cmp, vc)

    ns = t // SLC_LEN
    s_start = jnp.arange(ns) * SLC_LEN
    overlap = jnp.clip(jnp.minimum(c_start[:, None] + CMP_LEN, s_start[None, :] + SLC_LEN)
                       - jnp.maximum(c_start[:, None], s_start[None, :]), 0, None).astype(f32) / CMP_LEN
    imp = jnp.einsum('bgrtc,cs->bgts', p_cmp, overlap)
    blk = jnp.arange(ns)[None, :]
    cur = (tpos // SLC_LEN)[:, None]
    forced = (blk == 0) | (blk == cur) | (blk == cur - 1)
    imp = jnp.where(forced, FORCE, jnp.where(blk > cur, -FORCE, imp))
    n_top = min(SLC_TOPK, ns)
    _, idx = lax.top_k(imp, n_top)

    ks = kv_heads(k_slc).reshape(b, G, ns, SLC_LEN, dh)
    vs = kv_heads(v_slc).reshape(b, G, ns, SLC_LEN, dh)
    nq = t // SLC_QBLOCK
    q_blocks = q.reshape(b, G, R, nq, SLC_QBLOCK, dh).transpose(3, 0, 1, 2, 4, 5)
    idx_blocks = idx.reshape(b, G, nq, SLC_QBLOCK, n_top).transpose(2, 0, 1, 3, 4)
    starts = jnp.arange(nq) * SLC_QBLOCK
    bi = jnp.arange(b)[:, None, None, None]
    gi = jnp.arange(G)[None, :, None, None]

    def slc_block(args):
        qb, ib, start = args
        kg = ks[bi, gi, ib]
        vg = vs[bi, gi, ib]
        kpos = ib[..., None] * SLC_LEN + jnp.arange(SLC_LEN)
        qpos = start + jnp.arange(SLC_QBLOCK)
        dist = (qpos[None, None, :, None, None] - kpos).astype(f32)[:, :, None]
        sc = jnp.einsum('bgrqd,bgqkld->bgrqkl', qb, kg) - slopes[None, :, :, None, None, None] * dist
        sc = jnp.where(dist >= 0, sc, NEG_INF).reshape(b, G, R, SLC_QBLOCK, n_top * SLC_LEN)
        pr = jax.nn.softmax(sc, axis=-1).reshape(b, G, R, SLC_QBLOCK, n_top, SLC_LEN)
        return jnp.einsum('bgrqkl,bgqkld->bgrqd', pr, vg)

    o_slc = lax.map(slc_block, (q_blocks, idx_blocks, starts))
    o_slc = o_slc.transpose(1, 2, 3, 0, 4, 5).reshape(b, G, R, t, dh)

    nb = t // WIN_QBLOCK
    nband = WIN_LEN // WIN_QBLOCK + 1

    def band(z):
        zp = jnp.pad(z, ((0, 0), (0, 0), (WIN_LEN, 0), (0, 0))).reshape(b, G, nb + nband - 1, WIN_QBLOCK, dh)
        return jnp.concatenate([zp[:, :, i:i + nb] for i in range(nband)], axis=3)

    kb, vb = band(kv_heads(k_win)), band(kv_heads(v_win))
    qw = q.reshape(b, G, R, nb, WIN_QBLOCK, dh)
    qpos = jnp.arange(nb)[:, None] * WIN_QBLOCK + jnp.arange(WIN_QBLOCK)
    kpos = jnp.arange(nb)[:, None] * WIN_QBLOCK - WIN_LEN + jnp.arange(nband * WIN_QBLOCK)
    dist = qpos[:, :, None] - kpos[:, None, :]
    w_mask = (dist >= 0) & (dist < WIN_LEN) & (kpos[:, None, :] >= 0)
    sw = jnp.einsum('bgrnqd,bgnkd->bgrnqk', qw, kb) - slopes[:, :, None, None, None] * dist.astype(f32)
    pw = jax.nn.softmax(jnp.where(w_mask, sw, NEG_INF), axis=-1)
    o_win = jnp.einsum('bgrnqk,bgnkd->bgrnqd', pw, vb).reshape(b, G, R, t, dh)

    gt = jax.nn.sigmoid(gates.astype(f32)).reshape(b, t, 3, G, R).transpose(2, 0, 3, 4, 1)[..., None]
    o = gt[0] * o_cmp + gt[1] * o_slc + gt[2] * o_win
    return o.transpose(0, 3, 1, 2, 4).reshape(b, t, NSA_WIDTH)


def memory_attention(h, mem, wq, wkv, wo):
    b, t, _ = h.shape
    m = mem.shape[1]
    q = (h @ wq).reshape(b, t, XA_HEADS, XA_DIM)
    kv = (mem @ wkv).reshape(b, m, 2, XA_HEADS, XA_DIM)
    s = jnp.einsum('bthd,bmhd->bhtm', q, kv[:, :, 0]).astype(jnp.float32) * XA_DIM ** -0.5
    p = jax.nn.softmax(s, axis=-1).astype(h.dtype)
    o = jnp.einsum('bhtm,bmhd->bthd', p, kv[:, :, 1]).reshape(b, t, XA_WIDTH)
    return o @ wo


def setup_inputs(seed: int = 0) -> dict:
    key = jax.random.key(seed)
    ks = jax.random.split(key, 16)
    nrm = lambda k, shape, scale: jax.random.normal(k, shape, jnp.float32) * scale
    return {
        "x": nrm(ks[0], (BATCH, SEQ, D_MODEL), 1.0),
        "mem": nrm(ks[1], (BATCH, MEM_LEN, D_MODEL), 1.0),
        "w_in": nrm(ks[2], (DEPTH, D_MODEL, IN_WIDTH), D_MODEL ** -0.5),
        "w_out": nrm(ks[3], (DEPTH, MIX_WIDTH, D_MODEL), MIX_WIDTH ** -0.5 * DN_BETA),
        "cmp_pos": nrm(ks[4], (DEPTH, 2, CMP_LEN, NSA_DIM), 0.1),
        "cmp_w1": nrm(ks[5], (DEPTH, 2, CMP_LEN * NSA_DIM, CMP_HIDDEN), (CMP_LEN * NSA_DIM) ** -0.5),
        "cmp_w2": nrm(ks[6], (DEPTH, 2, CMP_HIDDEN, NSA_DIM), CMP_HIDDEN ** -0.5),
        "xa_wq": nrm(ks[7], (DEPTH, D_MODEL, XA_WIDTH), D_MODEL ** -0.5),
        "xa_wkv": nrm(ks[8], (DEPTH, D_MODEL, 2 * XA_WIDTH), D_MODEL ** -0.5),
        "xa_wo": nrm(ks[9], (DEPTH, XA_WIDTH, D_MODEL), XA_WIDTH ** -0.5 * DN_BETA),
        "w_ff1": nrm(ks[10], (DEPTH, D_MODEL, D_FF), D_MODEL ** -0.5),
        "w_ff2": nrm(ks[11], (DEPTH, D_FF, D_MODEL), D_FF ** -0.5 * DN_BETA),
        "ln_g": 1.0 + nrm(ks[12], (DEPTH, 3, D_MODEL), 0.02),
        "ln_b": nrm(ks[13], (DEPTH, 3, D_MODEL), 0.02),
    }


def reference(x, mem, w_in, w_out, cmp_pos, cmp_w1, cmp_w2, xa_wq, xa_wkv, xa_wo,
              w_ff1, w_ff2, ln_g, ln_b):
    offsets = np.cumsum(IN_SIZES)[:-1].tolist()
    h = x
    for l in range(DEPTH):
        proj = h @ w_in[l]
        (rq, rk, rv, rg, nq, kc, vc, ksl, vsl, kw, vw, gates) = jnp.split(proj, offsets, axis=-1)
        ret = retention(rq, rk, rv) * jax.nn.silu(rg.astype(jnp.float32))
        sparse = nsa(nq, kc, vc, ksl, vsl, kw, vw, gates, cmp_pos[l], cmp_w1[l], cmp_w2[l])
        mix = jnp.concatenate([ret, sparse], axis=-1).astype(h.dtype) @ w_out[l]
        h = layer_norm(DN_ALPHA * h + mix, ln_g[l, 0], ln_b[l, 0])
        h = layer_norm(DN_ALPHA * h + memory_attention(h, mem, xa_wq[l], xa_wkv[l], xa_wo[l]),
                       ln_g[l, 1], ln_b[l, 1])
        ff = jnp.square(jax.nn.relu(h @ w_ff1[l])) @ w_ff2[l]
        h = layer_norm(DN_ALPHA * h + ff, ln_g[l, 2], ln_b[l, 2])
    return h
```

```python
import contextlib
import math
import numpy as np
import ml_dtypes
import concourse.bass as bass
import concourse.mybir as mybir
from concourse.bass_utils import run_bass_kernel_spmd

F32, BF16 = mybir.dt.float32, mybir.dt.bfloat16
AF = mybir.ActivationFunctionType
ALU = mybir.AluOpType
AX = mybir.AxisListType

D = 4096
T_OWN = 1024
TB = 512
NKC = 32
ALPHA = 2.0 ** 0.25
EPS = 1e-5
IN_W = 13360
DFF = 16384
NEGM = -30000.0


class Reg:
    __slots__ = ("w", "r", "name")

    def __init__(self, name=""):
        self.w = None
        self.r = {}
        self.name = name


class Prog:
    ENG = ("pe", "act", "dve", "pool", "sp")

    def __init__(self, nc, stack, ndma=8):
        self.nc = nc
        self.q = {e: [] for e in self.ENG}
        self.sem = {}
        self.cnt = {}
        self.seen = {e: {} for e in self.ENG}
        self.fence = {e: {} for e in self.ENG}
        self.rr = {e: 0 for e in self.ENG}
        self.ndma = ndma
        for e in ("pe", "act", "dve", "pool"):
            self.sem[e] = stack.enter_context(nc.semaphore("s_" + e))
            self.cnt[e] = 0
        for e in ("sp", "pool", "act"):
            for i in range(ndma):
                k = ("d", e, i)
                self.sem[k] = stack.enter_context(nc.semaphore("d_%s_%d" % (e, i)))
                self.cnt[k] = 0

    def _waits(self, e, reads, writes):
        need = dict(self.fence[e])
        self.fence[e] = {}

        def add(k, v):
            if need.get(k, 0) < v:
                need[k] = v

        for r in reads:
            if r.w is not None:
                add(*r.w)
        for w in writes:
            if w.w is not None:
                add(*w.w)
            for k, v in w.r.items():
                add(k, v)
        out = []
        for k, v in need.items():
            if k == e and e == "pe":
                continue
            if self.seen[e].get(k, 0) >= v:
                continue
            self.seen[e][k] = v
            out.append((k, v))
        return out

    def _post(self, tok, reads, writes):
        for r in reads:
            if r.r.get(tok[0], 0) < tok[1]:
                r.r[tok[0]] = tok[1]
        for w in writes:
            w.w = tok
            w.r = {}

    def op(self, e, fn, reads=(), writes=(), sig=True):
        waits = self._waits(e, reads, writes)
        if sig:
            self.cnt[e] += 1
            tok = (e, self.cnt[e])
        else:
            tok = (e, self.cnt[e] + 1)
        sem = self.sem

        def run(eng):
            for k, v in waits:
                eng.wait_ge(sem[k], v)
            ins = fn(eng)
            if sig:
                ins.then_inc(sem[e], 1)

        self.q[e].append(run)
        self._post(tok, reads, writes)
        return tok

    def dma(self, e, out, in_, reads=(), writes=()):
        i = self.rr[e]
        self.rr[e] = (i + 1) % self.ndma
        key = ("d", e, i)
        waits = self._waits(e, reads, writes)
        prev = self.cnt[key]
        if prev > 0 and self.seen[e].get(key, 0) < prev:
            waits.append((key, prev))
            self.seen[e][key] = prev
        self.cnt[key] += 16
        tok = (key, self.cnt[key])
        sem = self.sem

        def run(eng):
            for k, v in waits:
                eng.wait_ge(sem[k], v)
            eng.dma_start(out=out, in_=in_).then_inc(sem[key], 16)

        self.q[e].append(run)
        self._post(tok, reads, writes)
        return tok

    def barrier(self):
        allt = {k: v for k, v in self.cnt.items() if v > 0}
        for e in self.ENG:
            for k, v in allt.items():
                if self.fence[e].get(k, 0) < v:
                    self.fence[e][k] = v

    def final_wait(self, e):
        self.barrier()
        waits = self._waits(e, (), ())
        sem = self.sem

        def run(eng):
            for k, v in waits:
                eng.wait_ge(sem[k], v)

        self.q[e].append(run)

    def replay(self):
        nc = self.nc
        q = self.q
        with nc.Block() as block:
            @block.tensor
            def _(eng):
                for f in q["pe"]:
                    f(eng)

            @block.scalar
            def _(eng):
                for f in q["act"]:
                    f(eng)

            @block.vector
            def _(eng):
                for f in q["dve"]:
                    f(eng)

            @block.gpsimd
            def _(eng):
                for f in q["pool"]:
                    f(eng)

            @block.sync
            def _(eng):
                for f in q["sp"]:
                    f(eng)


class SbAlloc:
    def __init__(self, nc, nbytes):
        self.words = nbytes // 4
        self.t = nc.alloc_sbuf_tensor("SB", [128, self.words], F32)
        self.off = 0

    def take(self, dims, dt, parts=128):
        n = int(np.prod(dims))
        sz = 2 if dt == BF16 else 4
        words = (n * sz + 3) // 4
        words = (words + 7) // 8 * 8
        assert self.off + words <= self.words, ("SBUF overflow", self.off, words, self.words)
        ap = self.t[0:parts, self.off:self.off + words]
        self.off += words
        if dt != F32:
            ap = ap.bitcast(dt)
        ap = ap[:, 0:n]
        if len(dims) == 2:
            ap = ap.rearrange("p (a b) -> p a b", a=dims[0])
        elif len(dims) == 3:
            ap = ap.rearrange("p (a b c) -> p a b c", a=dims[0], b=dims[1])
        elif len(dims) == 4:
            ap = ap.rearrange("p (a b c d) -> p a b c d", a=dims[0], b=dims[1], c=dims[2])
        return ap

    def mark(self):
        return self.off

    def reset(self, m):
        self.off = m


def build(cfg):
    nc = bass.Bass("TRN2", target_bir_lowering=False)
    dram = {}

    def din(name, shape, dt=F32):
        dram[name] = nc.dram_tensor(name, list(shape), dt, kind="ExternalInput").ap()
        return dram[name]

    if not cfg.get("dbgA"):
        x_own = din("x_own", [T_OWN, D])
        memT = din("memT", [D, 256])
        w_out_t = din("w_out_t", [8, 128, NKC * 512])
        xa_wq_t = din("xa_wq_t", [4, 128, NKC * 128])
        xa_wkv_t = din("xa_wkv_t", [8, 128, NKC * 128])
        xa_wo = din("xa_wo", [512, D])
        w_ff1_t = din("w_ff1_t", [DFF // 128, 128, NKC * 128])
        w_ff2 = din("w_ff2", [DFF, D])
        ln_g = din("ln_g", [3, D])
        ln_b = din("ln_b", [3, D])
    ident_d = din("ident", [128, 128])
    if cfg.get("phaseA", True):
        din("xT_prev", [D, 1024]); din("xT_own", [D, 1024])
        w_in_t = din("w_in_t", [104, 128, NKC * 128]); w_gates = din("w_gates", [128, NKC, 48])
        din("cmp_w1", [2, D, 128]); din("cmp_w2", [2, 128, 128]); din("cmp_posT", [2, 128, 32])
        din("vc_init", [64, 4, 2, 162], BF16)
        din("ret_rs1", [128, 8, 8]); din("ret_decT", [128, 8, 128]); din("ret_gq", [128, 8, 128]); din("ret_rs2", [128, 8])
        din("cmp_mask", [64, 2, 8, 128]); din("cmp_bias", [64, 2, 16, 8])
        din("imp_keep", [128, 8, 32]); din("imp_add", [128, 8, 32]); din("slc_bias", [128, 16, 16, 8])
        din("e2", [32, 16, 128]); din("tri", [128, 2, 128]); din("tri4", [128, 2, 4, 128])
    else:
        mix_in = din("mixT_in", [D, T_OWN], BF16)
    if not cfg.get("dbgA"):
        y = nc.dram_tensor("y", [T_OWN, D], F32, kind="ExternalOutput").ap()
    mixT_scr = nc.dram_tensor("mixT_scr", [D, T_OWN], BF16, kind=("ExternalOutput" if cfg.get("dbgA") else "Internal")).ap()

    stack = contextlib.ExitStack()
    with stack:
        P = Prog(nc, stack)
        SB = SbAlloc(nc, 206 * 1024)
        PS = nc.alloc_psum_tensor("PS", [128, 8, 512], F32)
        bankreg = [Reg("bank%d" % i) for i in range(8)]
        bank_rr = [0]

        def nextbank(lo=0, hi=8):
            b = lo + bank_rr[0] % (hi - lo)
            bank_rr[0] += 1
            return b

        def psf(b):
            return PS[:, b, :]

        def psb(b):
            return PS[:, b, :].bitcast(BF16)

        ident = SB.take([128], BF16)
        ident_r = Reg("ident")
        P.dma("pool", ident, ident_d, writes=[ident_r])
        kT = SB.take([4, 256], BF16)
        vaug = SB.take([2, 4, 130], BF16)
        kT_r, vaug_r = Reg("kT"), Reg("vaug")
        small = SB.take([64], F32)
        small_r = Reg("small")
        base_mark = SB.mark()

        def wview(ap, p=128):
            return ap.rearrange("(kc p) n -> p kc n", p=p)

        def mm_group(out_ap, pairs, reads, bank, extra_writes=()):
            n = len(pairs)
            for i, (l, r) in enumerate(pairs):
                P.op("pe", (lambda eng, l=l, r=r, i=i: eng.matmul(out_ap, l, r, start=(i == 0), stop=(i == n - 1))),
                     reads=reads, writes=[bankreg[bank]] + list(extra_writes), sig=(i == n - 1))

        def transpose_to(dst_aps, src_aps, src_regs, dst_reg, copy_eng):
            b = nextbank()
            n = len(src_aps)
            pb = psb(b)
            for j, s in enumerate(src_aps):
                P.op("pe", (lambda eng, s=s, j=j: eng.transpose(pb[:, j * 128:(j + 1) * 128], s, ident)),
                     reads=list(src_regs) + [ident_r], writes=[bankreg[b]], sig=(j == n - 1))
            src = pb[:, 0:n * 128].rearrange("p (a b) -> p a b", a=n)
            if copy_eng == "act":
                P.op("act", lambda eng: eng.copy(dst_aps, src), reads=[bankreg[b]], writes=[dst_reg])
            else:
                P.op("dve", lambda eng: eng.tensor_copy(dst_aps, src), reads=[bankreg[b]], writes=[dst_reg])

        def xattn_kv():
            m = SB.mark()
            memb = SB.take([NKC, 256], BF16)
            memb_r = Reg("memb")
            P.dma("pool", memb, wview(memT), writes=[memb_r])
            wr = [SB.take([NKC, 128], BF16) for _ in range(2)]
            wr_r = [Reg("wkvr0"), Reg("wkvr1")]
            P.op("dve", lambda eng: eng.memset(vaug, 1.0), writes=[vaug_r])
            for j in range(8):
                s = j % 2
                P.dma("pool", wr[s], xa_wkv_t[j].rearrange("p (k n) -> p k n", k=NKC), writes=[wr_r[s]])
                if j < 4:
                    b = nextbank()
                    mm_group(psf(b)[:, 0:256], [(wr[s][:, kc, :], memb[:, kc, :]) for kc in range(NKC)],
                             [wr_r[s], memb_r], b)
                    P.op("act", lambda eng, b=b, j=j: eng.copy(kT[:, j, :], psf(b)[:, 0:256]),
                         reads=[bankreg[b]], writes=[kT_r])
                else:
                    hd = j - 4
                    for mt in range(2):
                        b = nextbank()
                        mm_group(psf(b)[:, 0:128],
                                 [(memb[:, kc, mt * 128:(mt + 1) * 128], wr[s][:, kc, :]) for kc in range(NKC)],
                                 [wr_r[s], memb_r], b)
                        P.op("act", lambda eng, b=b, mt=mt, hd=hd: eng.copy(vaug[:, mt, hd, 0:128], psf(b)[:, 0:128]),
                             reads=[bankreg[b]], writes=[vaug_r])
            P.barrier()
            SB.reset(m)

        def layer_norm(H, Hr, HT, HT_r, GB, GB_r, HB, HB_r, stats, stats_r, do_T=True):
            for tt in range(4):
                h = H[:, tt, :]
                st = stats[:, tt, :]
                for c in range(8):
                    P.op("dve", lambda eng, c=c, h=h, st=st: eng.bn_stats(st[:, c * 6:(c + 1) * 6], h[:, c * 512:(c + 1) * 512]),
                         reads=[Hr[tt]], writes=[stats_r[tt]])
                P.op("dve", lambda eng, st=st: eng.bn_aggr(st[:, 48:50], st[:, 0:48]), reads=[stats_r[tt]], writes=[stats_r[tt]])
            for tt in range(4):
                st = stats[:, tt, :]
                P.op("act", lambda eng, st=st: eng.activation(st[:, 50:51], st[:, 49:50], AF.Sqrt, bias=EPS_AP[0], scale=1.0),
                     reads=[stats_r[tt], small_r], writes=[stats_r[tt]])
                P.op("dve", lambda eng, st=st: eng.reciprocal(st[:, 50:51], st[:, 50:51]), reads=[stats_r[tt]], writes=[stats_r[tt]])
                P.op("dve", lambda eng, st=st: eng.scalar_tensor_tensor(
                    out=st[:, 51:52], in0=st[:, 48:49], scalar=-1.0, in1=st[:, 50:51], op0=ALU.mult, op1=ALU.mult),
                    reads=[stats_r[tt]], writes=[stats_r[tt]])
            for tt in range(4):
                h = H[:, tt, :]
                st = stats[:, tt, :]
                P.op("act", lambda eng, h=h, st=st: eng.activation(h, h, AF.Identity, bias=st[:, 51:52], scale=st[:, 50:51]),
                     reads=[stats_r[tt], Hr[tt]], writes=[Hr[tt]])
            for tt in range(4):
                h = H[:, tt, :]
                P.op("dve", lambda eng, h=h: eng.tensor_tensor(h, h, GB[:, 0, :], ALU.mult), reads=[Hr[tt], GB_r], writes=[Hr[tt]])
                P.op("pool", lambda eng, h=h: eng.tensor_tensor(h, h, GB[:, 1, :], ALU.add), reads=[Hr[tt], GB_r], writes=[Hr[tt]])
                if do_T:
                    P.op("act", lambda eng, h=h: eng.copy(HB, h), reads=[Hr[tt]], writes=[HB_r])
                    for q4 in range(4):
                        transpose_to(HT[:, q4 * 8:(q4 + 1) * 8, tt * 128:(tt + 1) * 128],
                                     [HB[:, (q4 * 8 + j) * 128:(q4 * 8 + j + 1) * 128] for j in range(8)],
                                     [HB_r], HT_r, "act" if q4 % 2 else "dve")

        EPS_AP = [None]

        def load_gb(GB, GB_r, i):
            P.dma("sp", GB[:, 0, :], ln_g[i:i + 1, :].partition_broadcast(128), writes=[GB_r])
            P.dma("sp", GB[:, 1, :], ln_b[i:i + 1, :].partition_broadcast(128), writes=[GB_r])

        def token_block(tb, mix_src):
            m0 = SB.mark()
            H = SB.take([4, D], F32)
            Hr = [Reg("H%d" % i) for i in range(4)]
            HT = SB.take([NKC, TB], BF16)
            HT_r = Reg("HT")
            mX = SB.mark()

            def take_ln():
                return (SB.take([2, D], F32), Reg("GB"), SB.take([D], BF16), Reg("HB"), SB.take([4, 64], F32), [Reg("stats%d" % i) for i in range(4)])

            GB, GB_r, HB, HB_r, stats, stats_r = take_ln()
            t0 = tb * TB
            P.dma("sp", HT, wview(mix_src)[:, :, t0:t0 + TB], reads=([mscr_r] if mscr_r is not None else []), writes=[HT_r])
            for tt in range(4):
                P.dma("sp", H[:, tt, :], x_own[t0 + tt * 128:t0 + (tt + 1) * 128, :], writes=[Hr[tt]])
            load_gb(GB, GB_r, 0)
            wr = [SB.take([NKC, 512], BF16) for _ in range(2)]
            wr_r = [Reg("wo0"), Reg("wo1")]
            for nt in range(8):
                s = nt % 2
                P.dma("pool", wr[s], w_out_t[nt].rearrange("p (k n) -> p k n", k=NKC), writes=[wr_r[s]])
                for tt in range(4):
                    b = nextbank()
                    mm_group(psf(b), [(HT[:, kc, tt * 128:(tt + 1) * 128], wr[s][:, kc, :]) for kc in range(NKC)],
                             [HT_r, wr_r[s]], b)
                    hs = H[:, tt, nt * 512:(nt + 1) * 512]
                    P.op("dve", lambda eng, hs=hs, b=b: eng.scalar_tensor_tensor(
                        out=hs, in0=hs, scalar=ALPHA, in1=psf(b), op0=ALU.mult, op1=ALU.add),
                        reads=[bankreg[b], Hr[tt]], writes=[Hr[tt]])
            layer_norm(H, Hr, HT, HT_r, GB, GB_r, HB, HB_r, stats, stats_r)
            P.barrier()
            SB.reset(mX)
            GB, GB_r, HB, HB_r, stats, stats_r = take_ln()
            load_gb(GB, GB_r, 1)
            qT = SB.take([4, TB], BF16)
            qT_r = Reg("qT")
            pT = [SB.take([2, TB], BF16) for _ in range(2)]
            pT_r = [Reg("pT0"), Reg("pT1")]
            otok = SB.take([4, 512], BF16)
            otok_r = Reg("otok")
            oT = SB.take([4, TB], BF16)
            oT_r = Reg("oT")
            rc = SB.take([8], F32)
            rc_r = Reg("rc")
            wq = [SB.take([NKC, 128], BF16) for _ in range(2)]
            wq_r = [Reg("wq0"), Reg("wq1")]
            wo = [SB.take([4, 512], BF16) for _ in range(2)]
            wo_r = [Reg("wo0"), Reg("wo1")]
            for hd in range(4):
                s = hd % 2
                P.dma("pool", wq[s], xa_wq_t[hd].rearrange("p (k n) -> p k n", k=NKC), writes=[wq_r[s]])
                b = nextbank()
                mm_group(psf(b), [(wq[s][:, kc, :], HT[:, kc, :]) for kc in range(NKC)], [wq_r[s], HT_r], b)
                P.op("act", lambda eng, b=b, hd=hd: eng.activation(qT[:, hd, :], psf(b), AF.Copy, scale=128.0 ** -0.5),
                     reads=[bankreg[b]], writes=[qT_r])
            for hd in range(4):
                s = hd % 2
                for mt in range(2):
                    b = nextbank()
                    mm_group(psf(b), [(kT[:, hd, mt * 128:(mt + 1) * 128], qT[:, hd, :])], [kT_r, qT_r], b)
                    P.op("act", lambda eng, b=b, mt=mt, s=s: eng.activation(pT[s][:, mt, :], psf(b), AF.Exp),
                         reads=[bankreg[b]], writes=[pT_r[s]])
                for tt in range(4):
                    b = nextbank()
                    mm_group(psf(b)[:, 0:130],
                             [(pT[s][:, mt, tt * 128:(tt + 1) * 128], vaug[:, mt, hd, :]) for mt in range(2)],
                             [pT_r[s], vaug_r], b)
                    rcs = rc[:, tt:tt + 1]
                    P.op("dve", lambda eng, b=b, rcs=rcs: eng.reciprocal(rcs, psf(b)[:, 128:129]),
                         reads=[bankreg[b]], writes=[rc_r])
                    P.op("dve", lambda eng, b=b, rcs=rcs, tt=tt, hd=hd: eng.tensor_scalar(
                        otok[:, tt, hd * 128:(hd + 1) * 128], psf(b)[:, 0:128], rcs, None, ALU.mult),
                        reads=[bankreg[b], rc_r], writes=[otok_r])
            for tt in range(4):
                transpose_to(oT[:, 0:4, tt * 128:(tt + 1) * 128],
                             [otok[:, tt, hd * 128:(hd + 1) * 128] for hd in range(4)], [otok_r], oT_r, "act")
            for nt in range(8):
                s = nt % 2
                P.dma("pool", wo[s], xa_wo.rearrange("(h p) n -> p h n", p=128)[:, :, nt * 512:(nt + 1) * 512],
                      writes=[wo_r[s]])
                for tt in range(4):
                    b = nextbank()
                    mm_group(psf(b), [(oT[:, hd, tt * 128:(tt + 1) * 128], wo[s][:, hd, :]) for hd in range(4)],
                             [oT_r, wo_r[s]], b)
                    hs = H[:, tt, nt * 512:(nt + 1) * 512]
                    P.op("dve", lambda eng, hs=hs, b=b: eng.scalar_tensor_tensor(
                        out=hs, in0=hs, scalar=ALPHA, in1=psf(b), op0=ALU.mult, op1=ALU.add),
                        reads=[bankreg[b], Hr[tt]], writes=[Hr[tt]])
            layer_norm(H, Hr, HT, HT_r, GB, GB_r, HB, HB_r, stats, stats_r)
            P.barrier()
            SB.reset(mX)
            for tt in range(4):
                P.op("act", lambda eng, tt=tt: eng.mul(H[:, tt, :], H[:, tt, :], ALPHA), reads=[Hr[tt]], writes=[Hr[tt]])
            G = 4
            NW1, NW2 = 4, 6
            w1 = [SB.take([NKC, 128], BF16) for _ in range(NW1)]
            w1_r = [Reg("w1_%d" % i) for i in range(NW1)]
            w2 = [SB.take([D], BF16) for _ in range(NW2)]
            w2_r = [Reg("w2_%d" % i) for i in range(NW2)]
            hid = [SB.take([TB], BF16) for _ in range(NW2)]
            hid_r = [Reg("hid%d" % i) for i in range(NW2)]
            rl = [SB.take([TB], F32) for _ in range(2)]
            rl_r = [Reg("rl0"), Reg("rl1")]
            nfc = DFF // 128

            def ld_w1(g):
                for c in range(G):
                    fc = g * G + c
                    P.dma("pool", w1[fc % NW1], w_ff1_t[fc].rearrange("p (k n) -> p k n", k=NKC), writes=[w1_r[fc % NW1]])

            def ld_w2(g):
                for c in range(G):
                    fc = g * G + c
                    P.dma("pool", w2[fc % NW2], w_ff2[fc * 128:(fc + 1) * 128, :], writes=[w2_r[fc % NW2]])

            ld_w1(0)
            for g in range(nfc // G):
                ld_w2(g)
                for c in range(G):
                    fc = g * G + c
                    s1 = fc % NW1
                    s2 = fc % NW2
                    b = nextbank(0, 2)
                    mm_group(psf(b), [(w1[s1][:, kc, :], HT[:, kc, :]) for kc in range(NKC)], [w1_r[s1], HT_r], b)
                    k = fc % 2
                    P.op("act", lambda eng, b=b, k=k: eng.activation(rl[k], psf(b), AF.Relu), reads=[bankreg[b]], writes=[rl_r[k]])
                    P.op("act", lambda eng, k=k, s2=s2: eng.activation(hid[s2], rl[k], AF.Square), reads=[rl_r[k]], writes=[hid_r[s2]])
                if g + 1 < nfc // G:
                    ld_w1(g + 1)
                for tt in range(4):
                    for nt in range(8):
                        b = nextbank(2, 8)
                        prs = []
                        rds = []
                        for c in range(G):
                            s2 = (g * G + c) % NW2
                            prs.append((hid[s2][:, tt * 128:(tt + 1) * 128], w2[s2][:, nt * 512:(nt + 1) * 512]))
                            rds += [hid_r[s2], w2_r[s2]]
                        mm_group(psf(b), prs, rds, b)
                        hs = H[:, tt, nt * 512:(nt + 1) * 512]
                        P.op("dve", lambda eng, hs=hs, b=b: eng.tensor_tensor(hs, hs, psf(b), ALU.add),
                             reads=[bankreg[b], Hr[tt]], writes=[Hr[tt]])
            P.barrier()
            SB.reset(mX)
            GB, GB_r, HB, HB_r, stats, stats_r = take_ln()
            load_gb(GB, GB_r, 2)
            layer_norm(H, Hr, HT, HT_r, GB, GB_r, HB, HB_r, stats, stats_r, do_T=False)
            for tt in range(4):
                P.dma("sp", y[t0 + tt * 128:t0 + (tt + 1) * 128, :], H[:, tt, :], reads=[Hr[tt]])
            P.barrier()
            SB.reset(m0)

        def phase_a():
            mA = SB.mark()
            SL = SB.take([96], F32)
            XT = SB.take([NKC, 1024], BF16)
            XT_r = Reg("XT")
            WR = [SB.take([NKC, 256], BF16) for _ in range(2)]
            WR_r = [Reg("WR0"), Reg("WR1")]
            wslot = [0]

            def load_piece(col0, ncols=256):
                s = wslot[0] % 2
                wslot[0] += 1
                if ncols == 48:
                    P.dma("pool", WR[s][:, :, 0:48], w_gates, writes=[WR_r[s]])
                    return s
                for jj in range(ncols // 128):
                    P.dma("pool", WR[s][:, :, jj * 128:(jj + 1) * 128],
                          w_in_t[col0 // 128 + jj].rearrange("p (k n) -> p k n", k=NKC), writes=[WR_r[s]])
                return s

            ST = SB.take([8, 2, 256], F32)
            ST_r = [Reg("ST%d" % i) for i in range(8)]
            KSp = SB.take([4, 1024], BF16)
            VSp = SB.take([8, 4, 130], BF16)
            KWp = SB.take([4, 512], BF16)
            VWp = SB.take([4, 4, 130], BF16)
            KSp_r, VSp_r, KWp_r, VWp_r = Reg("KSp"), Reg("VSp"), Reg("KWp"), Reg("VWp")
            KC = SB.take([4, 2, 64], BF16)
            KC_r = Reg("KC")
            VC = SB.take([4, 2, 162], BF16, parts=64)
            VC_r = Reg("VC")
            ZT = SB.take([2, 4, 16], BF16)
            ZT_r = Reg("ZT")
            tabs_r = Reg("tabs")

            def ltab(name, dims, dt, parts=128):
                t = SB.take(dims, dt, parts=parts)
                P.dma("pool" if dt == BF16 else "sp", t, dram[name], writes=[tabs_r])
                return t

            P.dma("sp", VC, dram["vc_init"], writes=[VC_r])
            P.op("dve", lambda eng: eng.memset(VSp, 1.0), writes=[VSp_r])
            P.op("dve", lambda eng: eng.memset(VWp, 1.0), writes=[VWp_r])
            P.dma("pool", XT, wview(dram["xT_prev"]), writes=[XT_r])
            rs1 = ltab("ret_rs1", [8, 8], F32)
            mP1 = SB.mark()
            ktok = SB.take([8, 256], BF16)
            vtok = SB.take([8, 256], BF16)
            ktok_r, vtok_r = Reg("ktok"), Reg("vtok")
            ecnt = [0]

            def evac(dst, src, b, dst_reg, scale=None, func=None, extra_reads=()):
                ecnt[0] += 1
                if func is not None or scale is not None or ecnt[0] % 2 == 0:
                    f = func if func is not None else AF.Copy
                    if scale is None:
                        P.op("act", lambda eng: eng.activation(dst, src, f), reads=[bankreg[b]] + list(extra_reads), writes=[dst_reg])
                    else:
                        P.op("act", lambda eng: eng.activation(dst, src, f, scale=scale), reads=[bankreg[b]] + list(extra_reads), writes=[dst_reg])
                else:
                    P.op("dve", lambda eng: eng.tensor_copy(dst, src), reads=[bankreg[b]] + list(extra_reads), writes=[dst_reg])

            def proj_tm(s, col0, ncols, tile, xoff=0):
                b = nextbank()
                mm_group(psf(b)[:, 0:ncols],
                         [(XT[:, kc, tile * 128:(tile + 1) * 128], WR[s][:, kc, col0:col0 + ncols]) for kc in range(NKC)],
                         [XT_r, WR_r[s]], b)
                return b

            def proj_fm(s, col0, th):
                b = nextbank()
                mm_group(psf(b), [(WR[s][:, kc, col0:col0 + 128], XT[:, kc, th * 512:(th + 1) * 512]) for kc in range(NKC)],
                         [XT_r, WR_r[s]], b)
                return b

            for h in range(8):
                sk = load_piece(2048 + h * 256)
                sv = load_piece(4096 + h * 256)
                for n in range(8):
                    b = proj_tm(sk, 0, 256, n)
                    evac(ktok[:, n, :], psf(b)[:, 0:256], b, ktok_r, scale=rs1[:, h, n:n + 1], extra_reads=[tabs_r])
                for n in range(8):
                    b = proj_tm(sv, 0, 256, n)
                    evac(vtok[:, n, :], psf(b)[:, 0:256], b, vtok_r)
                for dh in range(2):
                    b = nextbank()
                    mm_group(psf(b)[:, 0:256], [(ktok[:, n, dh * 128:(dh + 1) * 128], vtok[:, n, :]) for n in range(8)],
                             [ktok_r, vtok_r], b)
                    evac(ST[:, h, dh, :], psf(b)[:, 0:256], b, ST_r[h])
            P.barrier()
            SB.reset(mP1)
            W1 = SB.take([2, 32, 128], BF16)
            W2 = SB.take([2, 128], BF16)
            posT = SB.take([2, 32], BF16)
            cw_r = Reg("cw")
            P.dma("pool", W1, dram["cmp_w1"].rearrange("k (j p) n -> p k j n", p=128), writes=[cw_r])
            P.dma("pool", W2, dram["cmp_w2"].rearrange("k p n -> p k n"), writes=[cw_r])
            P.dma("pool", posT, dram["cmp_posT"].rearrange("k p j -> p k j"), writes=[cw_r])
            cbias = SB.take([2], F32)
            cbias_r = Reg("cbias")
            for kv in range(2):
                b = nextbank()
                mm_group(psf(b)[:, 0:1], [(W1[:, kv, j, :], posT[:, kv, j:j + 1]) for j in range(32)], [cw_r], b)
                evac(cbias[:, kv:kv + 1], psf(b)[:, 0:1], b, cbias_r)
            mZ = SB.mark()
            zb = SB.take([2, 4, 1040], BF16)
            zb_r = Reg("zb")
            gel = SB.take([6, 64], F32)
            gel_r = Reg("gel")
            hidc = SB.take([64], BF16)
            hidc_r = Reg("hidc")

            def compress(kv, g, tile, NB):
                b = nextbank()
                zz = zb[:, kv, g, :]
                mm_group(psf(b)[:, 0:NB], [(W1[:, kv, j, :], zz[:, j:j + 16 * (NB - 1) + 1:16]) for j in range(32)], [cw_r, zb_r], b)
                u = gel[:, 0, 0:NB]
                t1 = gel[:, 1, 0:NB]
                t2 = gel[:, 2, 0:NB]
                sg = gel[:, 3, 0:NB]
                P.op("act", lambda eng: eng.activation(u, psf(b)[:, 0:NB], AF.Identity, bias=cbias[:, kv:kv + 1], scale=1.0),
                     reads=[bankreg[b], cbias_r], writes=[gel_r])
                P.op("dve", lambda eng: eng.tensor_tensor(t1, u, u, ALU.mult), reads=[gel_r], writes=[gel_r])
                P.op("dve", lambda eng: eng.tensor_scalar(t2, t1, 0.044715, 1.0, ALU.mult, ALU.add), reads=[gel_r], writes=[gel_r])
                P.op("dve", lambda eng: eng.tensor_tensor(t1, t2, u, ALU.mult), reads=[gel_r], writes=[gel_r])
                P.op("act", lambda eng: eng.activation(sg, t1, AF.Sigmoid, scale=1.5957691216057308), reads=[gel_r], writes=[gel_r])
                P.op("dve", lambda eng: eng.tensor_tensor(hidc[:, 0:NB], u, sg, ALU.mult), reads=[gel_r], writes=[hidc_r])
                b2 = nextbank()
                if kv == 0:
                    mm_group(psf(b2)[:, 0:NB], [(W2[:, 0, :], hidc[:, 0:NB])], [cw_r, hidc_r], b2)
                    evac(KC[:, g, tile, 0:NB], psf(b2)[:, 0:NB], b2, KC_r)
                else:
                    mm_group(psf(b2)[0:NB, 0:128], [(hidc[:, 0:NB], W2[:, 1, :])], [cw_r, hidc_r], b2)
                    evac(VC[0:NB, g, tile, 0:128], psf(b2)[0:NB, 0:128], b2, VC_r)

            def nsa_kv_pass(is_prev):
                for j in range(6):
                    if (not is_prev) and j >= 2:
                        break
                    for gp in range(2):
                        s = load_piece(10240 + j * 512 + gp * 256)
                        for gi in range(2):
                            g = gp * 2 + gi
                            if j in (0, 1):
                                for th in range(2):
                                    b = proj_fm(s, gi * 128, th)
                                    off = (0 if is_prev else 16) + th * 512
                                    evac(zb[:, j, g, off:off + 512], psf(b), b, zb_r)
                            elif j == 2:
                                for th in range(2):
                                    b = proj_fm(s, gi * 128, th)
                                    evac(KSp[:, g, th * 512:(th + 1) * 512], psf(b), b, KSp_r)
                            elif j == 4:
                                b = proj_fm(s, gi * 128, 1)
                                evac(KWp[:, g, :], psf(b), b, KWp_r)
                        if j == 3:
                            for n in range(8):
                                b = proj_tm(s, 0, 256, n)
                                evac(VSp[:, n, gp * 2:gp * 2 + 2, 0:128], psf(b)[:, 0:256].rearrange("p (g d) -> p g d", g=2), b, VSp_r)
                        elif j == 5:
                            for n in range(4, 8):
                                b = proj_tm(s, 0, 256, n)
                                evac(VWp[:, n - 4, gp * 2:gp * 2 + 2, 0:128], psf(b)[:, 0:256].rearrange("p (g d) -> p g d", g=2), b, VWp_r)

            nsa_kv_pass(True)
            for kv in range(2):
                for g in range(4):
                    compress(kv, g, 0, 63)
            P.op("dve", lambda eng: eng.tensor_copy(ZT, zb[:, :, :, 1008:1024]), reads=[zb_r], writes=[ZT_r])
            P.barrier()
            P.dma("pool", XT, wview(dram["xT_own"]), writes=[XT_r])
            P.op("dve", lambda eng: eng.tensor_copy(zb[:, :, :, 0:16], ZT), reads=[ZT_r], writes=[zb_r])
            nsa_kv_pass(False)
            for kv in range(2):
                for g in range(4):
                    compress(kv, g, 1, 64)
            P.barrier()
            SB.reset(mP1)
            mscr_r = Reg("mixscr")
            decT = ltab("ret_decT", [8, 128], F32)
            gq = ltab("ret_gq", [8, 128], F32)
            rs2 = ltab("ret_rs2", [8], F32)
            mR = SB.mark()
            class _B:
                pass
            RB = []
            for i in range(2):
                B_ = _B()
                B_.qT = SB.take([2, 1024], BF16)
                B_.kTt = SB.take([2, 1024], BF16)
                B_.vtok = SB.take([8, 256], BF16)
                B_.gs = SB.take([8, 256], BF16)
                B_.qT_r, B_.kTt_r, B_.vtok_r, B_.gs_r = (Reg(n_ + str(i)) for n_ in ("qT", "kTt", "vtok", "gs"))
                RB.append(B_)
            qhT = SB.take([2, 1024], BF16)
            ktok = SB.take([8, 256], BF16)
            Sbf = SB.take([2, 256], BF16)
            PT = [SB.take([128], BF16) for _ in range(2)]
            yn = SB.take([256], F32)
            rout = SB.take([8, 256], BF16)
            mst = SB.take([2, 1024], BF16)
            qhT_r, ktok_r, Sbf_r = (Reg(n) for n in ("qhT", "ktok", "Sbf"))
            PT_r = [Reg("PT0"), Reg("PT1")]
            yn_r, rout_r, mst_r, SL_r = Reg("yn"), Reg("rout"), Reg("mst"), Reg("SL")
            GAM = [1.0 - 2.0 ** (-5.0 - h) for h in range(8)]

            def proj_gen(h, B_):
                sq = load_piece(h * 256)
                for dh in range(2):
                    for th in range(2):
                        b = proj_fm(sq, dh * 128, th)
                        evac(B_.qT[:, dh, th * 512:(th + 1) * 512], psf(b), b, B_.qT_r)
                        yield
                sk = load_piece(2048 + h * 256)
                for dh in range(2):
                    for th in range(2):
                        b = proj_fm(sk, dh * 128, th)
                        evac(B_.kTt[:, dh, th * 512:(th + 1) * 512], psf(b), b, B_.kTt_r)
                        yield
                sv = load_piece(4096 + h * 256)
                for n in range(8):
                    b = proj_tm(sv, 0, 256, n)
                    evac(B_.vtok[:, n, :], psf(b)[:, 0:256], b, B_.vtok_r)
                    yield
                sg = load_piece(6144 + h * 256)
                for n in range(8):
                    b = proj_tm(sg, 0, 256, n)
                    evac(B_.gs[:, n, :], psf(b)[:, 0:256], b, B_.gs_r, func=AF.Silu)
                    yield

            for _ in proj_gen(0, RB[0]):
                pass
            for h in range(8):
                B_ = RB[h % 2]
                qT, kTt, vtok, gs = B_.qT, B_.kTt, B_.vtok, B_.gs
                qT_r, kTt_r, vtok_r, gs_r = B_.qT_r, B_.kTt_r, B_.vtok_r, B_.gs_r
                nxt = proj_gen(h + 1, RB[(h + 1) % 2]) if h < 7 else iter(())

                def pull(k, nxt=nxt):
                    for _ in range(k):
                        next(nxt, None)

                for dh in range(2):
                    for n in range(8):
                        P.op("dve", lambda eng, dh=dh, n=n, h=h, qT=qT: eng.tensor_tensor(
                            qhT[:, dh, n * 128:(n + 1) * 128], qT[:, dh, n * 128:(n + 1) * 128], gq[:, h, :], ALU.mult),
                            reads=[qT_r, tabs_r], writes=[qhT_r])
                for n in range(8):
                    b = nextbank()
                    pb = psb(b)
                    for dh in range(2):
                        P.op("pe", lambda eng, dh=dh, n=n, pb=pb, kTt=kTt: eng.transpose(pb[:, dh * 128:(dh + 1) * 128], kTt[:, dh, n * 128:(n + 1) * 128], ident),
                             reads=[kTt_r, ident_r], writes=[bankreg[b]], sig=(dh == 1))
                    evac(ktok[:, n, :], pb[:, 0:256], b, ktok_r, scale=rs2[:, h:h + 1], extra_reads=[tabs_r])
                P.op("act", lambda eng, h=h: eng.copy(Sbf, ST[:, h, :, :]), reads=[ST_r[h]], writes=[Sbf_r])
                for n in range(8):
                    cs = slice(n * 128, (n + 1) * 128)
                    b1 = nextbank()
                    mm_group(psf(b1)[:, 0:128], [(kTt[:, dh, cs], qT[:, dh, cs]) for dh in range(2)], [kTt_r, qT_r], b1)
                    p = n % 2
                    P.op("dve", lambda eng, b1=b1, p=p, h=h: eng.tensor_tensor(PT[p], psf(b1)[:, 0:128], decT[:, h, :], ALU.mult),
                         reads=[bankreg[b1], tabs_r], writes=[PT_r[p]])
                    pull(1)
                    b2 = nextbank()
                    mm_group(psf(b2)[:, 0:256], [(PT[p], vtok[:, n, :])] + [(qhT[:, dh, cs], Sbf[:, dh, :]) for dh in range(2)],
                             [PT_r[p], vtok_r, qhT_r, Sbf_r], b2)
                    o = psf(b2)[:, 0:256]
                    pull(1)
                    P.op("dve", lambda eng, o=o: eng.bn_stats(SL[:, 0:6], o), reads=[bankreg[b2]], writes=[SL_r])
                    P.op("dve", lambda eng: eng.bn_aggr(SL[:, 8:10], SL[:, 0:6]), reads=[SL_r], writes=[SL_r])
                    P.op("act", lambda eng: eng.activation(SL[:, 10:11], SL[:, 9:10], AF.Sqrt, bias=EPS_AP[0], scale=1.0),
                         reads=[SL_r, small_r], writes=[SL_r])
                    P.op("dve", lambda eng: eng.reciprocal(SL[:, 10:11], SL[:, 10:11]), reads=[SL_r], writes=[SL_r])
                    P.op("dve", lambda eng: eng.scalar_tensor_tensor(out=SL[:, 11:12], in0=SL[:, 8:9], scalar=-1.0, in1=SL[:, 10:11],
                                                                   op0=ALU.mult, op1=ALU.mult), reads=[SL_r], writes=[SL_r])
                    P.op("act", lambda eng, o=o: eng.activation(yn, o, AF.Identity, bias=SL[:, 11:12], scale=SL[:, 10:11]),
                         reads=[SL_r, bankreg[b2]], writes=[yn_r])
                    P.op("dve", lambda eng, n=n, gs=gs: eng.tensor_tensor(rout[:, n, :], yn, gs[:, n, :], ALU.mult),
                         reads=[yn_r, gs_r], writes=[rout_r])
                    pull(1)
                    if n < 7:
                        for dh in range(2):
                            b3 = nextbank()
                            mm_group(psf(b3)[:, 0:256], [(ktok[:, n, dh * 128:(dh + 1) * 128], vtok[:, n, :])], [ktok_r, vtok_r], b3)
                            P.op("dve", lambda eng, b3=b3, dh=dh, h=h: eng.scalar_tensor_tensor(
                                out=ST[:, h, dh, :], in0=ST[:, h, dh, :], scalar=GAM[h] ** 128, in1=psf(b3)[:, 0:256],
                                op0=ALU.mult, op1=ALU.add), reads=[bankreg[b3], ST_r[h]], writes=[ST_r[h]])
                        P.op("act", lambda eng, h=h: eng.copy(Sbf, ST[:, h, :, :]), reads=[ST_r[h]], writes=[Sbf_r])
                for _ in nxt:
                    pass
                for n in range(8):
                    transpose_to(mst[:, 0:2, n * 128:(n + 1) * 128], [rout[:, n, e * 128:(e + 1) * 128] for e in range(2)],
                                 [rout_r], mst_r, "act" if n % 2 else "dve")
                P.dma("sp", mixT_scr.rearrange("(a p) t -> p a t", p=128)[:, 2 * h:2 * h + 2, :], mst, reads=[mst_r], writes=[mscr_r])
            P.barrier()
            SB.reset(mP1)
            cmask = ltab("cmp_mask", [2, 8, 128], BF16, parts=64)
            cbi = ltab("cmp_bias", [2, 16, 8], F32, parts=64)
            ikeep = ltab("imp_keep", [8, 32], F32)
            iadd = ltab("imp_add", [8, 32], F32)
            sbias = ltab("slc_bias", [16, 16, 8], F32)
            E2 = ltab("e2", [16, 128], BF16, parts=32)
            tri = ltab("tri", [2, 128], BF16)
            gsig = SB.take([8, 48], F32)
            gsig_r = Reg("gsig")
            sgt = load_piece(13312, ncols=48)
            for qt in range(8):
                b = proj_tm(sgt, 0, 48, qt)
                evac(gsig[:, qt, :], psf(b)[:, 0:48], b, gsig_r, func=AF.Sigmoid)
            qn = SB.take([4, 1024], BF16)
            KSo = SB.take([1024], BF16)
            VSo = SB.take([8, 130], BF16)
            KWo = SB.take([1024], BF16)
            VWo = SB.take([8, 130], BF16)
            qn_r, KSo_r, VSo_r, KWo_r, VWo_r = (Reg(n) for n in ("qn", "KSo", "VSo", "KWo", "VWo"))
            pc = SB.take([2, 128], BF16, parts=64)
            pc_r = Reg("pc")
            pS = [SB.take([4, 128], BF16) for _ in range(4)]
            pS_r = [[Reg("pS%d_%d" % (i, r)) for r in range(4)] for i in range(4)]
            R4 = SB.take([4, 128], BF16, parts=32)
            R4_r = Reg("R4")
            tri4 = ltab("tri4", [2, 4, 128], BF16)
            SCB = (0, 1, 7)
            scb = [0]
            psi = [0]
            imp = SB.take([4, 32], F32)
            imp_r = Reg("imp")
            m8 = SB.take([16], F32)
            selb = SB.take([32], BF16)
            selb_r = Reg("selb")
            Rm = SB.take([128], BF16, parts=32)
            Rm_r = Reg("Rm")
            acc4 = SB.take([4, 128], F32)
            acc4_r = [Reg("acc%d" % i) for i in range(4)]
            ost = SB.take([4, 128], BF16)
            ost_r = Reg("ost")
            mst4 = SB.take([4, 1024], BF16)
            mst4_r = Reg("mst4")
            cf = SB.take([16], F32)
            cf_r = Reg("cf")
            P.op("dve", lambda eng: eng.memset(VSo, 1.0), writes=[VSo_r])
            P.op("dve", lambda eng: eng.memset(VWo, 1.0), writes=[VWo_r])
            SC = 128.0 ** -0.5

            def coef(b, col, gcol, k):
                P.op("dve", lambda eng: eng.tensor_scalar(cf[:, k:k + 1], psf(b)[:, col:col + 1], 1e-30, None, ALU.add),
                     reads=[bankreg[b]], writes=[cf_r])
                P.op("dve", lambda eng: eng.reciprocal(cf[:, k:k + 1], cf[:, k:k + 1]), reads=[cf_r], writes=[cf_r])
                if gcol is not None:
                    P.op("dve", lambda eng: eng.tensor_tensor(cf[:, k + 4:k + 5], cf[:, k:k + 1], gcol, ALU.mult),
                         reads=[cf_r, gsig_r], writes=[cf_r])

            for g in range(4):
                for rp in range(2):
                    s = load_piece(8192 + g * 512 + rp * 256)
                    for ri in range(2):
                        for th in range(2):
                            b = proj_fm(s, ri * 128, th)
                            evac(qn[:, rp * 2 + ri, th * 512:(th + 1) * 512], psf(b), b, qn_r, scale=SC)
                for pi, (Ko, Ko_r, Vo, Vo_r) in enumerate(((KSo, KSo_r, VSo, VSo_r), (KWo, KWo_r, VWo, VWo_r))):
                    s = wslot[0] % 2
                    wslot[0] += 1
                    for jj in range(2):
                        c0_ = 11264 + (2 * pi + jj) * 512 + g * 128
                        P.dma("pool", WR[s][:, :, jj * 128:(jj + 1) * 128],
                              w_in_t[c0_ // 128].rearrange("p (k n) -> p k n", k=NKC), writes=[WR_r[s]])
                    for th in range(2):
                        b = proj_fm(s, 0, th)
                        evac(Ko[:, th * 512:(th + 1) * 512], psf(b), b, Ko_r)
                    for n in range(8):
                        b = proj_tm(s, 128, 128, n)
                        evac(Vo[:, n, 0:128], psf(b)[:, 0:128], b, Vo_r)
                for qt in range(8):
                    qs = slice(qt * 128, (qt + 1) * 128)
                    qtc = 8 + qt
                    for r in range(4):
                        hd = g * 4 + r
                        b = nextbank(0, 2)
                        for tile, NB in ((0, 63), (1, 64)):
                            oc = psf(b)[0:NB, tile * 128:(tile + 1) * 128]
                            P.op("pe", lambda eng, oc=oc, l_=KC[:, g, tile, 0:NB], r_=qn[:, r, qs]: eng.matmul(oc, l_, r_, start=True, stop=False),
                                 reads=[KC_r, qn_r], writes=[bankreg[b]], sig=False)
                            P.op("pe", lambda eng, oc=oc, l_=ident[0:NB, 0:NB], r_=cmask[0:NB, tile, qt, :]: eng.matmul(oc, l_, r_, start=False, stop=True),
                                 reads=[ident_r, tabs_r], writes=[bankreg[b]], sig=True)
                            P.op("act", lambda eng, oc=oc, o_=pc[0:NB, tile, :], b_=cbi[0:NB, tile, hd, qt:qt + 1]: eng.activation(
                                o_, oc, AF.Exp, bias=b_, scale=1.0),
                                reads=[bankreg[b], tabs_r], writes=[pc_r])
                        b2 = 2
                        mm_group(psf(b2)[:, 0:162], [(pc[0:NB, tile, :], VC[0:NB, g, tile, :]) for tile, NB in ((0, 63), (1, 64))],
                                 [pc_r, VC_r], b2)
                        coef(b2, 128, gsig[:, qt, hd:hd + 1], r)
                        if r == 0:
                            P.op("dve", lambda eng, r=r: eng.tensor_scalar(imp[:, 0, :], psf(2)[:, 130:162], cf[:, r:r + 1], None, ALU.mult),
                                 reads=[bankreg[2], cf_r], writes=[imp_r])
                        else:
                            P.op("dve", lambda eng, r=r: eng.scalar_tensor_tensor(out=imp[:, 0, :], in0=psf(2)[:, 130:162], scalar=cf[:, r:r + 1],
                                                                               in1=imp[:, 0, :], op0=ALU.mult, op1=ALU.add),
                                 reads=[bankreg[2], cf_r, imp_r], writes=[imp_r])
                        P.op("dve", lambda eng, r=r: eng.tensor_scalar(acc4[:, r, :], psf(2)[:, 0:128], cf[:, r + 4:r + 5], None, ALU.mult),
                             reads=[bankreg[2], cf_r], writes=[acc4_r[r]])
                    P.op("dve", lambda eng, k_=ikeep[:, qt, :]: eng.tensor_tensor(imp[:, 1, :], imp[:, 0, :], k_, ALU.mult), reads=[imp_r, tabs_r], writes=[imp_r])
                    P.op("dve", lambda eng, k_=iadd[:, qt, :]: eng.tensor_tensor(imp[:, 1, :], imp[:, 1, :], k_, ALU.add), reads=[imp_r, tabs_r], writes=[imp_r])
                    P.op("dve", lambda eng: eng.max(m8[:, 0:8], imp[:, 1, :]), reads=[imp_r], writes=[imp_r])
                    P.op("dve", lambda eng: eng.match_replace(imp[:, 2, :], m8[:, 0:8], imp[:, 1, :], -3.0e38), reads=[imp_r], writes=[imp_r])
                    P.op("dve", lambda eng: eng.max(m8[:, 8:16], imp[:, 2, :]), reads=[imp_r], writes=[imp_r])
                    P.op("dve", lambda eng: eng.tensor_scalar(imp[:, 3, :], imp[:, 1, :], m8[:, 15:16], None, ALU.is_ge), reads=[imp_r], writes=[imp_r])
                    P.op("dve", lambda eng: eng.tensor_scalar(selb, imp[:, 3, :], -NEGM, NEGM, ALU.mult, ALU.add), reads=[imp_r], writes=[selb_r])
                    bt = 7
                    tasks = []
                    for br, klist in ((2, list(range(qtc - 4, qtc + 1))), (1, list(range(0, qtc + 1)))):
                        for i, kt in enumerate(klist):
                            tasks.append((br, i, kt, i == len(klist) - 1))

                    def emit_score(task):
                        br, i, kt, lastk = task
                        b = SCB[scb[0] % len(SCB)]
                        scb[0] += 1
                        if br == 1:
                            Kap, Kr = (KSp[:, g, kt * 128:(kt + 1) * 128], KSp_r) if kt < 8 else (KSo[:, (kt - 8) * 128:(kt - 7) * 128], KSo_r)
                            Vap, Vr = (VSp[:, kt, g, :], VSp_r) if kt < 8 else (VSo[:, kt - 8, :], VSo_r)
                        else:
                            Kap, Kr = (KWp[:, g, (kt - 4) * 128:(kt - 3) * 128], KWp_r) if kt < 8 else (KWo[:, (kt - 8) * 128:(kt - 7) * 128], KWo_r)
                            Vap, Vr = (VWp[:, kt - 4, g, :], VWp_r) if kt < 8 else (VWo[:, kt - 8, :], VWo_r)
                        extra = []
                        if br == 1:
                            extra.append((E2[:, kt, :], R4, [tabs_r, R4_r]))
                        if kt == qtc:
                            extra.append((ident, tri4[:, 0, :, :], [ident_r, tabs_r]))
                        if br == 2 and kt == qtc - 4:
                            extra.append((ident, tri4[:, 1, :, :], [ident_r, tabs_r]))
                        sc = psf(b).rearrange("p (r t) -> p r t", r=4)
                        P.op("pe", lambda eng, sc=sc, Kap=Kap, q_=qn[:, 0:4, qs], ne=len(extra): eng.matmul(sc, Kap, q_, start=True, stop=(ne == 0)),
                             reads=[Kr, qn_r], writes=[bankreg[b]], sig=(len(extra) == 0))
                        for ei, (l_, r_, rg_) in enumerate(extra):
                            last = ei == len(extra) - 1
                            P.op("pe", lambda eng, sc=sc, l_=l_, r_=r_, last=last: eng.matmul(sc, l_, r_, start=False, stop=last),
                                 reads=rg_, writes=[bankreg[b]], sig=last)
                        pi_ = psi[0] % len(pS)
                        psi[0] += 1
                        for r in range(4):
                            P.op("act", lambda eng, b=b, r=r, pi_=pi_, b_=sbias[:, g * 4 + r, kt, qt:qt + 1]: eng.activation(
                                pS[pi_][:, r, :], psf(b)[:, r * 128:(r + 1) * 128], AF.Exp, bias=b_, scale=1.0),
                                reads=[bankreg[b], tabs_r], writes=[pS_r[pi_][r]])
                        return (br, i, lastk, pi_, Vap, Vr)

                    def emit_pv(info):
                        br, i, lastk, pi_, Vap, Vr = info
                        for r in range(4):
                            bo = 3 + r
                            hd = g * 4 + r
                            P.op("pe", lambda eng, bo=bo, r=r, pi_=pi_, Vap=Vap, i=i, lastk=lastk: eng.matmul(psf(bo)[:, 0:130], pS[pi_][:, r, :], Vap, start=(i == 0), stop=lastk),
                                 reads=[pS_r[pi_][r], Vr], writes=[bankreg[bo]], sig=lastk)
                            if lastk:
                                coef(bo, 128, gsig[:, qt, br * 16 + hd:br * 16 + hd + 1], br)
                                P.op("dve", lambda eng, bo=bo, br=br, r=r: eng.scalar_tensor_tensor(
                                    out=acc4[:, r, :], in0=psf(bo)[:, 0:128], scalar=cf[:, br + 4:br + 5], in1=acc4[:, r, :], op0=ALU.mult, op1=ALU.add),
                                    reads=[bankreg[bo], cf_r, acc4_r[r]], writes=[acc4_r[r]])
                                if br == 1:
                                    P.op("act", lambda eng, r=r: eng.copy(ost[:, r, :], acc4[:, r, :]), reads=[acc4_r[r]], writes=[ost_r])

                    LAG = 2
                    pend = []
                    for task in tasks:
                        if task[0] == 1 and task[1] == 0:
                            P.op("pe", lambda eng: eng.transpose(psb(bt)[0:32, 0:128], selb, ident), reads=[selb_r, ident_r], writes=[bankreg[bt]])
                            for r in range(4):
                                if r % 2 == 0:
                                    P.op("act", lambda eng, r=r: eng.copy(R4[:, r, :], psb(bt)[0:32, 0:128]), reads=[bankreg[bt]], writes=[R4_r])
                                else:
                                    P.op("dve", lambda eng, r=r: eng.tensor_copy(R4[:, r, :], psb(bt)[0:32, 0:128]), reads=[bankreg[bt]], writes=[R4_r])
                        pend.append(emit_score(task))
                        if len(pend) > LAG:
                            emit_pv(pend.pop(0))
                    while pend:
                        emit_pv(pend.pop(0))
                    transpose_to(mst4[:, 0:4, qs], [ost[:, r, :] for r in range(4)], [ost_r], mst4_r, "dve")
                P.dma("sp", mixT_scr.rearrange("(a p) t -> p a t", p=128)[:, 16 + 4 * g:20 + 4 * g, :], mst4, reads=[mst4_r], writes=[mscr_r])
            P.barrier()
            SB.reset(mA)
            return mscr_r

        P.op("dve", lambda eng: eng.memset(small[:, 0:1], EPS), writes=[small_r])
        EPS_AP[0] = small[:, 0:1]
        mscr_r = None
        if cfg.get("phaseA", True):
            mscr_r = phase_a()
        mix_src = mix_in if not cfg.get("phaseA", True) else mixT_scr
        if not cfg.get("dbgA"):
            xattn_kv()
            for tb in range(cfg.get("ntb", 2)):
                token_block(tb, mix_src)
        P.final_wait("sp")
        P.replay()
    return nc


def _tables(hf):
    f64 = np.float64
    t = {}
    gam = 1.0 - 2.0 ** (-5.0 - np.arange(8, dtype=f64))
    p = np.arange(128, dtype=f64)
    n = np.arange(8, dtype=f64)
    t["ret_rs1"] = (gam[None, :, None] ** (1023.0 - (128.0 * n[None, None, :] + p[:, None, None])) / 16.0)
    dcs = p[None, :] - p[:, None]
    dec = np.where(dcs[:, None, :] >= 0, gam[None, :, None] ** np.maximum(dcs[:, None, :], 0.0), 0.0) / 16.0
    t["ret_decT"] = dec
    t["ret_gq"] = np.broadcast_to(gam[None, :, None] ** (p[None, None, :] + 1.0), (128, 8, 128))
    t["ret_rs2"] = gam[None, :] ** (127.0 - p[:, None]) / 16.0
    slopes = 2.0 ** (-8.0 * np.arange(1, 17, dtype=f64) / 16.0)
    vstart = 1024 * (1 - hf)
    qt = np.arange(8)
    bmid = 1024.0 + 128.0 * qt + 64.0
    pc_ = np.arange(64)
    cidx = np.stack([pc_, 63 + pc_], axis=1)
    cend = 16 * cidx + 31
    ctx_t = 1024 + 128 * qt[:, None] + np.arange(128)[None, :]
    valid = (cend[:, :, None, None] <= ctx_t[None, None]) & (16 * cidx[:, :, None, None] >= vstart)
    valid[63, 0] = False
    t["cmp_mask"] = np.where(valid, 0.0, NEGM)
    t["cmp_bias"] = slopes[None, None, :, None] * (cend[:, :, None, None] - bmid[None, None, None, :])
    s_ = np.arange(32)
    c0 = 16.0 * cidx[:, :, None]
    ov = np.clip(np.minimum(c0 + 32, 64.0 * s_ + 64) - np.maximum(c0, 64.0 * s_), 0, None) / 32.0
    vci = np.zeros((64, 4, 2, 162), f64)
    vci[:, :, :, 128:130] = 1.0
    vci[:, :, :, 130:162] = ov[:, None, :, :]
    t["vc_init"] = vci.astype(ml_dtypes.bfloat16)
    ctxp = 1024 + 128 * qt[None, :, None] + np.arange(128)[:, None, None]
    cur = ctxp // 64
    blk = np.arange(32)[None, None, :]
    blk0 = 16 * (1 - hf)
    forced = (blk == blk0) | (blk == cur) | (blk == cur - 1)
    dead = (blk > cur) | (blk < blk0)
    t["imp_keep"] = np.where(forced | dead, 0.0, 1.0)
    t["imp_add"] = np.where(forced, 1e9, np.where(dead, -1e9, 0.0))
    kt = np.arange(16)
    kpos = 128 * kt[None, None, :, None] + np.arange(128)[:, None, None, None]
    sb = slopes[None, :, None, None] * (kpos - bmid[None, None, None, :])
    t["slc_bias"] = np.where(kpos >= vstart, sb, -1e30)
    key = np.arange(128)
    t["e2"] = (np.arange(32)[:, None, None] == (2 * kt[None, :, None] + key[None, None, :] // 64)).astype(f64)
    tri = np.zeros((128, 2, 128), f64)
    tri[:, 0, :] = np.where(key[:, None] <= key[None, :], 0.0, NEGM)
    tri[:, 1, :] = np.where(key[:, None] > key[None, :], 0.0, NEGM)
    t["tri"] = tri
    t["tri4"] = np.broadcast_to(tri[:, :, None, :], (128, 2, 4, 128))
    return {k: (np.ascontiguousarray(v) if v.dtype == ml_dtypes.bfloat16 else np.ascontiguousarray(v, dtype=np.float32)) for k, v in t.items()}


def make_maps(inputs, cores, cfg):
    x = np.asarray(inputs["x"])
    maps = []
    tabs = [_tables(0), _tables(1)]
    ident = np.eye(128, dtype=np.float32)
    shared = {}
    if cfg.get("phaseA", True):
        wi = np.asarray(inputs["w_in"])[0]
        shared.update(w_in_t=np.ascontiguousarray(wi[:, :13312].reshape(NKC, 128, 104, 128).transpose(2, 1, 0, 3)).reshape(104, 128, NKC * 128),
                      w_gates=np.ascontiguousarray(wi[:, 13312:].reshape(NKC, 128, 48).transpose(1, 0, 2)), cmp_w1=np.asarray(inputs["cmp_w1"])[0], cmp_w2=np.asarray(inputs["cmp_w2"])[0],
                      cmp_posT=np.ascontiguousarray(np.asarray(inputs["cmp_pos"])[0].transpose(0, 2, 1)))
    if not cfg.get("dbgA"):
        def tl(w, n):
            k = w.shape[1] // n
            return np.ascontiguousarray(w.reshape(NKC, 128, k, n).transpose(2, 1, 0, 3)).reshape(k, 128, NKC * n)
        shared.update(w_out_t=tl(np.asarray(inputs["w_out"])[0], 512), xa_wq_t=tl(np.asarray(inputs["xa_wq"])[0], 128),
                      xa_wkv_t=tl(np.asarray(inputs["xa_wkv"])[0], 128),
                      xa_wo=np.asarray(inputs["xa_wo"])[0], w_ff1_t=tl(np.asarray(inputs["w_ff1"])[0], 128), w_ff2=np.asarray(inputs["w_ff2"])[0],
                      ln_g=np.asarray(inputs["ln_g"])[0], ln_b=np.asarray(inputs["ln_b"])[0])
    for c in cores:
        b, hf = c // 2, c % 2
        m = dict(shared)
        m["ident"] = ident
        own = x[b, hf * 1024:(hf + 1) * 1024]
        if cfg.get("phaseA", True):
            m["xT_own"] = np.ascontiguousarray(own.T)
            m["xT_prev"] = np.ascontiguousarray(x[b, 0:1024].T) if hf == 1 else np.zeros((D, 1024), np.float32)
            m.update(tabs[hf])
        if not cfg.get("dbgA"):
            m["x_own"] = np.ascontiguousarray(own)
            m["memT"] = np.ascontiguousarray(np.asarray(inputs["mem"])[b].T)
        maps.append(m)
    return maps


def kernel(**inputs):
    cfg = dict(phaseA=True)
    nc = build(cfg)
    cores = list(range(8))
    maps = make_maps(inputs, cores, cfg)
    res = run_bass_kernel_spmd(nc, maps, core_ids=cores)
    out = np.empty((4, 2048, D), np.float32)
    for c in cores:
        out[c // 2, (c % 2) * 1024:(c % 2 + 1) * 1024] = res.results[c]["y"]
    return out
```

```python
import contextlib
import math
import numpy as np
import ml_dtypes
import concourse.bass as bass
import concourse.mybir as mybir
from concourse.bass_utils import run_bass_kernel_spmd

F32, BF16 = mybir.dt.float32, mybir.dt.bfloat16
AF = mybir.ActivationFunctionType
ALU = mybir.AluOpType
AX = mybir.AxisListType

D = 4096
T_OWN = 1024
TB = 512
NKC = 32
ALPHA = 2.0 ** 0.25
EPS = 1e-5
IN_W = 13360
DFF = 16384
NEGM = -30000.0


class Reg:
    __slots__ = ("w", "r", "name")

    def __init__(self, name=""):
        self.w = None
        self.r = {}
        self.name = name


class Prog:
    ENG = ("pe", "act", "dve", "pool", "sp")

    def __init__(self, nc, stack, ndma=8):
        self.nc = nc
        self.q = {e: [] for e in self.ENG}
        self.sem = {}
        self.cnt = {}
        self.seen = {e: {} for e in self.ENG}
        self.fence = {e: {} for e in self.ENG}
        self.rr = {e: 0 for e in self.ENG}
        self.ndma = ndma
        for e in ("pe", "act", "dve", "pool"):
            self.sem[e] = stack.enter_context(nc.semaphore("s_" + e))
            self.cnt[e] = 0
        for e in ("sp", "pool", "act"):
            for i in range(ndma):
                k = ("d", e, i)
                self.sem[k] = stack.enter_context(nc.semaphore("d_%s_%d" % (e, i)))
                self.cnt[k] = 0

    def _waits(self, e, reads, writes):
        need = dict(self.fence[e])
        self.fence[e] = {}

        def add(k, v):
            if need.get(k, 0) < v:
                need[k] = v

        for r in reads:
            if r.w is not None:
                add(*r.w)
        for w in writes:
            if w.w is not None:
                add(*w.w)
            for k, v in w.r.items():
                add(k, v)
        out = []
        for k, v in need.items():
            if k == e and e == "pe":
                continue
            if self.seen[e].get(k, 0) >= v:
                continue
            self.seen[e][k] = v
            out.append((k, v))
        return out

    def _post(self, tok, reads, writes):
        for r in reads:
            if r.r.get(tok[0], 0) < tok[1]:
                r.r[tok[0]] = tok[1]
        for w in writes:
            w.w = tok
            w.r = {}

    def op(self, e, fn, reads=(), writes=(), sig=True):
        waits = self._waits(e, reads, writes)
        if sig:
            self.cnt[e] += 1
            tok = (e, self.cnt[e])
        else:
            tok = (e, self.cnt[e] + 1)
        sem = self.sem

        def run(eng):
            for k, v in waits:
                eng.wait_ge(sem[k], v)
            ins = fn(eng)
            if sig:
                ins.then_inc(sem[e], 1)

        self.q[e].append(run)
        self._post(tok, reads, writes)
        return tok

    def dma(self, e, out, in_, reads=(), writes=()):
        i = self.rr[e]
        self.rr[e] = (i + 1) % self.ndma
        key = ("d", e, i)
        waits = self._waits(e, reads, writes)
        prev = self.cnt[key]
        if prev > 0 and self.seen[e].get(key, 0) < prev:
            waits.append((key, prev))
            self.seen[e][key] = prev
        self.cnt[key] += 16
        tok = (key, self.cnt[key])
        sem = self.sem

        def run(eng):
            for k, v in waits:
                eng.wait_ge(sem[k], v)
            eng.dma_start(out=out, in_=in_).then_inc(sem[key], 16)

        self.q[e].append(run)
        self._post(tok, reads, writes)
        return tok

    def barrier(self):
        allt = {k: v for k, v in self.cnt.items() if v > 0}
        for e in self.ENG:
            for k, v in allt.items():
                if self.fence[e].get(k, 0) < v:
                    self.fence[e][k] = v

    def final_wait(self, e):
        self.barrier()
        waits = self._waits(e, (), ())
        sem = self.sem

        def run(eng):
            for k, v in waits:
                eng.wait_ge(sem[k], v)

        self.q[e].append(run)

    def replay(self):
        nc = self.nc
        q = self.q
        with nc.Block() as block:
            @block.tensor
            def _(eng):
                for f in q["pe"]:
                    f(eng)

            @block.scalar
            def _(eng):
                for f in q["act"]:
                    f(eng)

            @block.vector
            def _(eng):
                for f in q["dve"]:
                    f(eng)

            @block.gpsimd
            def _(eng):
                for f in q["pool"]:
                    f(eng)

            @block.sync
            def _(eng):
                for f in q["sp"]:
                    f(eng)


class SbAlloc:
    def __init__(self, nc, nbytes):
        self.words = nbytes // 4
        self.t = nc.alloc_sbuf_tensor("SB", [128, self.words], F32)
        self.off = 0

    def take(self, dims, dt, parts=128):
        n = int(np.prod(dims))
        sz = 2 if dt == BF16 else 4
        words = (n * sz + 3) // 4
        words = (words + 7) // 8 * 8
        assert self.off + words <= self.words, ("SBUF overflow", self.off, words, self.words)
        ap = self.t[0:parts, self.off:self.off + words]
        self.off += words
        if dt != F32:
            ap = ap.bitcast(dt)
        ap = ap[:, 0:n]
        if len(dims) == 2:
            ap = ap.rearrange("p (a b) -> p a b", a=dims[0])
        elif len(dims) == 3:
            ap = ap.rearrange("p (a b c) -> p a b c", a=dims[0], b=dims[1])
        elif len(dims) == 4:
            ap = ap.rearrange("p (a b c d) -> p a b c d", a=dims[0], b=dims[1], c=dims[2])
        return ap

    def mark(self):
        return self.off

    def reset(self, m):
        self.off = m


def build(cfg):
    nc = bass.Bass("TRN2", target_bir_lowering=False)
    dram = {}

    def din(name, shape, dt=F32):
        dram[name] = nc.dram_tensor(name, list(shape), dt, kind="ExternalInput").ap()
        return dram[name]

    if not cfg.get("dbgA"):
        x_own = din("x_own", [T_OWN, D])
        memT = din("memT", [D, 256])
        w_out_t = din("w_out_t", [8, 128, NKC * 512])
        xa_wq_t = din("xa_wq_t", [4, 128, NKC * 128])
        xa_wkv_t = din("xa_wkv_t", [8, 128, NKC * 128])
        xa_wo = din("xa_wo", [512, D])
        w_ff1_t = din("w_ff1_t", [DFF // 128, 128, NKC * 128])
        w_ff2 = din("w_ff2", [DFF, D])
        ln_g = din("ln_g", [3, D])
        ln_b = din("ln_b", [3, D])
    ident_d = din("ident", [128, 128])
    if cfg.get("phaseA", True):
        din("xT_prev", [D, 1024]); din("xT_own", [D, 1024])
        w_in_t = din("w_in_t", [104, 128, NKC * 128]); w_gates = din("w_gates", [128, NKC, 48])
        din("cmp_w1", [2, D, 128]); din("cmp_w2", [2, 128, 128]); din("cmp_posT", [2, 128, 32])
        din("vc_init", [64, 4, 2, 162], BF16)
        din("ret_rs1", [128, 8, 8]); din("ret_decT", [128, 8, 128]); din("ret_gq", [128, 8, 128]); din("ret_rs2", [128, 8])
        din("cmp_mask", [64, 2, 8, 128]); din("cmp_bias", [64, 2, 16, 8])
        din("imp_keep", [128, 8, 32]); din("imp_add", [128, 8, 32]); din("slc_bias", [128, 16, 16, 8])
        din("e2", [32, 16, 128]); din("tri", [128, 2, 128]); din("tri4", [128, 2, 4, 128])
    else:
        mix_in = din("mixT_in", [D, T_OWN], BF16)
    if not cfg.get("dbgA"):
        y = nc.dram_tensor("y", [T_OWN, D], F32, kind="ExternalOutput").ap()
    mixT_scr = nc.dram_tensor("mixT_scr", [D, T_OWN], BF16, kind=("ExternalOutput" if cfg.get("dbgA") else "Internal")).ap()

    stack = contextlib.ExitStack()
    with stack:
        P = Prog(nc, stack)
        SB = SbAlloc(nc, 206 * 1024)
        PS = nc.alloc_psum_tensor("PS", [128, 8, 512], F32)
        bankreg = [Reg("bank%d" % i) for i in range(8)]
        bank_rr = [0]

        def nextbank(lo=0, hi=8):
            b = lo + bank_rr[0] % (hi - lo)
            bank_rr[0] += 1
            return b

        def psf(b):
            return PS[:, b, :]

        def psb(b):
            return PS[:, b, :].bitcast(BF16)

        ident = SB.take([128], BF16)
        ident_r = Reg("ident")
        P.dma("pool", ident, ident_d, writes=[ident_r])
        kT = SB.take([4, 256], BF16)
        vaug = SB.take([2, 4, 130], BF16)
        kT_r, vaug_r = Reg("kT"), Reg("vaug")
        small = SB.take([64], F32)
        small_r = Reg("small")
        base_mark = SB.mark()

        def wview(ap, p=128):
            return ap.rearrange("(kc p) n -> p kc n", p=p)

        def mm_group(out_ap, pairs, reads, bank, extra_writes=()):
            n = len(pairs)
            for i, (l, r) in enumerate(pairs):
                P.op("pe", (lambda eng, l=l, r=r, i=i: eng.matmul(out_ap, l, r, start=(i == 0), stop=(i == n - 1))),
                     reads=reads, writes=[bankreg[bank]] + list(extra_writes), sig=(i == n - 1))

        def transpose_to(dst_aps, src_aps, src_regs, dst_reg, copy_eng):
            b = nextbank()
            n = len(src_aps)
            pb = psb(b)
            for j, s in enumerate(src_aps):
                P.op("pe", (lambda eng, s=s, j=j: eng.transpose(pb[:, j * 128:(j + 1) * 128], s, ident)),
                     reads=list(src_regs) + [ident_r], writes=[bankreg[b]], sig=(j == n - 1))
            src = pb[:, 0:n * 128].rearrange("p (a b) -> p a b", a=n)
            if copy_eng == "act":
                P.op("act", lambda eng: eng.copy(dst_aps, src), reads=[bankreg[b]], writes=[dst_reg])
            else:
                P.op("dve", lambda eng: eng.tensor_copy(dst_aps, src), reads=[bankreg[b]], writes=[dst_reg])

        def xattn_kv():
            m = SB.mark()
            memb = SB.take([NKC, 256], BF16)
            memb_r = Reg("memb")
            P.dma("pool", memb, wview(memT), writes=[memb_r])
            wr = [SB.take([NKC, 128], BF16) for _ in range(2)]
            wr_r = [Reg("wkvr0"), Reg("wkvr1")]
            P.op("dve", lambda eng: eng.memset(vaug, 1.0), writes=[vaug_r])
            for j in range(8):
                s = j % 2
                P.dma("pool", wr[s], xa_wkv_t[j].rearrange("p (k n) -> p k n", k=NKC), writes=[wr_r[s]])
                if j < 4:
                    b = nextbank()
                    mm_group(psf(b)[:, 0:256], [(wr[s][:, kc, :], memb[:, kc, :]) for kc in range(NKC)],
                             [wr_r[s], memb_r], b)
                    P.op("act", lambda eng, b=b, j=j: eng.copy(kT[:, j, :], psf(b)[:, 0:256]),
                         reads=[bankreg[b]], writes=[kT_r])
                else:
                    hd = j - 4
                    for mt in range(2):
                        b = nextbank()
                        mm_group(psf(b)[:, 0:128],
                                 [(memb[:, kc, mt * 128:(mt + 1) * 128], wr[s][:, kc, :]) for kc in range(NKC)],
                                 [wr_r[s], memb_r], b)
                        P.op("act", lambda eng, b=b, mt=mt, hd=hd: eng.copy(vaug[:, mt, hd, 0:128], psf(b)[:, 0:128]),
                             reads=[bankreg[b]], writes=[vaug_r])
            P.barrier()
            SB.reset(m)

        def layer_norm(H, Hr, HT, HT_r, GB, GB_r, HB, HB_r, stats, stats_r, do_T=True):
            for tt in range(4):
                h = H[:, tt, :]
                st = stats[:, tt, :]
                for c in range(8):
                    P.op("dve", lambda eng, c=c, h=h, st=st: eng.bn_stats(st[:, c * 6:(c + 1) * 6], h[:, c * 512:(c + 1) * 512]),
                         reads=[Hr[tt]], writes=[stats_r[tt]])
                P.op("dve", lambda eng, st=st: eng.bn_aggr(st[:, 48:50], st[:, 0:48]), reads=[stats_r[tt]], writes=[stats_r[tt]])
            for tt in range(4):
                st = stats[:, tt, :]
                P.op("act", lambda eng, st=st: eng.activation(st[:, 50:51], st[:, 49:50], AF.Sqrt, bias=EPS_AP[0], scale=1.0),
                     reads=[stats_r[tt], small_r], writes=[stats_r[tt]])
                P.op("dve", lambda eng, st=st: eng.reciprocal(st[:, 50:51], st[:, 50:51]), reads=[stats_r[tt]], writes=[stats_r[tt]])
                P.op("dve", lambda eng, st=st: eng.scalar_tensor_tensor(
                    out=st[:, 51:52], in0=st[:, 48:49], scalar=-1.0, in1=st[:, 50:51], op0=ALU.mult, op1=ALU.mult),
                    reads=[stats_r[tt]], writes=[stats_r[tt]])
            for tt in range(4):
                h = H[:, tt, :]
                st = stats[:, tt, :]
                P.op("act", lambda eng, h=h, st=st: eng.activation(h, h, AF.Identity, bias=st[:, 51:52], scale=st[:, 50:51]),
                     reads=[stats_r[tt], Hr[tt]], writes=[Hr[tt]])
            for tt in range(4):
                h = H[:, tt, :]
                P.op("dve", lambda eng, h=h: eng.tensor_tensor(h, h, GB[:, 0, :], ALU.mult), reads=[Hr[tt], GB_r], writes=[Hr[tt]])
                P.op("dve", lambda eng, h=h: eng.tensor_tensor(h, h, GB[:, 1, :], ALU.add), reads=[Hr[tt], GB_r], writes=[Hr[tt]])
                if do_T:
                    P.op("act", lambda eng, h=h: eng.copy(HB, h), reads=[Hr[tt]], writes=[HB_r])
                    for q4 in range(4):
                        transpose_to(HT[:, q4 * 8:(q4 + 1) * 8, tt * 128:(tt + 1) * 128],
                                     [HB[:, (q4 * 8 + j) * 128:(q4 * 8 + j + 1) * 128] for j in range(8)],
                                     [HB_r], HT_r, "act" if q4 % 2 else "dve")

        EPS_AP = [None]

        def load_gb(GB, GB_r, i):
            P.dma("sp", GB[:, 0, :], ln_g[i:i + 1, :].partition_broadcast(128), writes=[GB_r])
            P.dma("sp", GB[:, 1, :], ln_b[i:i + 1, :].partition_broadcast(128), writes=[GB_r])

        def token_block(tb, mix_src):
            m0 = SB.mark()
            H = SB.take([4, D], F32)
            Hr = [Reg("H%d" % i) for i in range(4)]
            HT = SB.take([NKC, TB], BF16)
            HT_r = Reg("HT")
            mX = SB.mark()

            def take_ln():
                return (SB.take([2, D], F32), Reg("GB"), SB.take([D], BF16), Reg("HB"), SB.take([4, 64], F32), [Reg("stats%d" % i) for i in range(4)])

            GB, GB_r, HB, HB_r, stats, stats_r = take_ln()
            t0 = tb * TB
            P.dma("sp", HT, wview(mix_src)[:, :, t0:t0 + TB], reads=([mscr_r] if mscr_r is not None else []), writes=[HT_r])
            for tt in range(4):
                P.dma("sp", H[:, tt, :], x_own[t0 + tt * 128:t0 + (tt + 1) * 128, :], writes=[Hr[tt]])
            load_gb(GB, GB_r, 0)
            wr = [SB.take([NKC, 512], BF16) for _ in range(2)]
            wr_r = [Reg("wo0"), Reg("wo1")]
            for nt in range(8):
                s = nt % 2
                P.dma("pool", wr[s], w_out_t[nt].rearrange("p (k n) -> p k n", k=NKC), writes=[wr_r[s]])
                for tt in range(4):
                    b = nextbank()
                    mm_group(psf(b), [(HT[:, kc, tt * 128:(tt + 1) * 128], wr[s][:, kc, :]) for kc in range(NKC)],
                             [HT_r, wr_r[s]], b)
                    hs = H[:, tt, nt * 512:(nt + 1) * 512]
                    P.op("dve", lambda eng, hs=hs, b=b: eng.scalar_tensor_tensor(
                        out=hs, in0=hs, scalar=ALPHA, in1=psf(b), op0=ALU.mult, op1=ALU.add),
                        reads=[bankreg[b], Hr[tt]], writes=[Hr[tt]])
            layer_norm(H, Hr, HT, HT_r, GB, GB_r, HB, HB_r, stats, stats_r)
            P.barrier()
            SB.reset(mX)
            GB, GB_r, HB, HB_r, stats, stats_r = take_ln()
            load_gb(GB, GB_r, 1)
            qT = SB.take([4, TB], BF16)
            qT_r = Reg("qT")
            pT = [SB.take([2, TB], BF16) for _ in range(2)]
            pT_r = [Reg("pT0"), Reg("pT1")]
            otok = SB.take([4, 512], BF16)
            otok_r = Reg("otok")
            oT = SB.take([4, TB], BF16)
            oT_r = Reg("oT")
            rc = SB.take([8], F32)
            rc_r = Reg("rc")
            wq = [SB.take([NKC, 128], BF16) for _ in range(2)]
            wq_r = [Reg("wq0"), Reg("wq1")]
            wo = [SB.take([4, 512], BF16) for _ in range(2)]
            wo_r = [Reg("wo0"), Reg("wo1")]
            for hd in range(4):
                s = hd % 2
                P.dma("pool", wq[s], xa_wq_t[hd].rearrange("p (k n) -> p k n", k=NKC), writes=[wq_r[s]])
                b = nextbank()
                mm_group(psf(b), [(wq[s][:, kc, :], HT[:, kc, :]) for kc in range(NKC)], [wq_r[s], HT_r], b)
                P.op("act", lambda eng, b=b, hd=hd: eng.activation(qT[:, hd, :], psf(b), AF.Copy, scale=128.0 ** -0.5),
                     reads=[bankreg[b]], writes=[qT_r])
            for hd in range(4):
                s = hd % 2
                for mt in range(2):
                    b = nextbank()
                    mm_group(psf(b), [(kT[:, hd, mt * 128:(mt + 1) * 128], qT[:, hd, :])], [kT_r, qT_r], b)
                    P.op("act", lambda eng, b=b, mt=mt, s=s: eng.activation(pT[s][:, mt, :], psf(b), AF.Exp),
                         reads=[bankreg[b]], writes=[pT_r[s]])
                for tt in range(4):
                    b = nextbank()
                    mm_group(psf(b)[:, 0:130],
                             [(pT[s][:, mt, tt * 128:(tt + 1) * 128], vaug[:, mt, hd, :]) for mt in range(2)],
                             [pT_r[s], vaug_r], b)
                    rcs = rc[:, tt:tt + 1]
                    P.op("dve", lambda eng, b=b, rcs=rcs: eng.reciprocal(rcs, psf(b)[:, 128:129]),
                         reads=[bankreg[b]], writes=[rc_r])
                    P.op("dve", lambda eng, b=b, rcs=rcs, tt=tt, hd=hd: eng.tensor_scalar(
                        otok[:, tt, hd * 128:(hd + 1) * 128], psf(b)[:, 0:128], rcs, None, ALU.mult),
                        reads=[bankreg[b], rc_r], writes=[otok_r])
            for tt in range(4):
                transpose_to(oT[:, 0:4, tt * 128:(tt + 1) * 128],
                             [otok[:, tt, hd * 128:(hd + 1) * 128] for hd in range(4)], [otok_r], oT_r, "act")
            for nt in range(8):
                s = nt % 2
                P.dma("pool", wo[s], xa_wo.rearrange("(h p) n -> p h n", p=128)[:, :, nt * 512:(nt + 1) * 512],
                      writes=[wo_r[s]])
                for tt in range(4):
                    b = nextbank()
                    mm_group(psf(b), [(oT[:, hd, tt * 128:(tt + 1) * 128], wo[s][:, hd, :]) for hd in range(4)],
                             [oT_r, wo_r[s]], b)
                    hs = H[:, tt, nt * 512:(nt + 1) * 512]
                    P.op("dve", lambda eng, hs=hs, b=b: eng.scalar_tensor_tensor(
                        out=hs, in0=hs, scalar=ALPHA, in1=psf(b), op0=ALU.mult, op1=ALU.add),
                        reads=[bankreg[b], Hr[tt]], writes=[Hr[tt]])
            layer_norm(H, Hr, HT, HT_r, GB, GB_r, HB, HB_r, stats, stats_r)
            P.barrier()
            SB.reset(mX)
            for tt in range(4):
                P.op("act", lambda eng, tt=tt: eng.mul(H[:, tt, :], H[:, tt, :], ALPHA), reads=[Hr[tt]], writes=[Hr[tt]])
            G = 4
            NW1, NW2 = 4, 6
            w1 = [SB.take([NKC, 128], BF16) for _ in range(NW1)]
            w1_r = [Reg("w1_%d" % i) for i in range(NW1)]
            w2 = [SB.take([D], BF16) for _ in range(NW2)]
            w2_r = [Reg("w2_%d" % i) for i in range(NW2)]
            hid = [SB.take([TB], BF16) for _ in range(NW2)]
            hid_r = [Reg("hid%d" % i) for i in range(NW2)]
            rl = [SB.take([TB], F32) for _ in range(2)]
            rl_r = [Reg("rl0"), Reg("rl1")]
            nfc = DFF // 128

            def ld_w1(g):
                for c in range(G):
                    fc = g * G + c
                    P.dma("pool", w1[fc % NW1], w_ff1_t[fc].rearrange("p (k n) -> p k n", k=NKC), writes=[w1_r[fc % NW1]])

            def ld_w2(g):
                for c in range(G):
                    fc = g * G + c
                    P.dma("pool", w2[fc % NW2], w_ff2[fc * 128:(fc + 1) * 128, :], writes=[w2_r[fc % NW2]])

            ld_w1(0)
            for g in range(nfc // G):
                ld_w2(g)
                for c in range(G):
                    fc = g * G + c
                    s1 = fc % NW1
                    s2 = fc % NW2
                    b = nextbank(0, 2)
                    mm_group(psf(b), [(w1[s1][:, kc, :], HT[:, kc, :]) for kc in range(NKC)], [w1_r[s1], HT_r], b)
                    k = fc % 2
                    P.op("act", lambda eng, b=b, k=k: eng.activation(rl[k], psf(b), AF.Relu), reads=[bankreg[b]], writes=[rl_r[k]])
                    P.op("act", lambda eng, k=k, s2=s2: eng.activation(hid[s2], rl[k], AF.Square), reads=[rl_r[k]], writes=[hid_r[s2]])
                if g + 1 < nfc // G:
                    ld_w1(g + 1)
                for tt in range(4):
                    for nt in range(8):
                        b = nextbank(2, 8)
                        prs = []
                        rds = []
                        for c in range(G):
                            s2 = (g * G + c) % NW2
                            prs.append((hid[s2][:, tt * 128:(tt + 1) * 128], w2[s2][:, nt * 512:(nt + 1) * 512]))
                            rds += [hid_r[s2], w2_r[s2]]
                        mm_group(psf(b), prs, rds, b)
                        hs = H[:, tt, nt * 512:(nt + 1) * 512]
                        P.op("dve", lambda eng, hs=hs, b=b: eng.tensor_tensor(hs, hs, psf(b), ALU.add),
                             reads=[bankreg[b], Hr[tt]], writes=[Hr[tt]])
            P.barrier()
            SB.reset(mX)
            GB, GB_r, HB, HB_r, stats, stats_r = take_ln()
            load_gb(GB, GB_r, 2)
            layer_norm(H, Hr, HT, HT_r, GB, GB_r, HB, HB_r, stats, stats_r, do_T=False)
            for tt in range(4):
                P.dma("sp", y[t0 + tt * 128:t0 + (tt + 1) * 128, :], H[:, tt, :], reads=[Hr[tt]])
            P.barrier()
            SB.reset(m0)

        def phase_a():
            mA = SB.mark()
            SL = SB.take([96], F32)
            XT = SB.take([NKC, 1024], BF16)
            XT_r = Reg("XT")
            WR = [SB.take([NKC, 256], BF16) for _ in range(2)]
            WR_r = [Reg("WR0"), Reg("WR1")]
            wslot = [0]

            def load_piece(col0, ncols=256):
                s = wslot[0] % 2
                wslot[0] += 1
                if ncols == 48:
                    P.dma("pool", WR[s][:, :, 0:48], w_gates, writes=[WR_r[s]])
                    return s
                for jj in range(ncols // 128):
                    P.dma("pool", WR[s][:, :, jj * 128:(jj + 1) * 128],
                          w_in_t[col0 // 128 + jj].rearrange("p (k n) -> p k n", k=NKC), writes=[WR_r[s]])
                return s

            ST = SB.take([8, 2, 256], F32)
            ST_r = [Reg("ST%d" % i) for i in range(8)]
            KSp = SB.take([4, 1024], BF16)
            VSp = SB.take([8, 4, 130], BF16)
            KWp = SB.take([4, 512], BF16)
            VWp = SB.take([4, 4, 130], BF16)
            KSp_r, VSp_r, KWp_r, VWp_r = Reg("KSp"), Reg("VSp"), Reg("KWp"), Reg("VWp")
            KC = SB.take([4, 2, 64], BF16)
            KC_r = Reg("KC")
            VC = SB.take([4, 2, 162], BF16, parts=64)
            VC_r = Reg("VC")
            ZT = SB.take([2, 4, 16], BF16)
            ZT_r = Reg("ZT")
            tabs_r = Reg("tabs")

            def ltab(name, dims, dt, parts=128):
                t = SB.take(dims, dt, parts=parts)
                P.dma("pool" if dt == BF16 else "sp", t, dram[name], writes=[tabs_r])
                return t

            P.dma("sp", VC, dram["vc_init"], writes=[VC_r])
            P.op("dve", lambda eng: eng.memset(VSp, 1.0), writes=[VSp_r])
            P.op("dve", lambda eng: eng.memset(VWp, 1.0), writes=[VWp_r])
            P.dma("pool", XT, wview(dram["xT_prev"]), writes=[XT_r])
            rs1 = ltab("ret_rs1", [8, 8], F32)
            mP1 = SB.mark()
            ktok = SB.take([8, 256], BF16)
            vtok = SB.take([8, 256], BF16)
            ktok_r, vtok_r = Reg("ktok"), Reg("vtok")
            ecnt = [0]

            def evac(dst, src, b, dst_reg, scale=None, func=None, extra_reads=()):
                ecnt[0] += 1
                if func is not None or scale is not None or ecnt[0] % 2 == 0:
                    f = func if func is not None else AF.Copy
                    if scale is None:
                        P.op("act", lambda eng: eng.activation(dst, src, f), reads=[bankreg[b]] + list(extra_reads), writes=[dst_reg])
                    else:
                        P.op("act", lambda eng: eng.activation(dst, src, f, scale=scale), reads=[bankreg[b]] + list(extra_reads), writes=[dst_reg])
                else:
                    P.op("dve", lambda eng: eng.tensor_copy(dst, src), reads=[bankreg[b]] + list(extra_reads), writes=[dst_reg])

            def proj_tm(s, col0, ncols, tile, xoff=0):
                b = nextbank()
                mm_group(psf(b)[:, 0:ncols],
                         [(XT[:, kc, tile * 128:(tile + 1) * 128], WR[s][:, kc, col0:col0 + ncols]) for kc in range(NKC)],
                         [XT_r, WR_r[s]], b)
                return b

            def proj_fm(s, col0, th):
                b = nextbank()
                mm_group(psf(b), [(WR[s][:, kc, col0:col0 + 128], XT[:, kc, th * 512:(th + 1) * 512]) for kc in range(NKC)],
                         [XT_r, WR_r[s]], b)
                return b

            for h in range(8):
                sk = load_piece(2048 + h * 256)
                sv = load_piece(4096 + h * 256)
                for n in range(8):
                    b = proj_tm(sk, 0, 256, n)
                    evac(ktok[:, n, :], psf(b)[:, 0:256], b, ktok_r, scale=rs1[:, h, n:n + 1], extra_reads=[tabs_r])
                for n in range(8):
                    b = proj_tm(sv, 0, 256, n)
                    evac(vtok[:, n, :], psf(b)[:, 0:256], b, vtok_r)
                for dh in range(2):
                    b = nextbank()
                    mm_group(psf(b)[:, 0:256], [(ktok[:, n, dh * 128:(dh + 1) * 128], vtok[:, n, :]) for n in range(8)],
                             [ktok_r, vtok_r], b)
                    evac(ST[:, h, dh, :], psf(b)[:, 0:256], b, ST_r[h])
            P.barrier()
            SB.reset(mP1)
            W1 = SB.take([2, 32, 128], BF16)
            W2 = SB.take([2, 128], BF16)
            posT = SB.take([2, 32], BF16)
            cw_r = Reg("cw")
            P.dma("pool", W1, dram["cmp_w1"].rearrange("k (j p) n -> p k j n", p=128), writes=[cw_r])
            P.dma("pool", W2, dram["cmp_w2"].rearrange("k p n -> p k n"), writes=[cw_r])
            P.dma("pool", posT, dram["cmp_posT"].rearrange("k p j -> p k j"), writes=[cw_r])
            cbias = SB.take([2], F32)
            cbias_r = Reg("cbias")
            for kv in range(2):
                b = nextbank()
                mm_group(psf(b)[:, 0:1], [(W1[:, kv, j, :], posT[:, kv, j:j + 1]) for j in range(32)], [cw_r], b)
                evac(cbias[:, kv:kv + 1], psf(b)[:, 0:1], b, cbias_r)
            mZ = SB.mark()
            zb = SB.take([2, 4, 1040], BF16)
            zb_r = Reg("zb")
            gel = SB.take([6, 64], F32)
            gel_r = Reg("gel")
            hidc = SB.take([64], BF16)
            hidc_r = Reg("hidc")

            def compress(kv, g, tile, NB):
                b = nextbank()
                zz = zb[:, kv, g, :]
                mm_group(psf(b)[:, 0:NB], [(W1[:, kv, j, :], zz[:, j:j + 16 * (NB - 1) + 1:16]) for j in range(32)], [cw_r, zb_r], b)
                u = gel[:, 0, 0:NB]
                t1 = gel[:, 1, 0:NB]
                t2 = gel[:, 2, 0:NB]
                sg = gel[:, 3, 0:NB]
                P.op("act", lambda eng: eng.activation(u, psf(b)[:, 0:NB], AF.Identity, bias=cbias[:, kv:kv + 1], scale=1.0),
                     reads=[bankreg[b], cbias_r], writes=[gel_r])
                P.op("dve", lambda eng: eng.tensor_tensor(t1, u, u, ALU.mult), reads=[gel_r], writes=[gel_r])
                P.op("dve", lambda eng: eng.tensor_scalar(t2, t1, 0.044715, 1.0, ALU.mult, ALU.add), reads=[gel_r], writes=[gel_r])
                P.op("dve", lambda eng: eng.tensor_tensor(t1, t2, u, ALU.mult), reads=[gel_r], writes=[gel_r])
                P.op("act", lambda eng: eng.activation(sg, t1, AF.Sigmoid, scale=1.5957691216057308), reads=[gel_r], writes=[gel_r])
                P.op("dve", lambda eng: eng.tensor_tensor(hidc[:, 0:NB], u, sg, ALU.mult), reads=[gel_r], writes=[hidc_r])
                b2 = nextbank()
                if kv == 0:
                    mm_group(psf(b2)[:, 0:NB], [(W2[:, 0, :], hidc[:, 0:NB])], [cw_r, hidc_r], b2)
                    evac(KC[:, g, tile, 0:NB], psf(b2)[:, 0:NB], b2, KC_r)
                else:
                    mm_group(psf(b2)[0:NB, 0:128], [(hidc[:, 0:NB], W2[:, 1, :])], [cw_r, hidc_r], b2)
                    evac(VC[0:NB, g, tile, 0:128], psf(b2)[0:NB, 0:128], b2, VC_r)

            def nsa_kv_pass(is_prev):
                for j in range(6):
                    if (not is_prev) and j >= 2:
                        break
                    for gp in range(2):
                        s = load_piece(10240 + j * 512 + gp * 256)
                        for gi in range(2):
                            g = gp * 2 + gi
                            if j in (0, 1):
                                for th in range(2):
                                    b = proj_fm(s, gi * 128, th)
                                    off = (0 if is_prev else 16) + th * 512
                                    evac(zb[:, j, g, off:off + 512], psf(b), b, zb_r)
                            elif j == 2:
                                for th in range(2):
                                    b = proj_fm(s, gi * 128, th)
                                    evac(KSp[:, g, th * 512:(th + 1) * 512], psf(b), b, KSp_r)
                            elif j == 4:
                                b = proj_fm(s, gi * 128, 1)
                                evac(KWp[:, g, :], psf(b), b, KWp_r)
                        if j == 3:
                            for n in range(8):
                                b = proj_tm(s, 0, 256, n)
                                evac(VSp[:, n, gp * 2:gp * 2 + 2, 0:128], psf(b)[:, 0:256].rearrange("p (g d) -> p g d", g=2), b, VSp_r)
                        elif j == 5:
                            for n in range(4, 8):
                                b = proj_tm(s, 0, 256, n)
                                evac(VWp[:, n - 4, gp * 2:gp * 2 + 2, 0:128], psf(b)[:, 0:256].rearrange("p (g d) -> p g d", g=2), b, VWp_r)

            nsa_kv_pass(True)
            for kv in range(2):
                for g in range(4):
                    compress(kv, g, 0, 63)
            P.op("dve", lambda eng: eng.tensor_copy(ZT, zb[:, :, :, 1008:1024]), reads=[zb_r], writes=[ZT_r])
            P.barrier()
            P.dma("pool", XT, wview(dram["xT_own"]), writes=[XT_r])
            P.op("dve", lambda eng: eng.tensor_copy(zb[:, :, :, 0:16], ZT), reads=[ZT_r], writes=[zb_r])
            nsa_kv_pass(False)
            for kv in range(2):
                for g in range(4):
                    compress(kv, g, 1, 64)
            P.barrier()
            SB.reset(mP1)
            mscr_r = Reg("mixscr")
            decT = ltab("ret_decT", [8, 128], F32)
            gq = ltab("ret_gq", [8, 128], F32)
            rs2 = ltab("ret_rs2", [8], F32)
            mR = SB.mark()
            class _B:
                pass
            RB = []
            for i in range(2):
                B_ = _B()
                B_.qT = SB.take([2, 1024], BF16)
                B_.kTt = SB.take([2, 1024], BF16)
                B_.vtok = SB.take([8, 256], BF16)
                B_.gs = SB.take([8, 256], BF16)
                B_.qT_r, B_.kTt_r, B_.vtok_r, B_.gs_r = (Reg(n_ + str(i)) for n_ in ("qT", "kTt", "vtok", "gs"))
                RB.append(B_)
            qhT = SB.take([2, 1024], BF16)
            ktok = SB.take([8, 256], BF16)
            Sbf = SB.take([2, 256], BF16)
            PT = [SB.take([128], BF16) for _ in range(2)]
            yn = SB.take([256], F32)
            rout = SB.take([8, 256], BF16)
            mst = SB.take([2, 1024], BF16)
            qhT_r, ktok_r, Sbf_r = (Reg(n) for n in ("qhT", "ktok", "Sbf"))
            PT_r = [Reg("PT0"), Reg("PT1")]
            yn_r, rout_r, mst_r, SL_r = Reg("yn"), Reg("rout"), Reg("mst"), Reg("SL")
            GAM = [1.0 - 2.0 ** (-5.0 - h) for h in range(8)]

            def proj_gen(h, B_):
                sq = load_piece(h * 256)
                for dh in range(2):
                    for th in range(2):
                        b = proj_fm(sq, dh * 128, th)
                        evac(B_.qT[:, dh, th * 512:(th + 1) * 512], psf(b), b, B_.qT_r)
                        yield
                sk = load_piece(2048 + h * 256)
                for dh in range(2):
                    for th in range(2):
                        b = proj_fm(sk, dh * 128, th)
                        evac(B_.kTt[:, dh, th * 512:(th + 1) * 512], psf(b), b, B_.kTt_r)
                        yield
                sv = load_piece(4096 + h * 256)
                for n in range(8):
                    b = proj_tm(sv, 0, 256, n)
                    evac(B_.vtok[:, n, :], psf(b)[:, 0:256], b, B_.vtok_r)
                    yield
                sg = load_piece(6144 + h * 256)
                for n in range(8):
                    b = proj_tm(sg, 0, 256, n)
                    evac(B_.gs[:, n, :], psf(b)[:, 0:256], b, B_.gs_r, func=AF.Silu)
                    yield

            for _ in proj_gen(0, RB[0]):
                pass
            for h in range(8):
                B_ = RB[h % 2]
                qT, kTt, vtok, gs = B_.qT, B_.kTt, B_.vtok, B_.gs
                qT_r, kTt_r, vtok_r, gs_r = B_.qT_r, B_.kTt_r, B_.vtok_r, B_.gs_r
                nxt = proj_gen(h + 1, RB[(h + 1) % 2]) if h < 7 else iter(())

                def pull(k, nxt=nxt):
                    for _ in range(k):
                        next(nxt, None)

                for dh in range(2):
                    for n in range(8):
                        P.op("dve", lambda eng, dh=dh, n=n, h=h, qT=qT: eng.tensor_tensor(
                            qhT[:, dh, n * 128:(n + 1) * 128], qT[:, dh, n * 128:(n + 1) * 128], gq[:, h, :], ALU.mult),
                            reads=[qT_r, tabs_r], writes=[qhT_r])
                for n in range(8):
                    b = nextbank()
                    pb = psb(b)
                    for dh in range(2):
                        P.op("pe", lambda eng, dh=dh, n=n, pb=pb, kTt=kTt: eng.transpose(pb[:, dh * 128:(dh + 1) * 128], kTt[:, dh, n * 128:(n + 1) * 128], ident),
                             reads=[kTt_r, ident_r], writes=[bankreg[b]], sig=(dh == 1))
                    evac(ktok[:, n, :], pb[:, 0:256], b, ktok_r, scale=rs2[:, h:h + 1], extra_reads=[tabs_r])
                P.op("act", lambda eng, h=h: eng.copy(Sbf, ST[:, h, :, :]), reads=[ST_r[h]], writes=[Sbf_r])
                for n in range(8):
                    cs = slice(n * 128, (n + 1) * 128)
                    b1 = nextbank()
                    mm_group(psf(b1)[:, 0:128], [(kTt[:, dh, cs], qT[:, dh, cs]) for dh in range(2)], [kTt_r, qT_r], b1)
                    p = n % 2
                    P.op("dve", lambda eng, b1=b1, p=p, h=h: eng.tensor_tensor(PT[p], psf(b1)[:, 0:128], decT[:, h, :], ALU.mult),
                         reads=[bankreg[b1], tabs_r], writes=[PT_r[p]])
                    pull(1)
                    b2 = nextbank()
                    mm_group(psf(b2)[:, 0:256], [(PT[p], vtok[:, n, :])] + [(qhT[:, dh, cs], Sbf[:, dh, :]) for dh in range(2)],
                             [PT_r[p], vtok_r, qhT_r, Sbf_r], b2)
                    o = psf(b2)[:, 0:256]
                    pull(1)
                    P.op("dve", lambda eng, o=o: eng.bn_stats(SL[:, 0:6], o), reads=[bankreg[b2]], writes=[SL_r])
                    P.op("dve", lambda eng: eng.bn_aggr(SL[:, 8:10], SL[:, 0:6]), reads=[SL_r], writes=[SL_r])
                    P.op("act", lambda eng: eng.activation(SL[:, 10:11], SL[:, 9:10], AF.Sqrt, bias=EPS_AP[0], scale=1.0),
                         reads=[SL_r, small_r], writes=[SL_r])
                    P.op("dve", lambda eng: eng.reciprocal(SL[:, 10:11], SL[:, 10:11]), reads=[SL_r], writes=[SL_r])
                    P.op("dve", lambda eng: eng.scalar_tensor_tensor(out=SL[:, 11:12], in0=SL[:, 8:9], scalar=-1.0, in1=SL[:, 10:11],
                                                                   op0=ALU.mult, op1=ALU.mult), reads=[SL_r], writes=[SL_r])
                    P.op("act", lambda eng, o=o: eng.activation(yn, o, AF.Identity, bias=SL[:, 11:12], scale=SL[:, 10:11]),
                         reads=[SL_r, bankreg[b2]], writes=[yn_r])
                    P.op("dve", lambda eng, n=n, gs=gs: eng.tensor_tensor(rout[:, n, :], yn, gs[:, n, :], ALU.mult),
                         reads=[yn_r, gs_r], writes=[rout_r])
                    pull(1)
                    if n < 7:
                        for dh in range(2):
                            b3 = nextbank()
                            mm_group(psf(b3)[:, 0:256], [(ktok[:, n, dh * 128:(dh + 1) * 128], vtok[:, n, :])], [ktok_r, vtok_r], b3)
                            P.op("dve", lambda eng, b3=b3, dh=dh, h=h: eng.scalar_tensor_tensor(
                                out=ST[:, h, dh, :], in0=ST[:, h, dh, :], scalar=GAM[h] ** 128, in1=psf(b3)[:, 0:256],
                                op0=ALU.mult, op1=ALU.add), reads=[bankreg[b3], ST_r[h]], writes=[ST_r[h]])
                        P.op("act", lambda eng, h=h: eng.copy(Sbf, ST[:, h, :, :]), reads=[ST_r[h]], writes=[Sbf_r])
                for _ in nxt:
                    pass
                for n in range(8):
                    transpose_to(mst[:, 0:2, n * 128:(n + 1) * 128], [rout[:, n, e * 128:(e + 1) * 128] for e in range(2)],
                                 [rout_r], mst_r, "act" if n % 2 else "dve")
                P.dma("sp", mixT_scr.rearrange("(a p) t -> p a t", p=128)[:, 2 * h:2 * h + 2, :], mst, reads=[mst_r], writes=[mscr_r])
            P.barrier()
            SB.reset(mP1)
            cmask = ltab("cmp_mask", [2, 8, 128], BF16, parts=64)
            cbi = ltab("cmp_bias", [2, 16, 8], F32, parts=64)
            ikeep = ltab("imp_keep", [8, 32], F32)
            iadd = ltab("imp_add", [8, 32], F32)
            sbias = ltab("slc_bias", [16, 16, 8], F32)
            E2 = ltab("e2", [16, 128], BF16, parts=32)
            tri = ltab("tri", [2, 128], BF16)
            gsig = SB.take([8, 48], F32)
            gsig_r = Reg("gsig")
            sgt = load_piece(13312, ncols=48)
            for qt in range(8):
                b = proj_tm(sgt, 0, 48, qt)
                evac(gsig[:, qt, :], psf(b)[:, 0:48], b, gsig_r, func=AF.Sigmoid)
            qn = SB.take([4, 1024], BF16)
            KSo = SB.take([1024], BF16)
            VSo = SB.take([8, 130], BF16)
            KWo = SB.take([1024], BF16)
            VWo = SB.take([8, 130], BF16)
            qn_r, KSo_r, VSo_r, KWo_r, VWo_r = (Reg(n) for n in ("qn", "KSo", "VSo", "KWo", "VWo"))
            pc = SB.take([2, 128], BF16, parts=64)
            pc_r = Reg("pc")
            pS = [SB.take([4, 128], BF16) for _ in range(4)]
            pS_r = [[Reg("pS%d_%d" % (i, r)) for r in range(4)] for i in range(4)]
            R4 = SB.take([4, 128], BF16, parts=32)
            R4_r = Reg("R4")
            tri4 = ltab("tri4", [2, 4, 128], BF16)
            SCB = (0, 1, 7)
            scb = [0]
            psi = [0]
            imp = SB.take([4, 32], F32)
            imp_r = Reg("imp")
            m8 = SB.take([16], F32)
            selb = SB.take([32], BF16)
            selb_r = Reg("selb")
            Rm = SB.take([128], BF16, parts=32)
            Rm_r = Reg("Rm")
            acc4 = SB.take([4, 128], F32)
            acc4_r = [Reg("acc%d" % i) for i in range(4)]
            ost = SB.take([4, 128], BF16)
            ost_r = Reg("ost")
            mst4 = SB.take([4, 1024], BF16)
            mst4_r = Reg("mst4")
            cf = SB.take([16], F32)
            cf_r = Reg("cf")
            P.op("dve", lambda eng: eng.memset(VSo, 1.0), writes=[VSo_r])
            P.op("dve", lambda eng: eng.memset(VWo, 1.0), writes=[VWo_r])
            SC = 128.0 ** -0.5

            def coef(b, col, gcol, k):
                P.op("dve", lambda eng: eng.tensor_scalar(cf[:, k:k + 1], psf(b)[:, col:col + 1], 1e-30, None, ALU.add),
                     reads=[bankreg[b]], writes=[cf_r])
                P.op("dve", lambda eng: eng.reciprocal(cf[:, k:k + 1], cf[:, k:k + 1]), reads=[cf_r], writes=[cf_r])
                if gcol is not None:
                    P.op("dve", lambda eng: eng.tensor_tensor(cf[:, k + 4:k + 5], cf[:, k:k + 1], gcol, ALU.mult),
                         reads=[cf_r, gsig_r], writes=[cf_r])

            for g in range(4):
                for rp in range(2):
                    s = load_piece(8192 + g * 512 + rp * 256)
                    for ri in range(2):
                        for th in range(2):
                            b = proj_fm(s, ri * 128, th)
                            evac(qn[:, rp * 2 + ri, th * 512:(th + 1) * 512], psf(b), b, qn_r, scale=SC)
                for pi, (Ko, Ko_r, Vo, Vo_r) in enumerate(((KSo, KSo_r, VSo, VSo_r), (KWo, KWo_r, VWo, VWo_r))):
                    s = wslot[0] % 2
                    wslot[0] += 1
                    for jj in range(2):
                        c0_ = 11264 + (2 * pi + jj) * 512 + g * 128
                        P.dma("pool", WR[s][:, :, jj * 128:(jj + 1) * 128],
                              w_in_t[c0_ // 128].rearrange("p (k n) -> p k n", k=NKC), writes=[WR_r[s]])
                    for th in range(2):
                        b = proj_fm(s, 0, th)
                        evac(Ko[:, th * 512:(th + 1) * 512], psf(b), b, Ko_r)
                    for n in range(8):
                        b = proj_tm(s, 128, 128, n)
                        evac(Vo[:, n, 0:128], psf(b)[:, 0:128], b, Vo_r)
                for qt in range(8):
                    qs = slice(qt * 128, (qt + 1) * 128)
                    qtc = 8 + qt
                    for r in range(4):
                        hd = g * 4 + r
                        b = nextbank(0, 2)
                        for tile, NB in ((0, 63), (1, 64)):
                            oc = psf(b)[0:NB, tile * 128:(tile + 1) * 128]
                            P.op("pe", lambda eng, oc=oc, l_=KC[:, g, tile, 0:NB], r_=qn[:, r, qs]: eng.matmul(oc, l_, r_, start=True, stop=False),
                                 reads=[KC_r, qn_r], writes=[bankreg[b]], sig=False)
                            P.op("pe", lambda eng, oc=oc, l_=ident[0:NB, 0:NB], r_=cmask[0:NB, tile, qt, :]: eng.matmul(oc, l_, r_, start=False, stop=True),
                                 reads=[ident_r, tabs_r], writes=[bankreg[b]], sig=True)
                            P.op("act", lambda eng, oc=oc, o_=pc[0:NB, tile, :], b_=cbi[0:NB, tile, hd, qt:qt + 1]: eng.activation(
                                o_, oc, AF.Exp, bias=b_, scale=1.0),
                                reads=[bankreg[b], tabs_r], writes=[pc_r])
                        b2 = 2
                        mm_group(psf(b2)[:, 0:162], [(pc[0:NB, tile, :], VC[0:NB, g, tile, :]) for tile, NB in ((0, 63), (1, 64))],
                                 [pc_r, VC_r], b2)
                        coef(b2, 128, gsig[:, qt, hd:hd + 1], r)
                        if r == 0:
                            P.op("dve", lambda eng, r=r: eng.tensor_scalar(imp[:, 0, :], psf(2)[:, 130:162], cf[:, r:r + 1], None, ALU.mult),
                                 reads=[bankreg[2], cf_r], writes=[imp_r])
                        else:
                            P.op("dve", lambda eng, r=r: eng.scalar_tensor_tensor(out=imp[:, 0, :], in0=psf(2)[:, 130:162], scalar=cf[:, r:r + 1],
                                                                               in1=imp[:, 0, :], op0=ALU.mult, op1=ALU.add),
                                 reads=[bankreg[2], cf_r, imp_r], writes=[imp_r])
                        P.op("dve", lambda eng, r=r: eng.tensor_scalar(acc4[:, r, :], psf(2)[:, 0:128], cf[:, r + 4:r + 5], None, ALU.mult),
                             reads=[bankreg[2], cf_r], writes=[acc4_r[r]])
                    P.op("dve", lambda eng, k_=ikeep[:, qt, :]: eng.tensor_tensor(imp[:, 1, :], imp[:, 0, :], k_, ALU.mult), reads=[imp_r, tabs_r], writes=[imp_r])
                    P.op("dve", lambda eng, k_=iadd[:, qt, :]: eng.tensor_tensor(imp[:, 1, :], imp[:, 1, :], k_, ALU.add), reads=[imp_r, tabs_r], writes=[imp_r])
                    P.op("dve", lambda eng: eng.max(m8[:, 0:8], imp[:, 1, :]), reads=[imp_r], writes=[imp_r])
                    P.op("dve", lambda eng: eng.match_replace(imp[:, 2, :], m8[:, 0:8], imp[:, 1, :], -3.0e38), reads=[imp_r], writes=[imp_r])
                    P.op("dve", lambda eng: eng.max(m8[:, 8:16], imp[:, 2, :]), reads=[imp_r], writes=[imp_r])
                    P.op("dve", lambda eng: eng.tensor_scalar(imp[:, 3, :], imp[:, 1, :], m8[:, 15:16], None, ALU.is_ge), reads=[imp_r], writes=[imp_r])
                    P.op("dve", lambda eng: eng.tensor_scalar(selb, imp[:, 3, :], -NEGM, NEGM, ALU.mult, ALU.add), reads=[imp_r], writes=[selb_r])
                    bt = 7
                    tasks = []
                    for br, klist in ((2, list(range(qtc - 4, qtc + 1))), (1, list(range(0, qtc + 1)))):
                        for i, kt in enumerate(klist):
                            tasks.append((br, i, kt, i == len(klist) - 1))

                    def emit_score(task):
                        br, i, kt, lastk = task
                        b = SCB[scb[0] % len(SCB)]
                        scb[0] += 1
                        if br == 1:
                            Kap, Kr = (KSp[:, g, kt * 128:(kt + 1) * 128], KSp_r) if kt < 8 else (KSo[:, (kt - 8) * 128:(kt - 7) * 128], KSo_r)
                            Vap, Vr = (VSp[:, kt, g, :], VSp_r) if kt < 8 else (VSo[:, kt - 8, :], VSo_r)
                        else:
                            Kap, Kr = (KWp[:, g, (kt - 4) * 128:(kt - 3) * 128], KWp_r) if kt < 8 else (KWo[:, (kt - 8) * 128:(kt - 7) * 128], KWo_r)
                            Vap, Vr = (VWp[:, kt - 4, g, :], VWp_r) if kt < 8 else (VWo[:, kt - 8, :], VWo_r)
                        extra = []
                        if br == 1:
                            extra.append((E2[:, kt, :], R4, [tabs_r, R4_r]))
                        if kt == qtc:
                            extra.append((ident, tri4[:, 0, :, :], [ident_r, tabs_r]))
                        if br == 2 and kt == qtc - 4:
                            extra.append((ident, tri4[:, 1, :, :], [ident_r, tabs_r]))
                        sc = psf(b).rearrange("p (r t) -> p r t", r=4)
                        P.op("pe", lambda eng, sc=sc, Kap=Kap, q_=qn[:, 0:4, qs], ne=len(extra): eng.matmul(sc, Kap, q_, start=True, stop=(ne == 0)),
                             reads=[Kr, qn_r], writes=[bankreg[b]], sig=(len(extra) == 0))
                        for ei, (l_, r_, rg_) in enumerate(extra):
                            last = ei == len(extra) - 1
                            P.op("pe", lambda eng, sc=sc, l_=l_, r_=r_, last=last: eng.matmul(sc, l_, r_, start=False, stop=last),
                                 reads=rg_, writes=[bankreg[b]], sig=last)
                        pi_ = psi[0] % len(pS)
                        psi[0] += 1
                        for r in range(4):
                            P.op("act", lambda eng, b=b, r=r, pi_=pi_, b_=sbias[:, g * 4 + r, kt, qt:qt + 1]: eng.activation(
                                pS[pi_][:, r, :], psf(b)[:, r * 128:(r + 1) * 128], AF.Exp, bias=b_, scale=1.0),
                                reads=[bankreg[b], tabs_r], writes=[pS_r[pi_][r]])
                        return (br, i, lastk, pi_, Vap, Vr)

                    def emit_pv(info):
                        br, i, lastk, pi_, Vap, Vr = info
                        for r in range(4):
                            bo = 3 + r
                            hd = g * 4 + r
                            P.op("pe", lambda eng, bo=bo, r=r, pi_=pi_, Vap=Vap, i=i, lastk=lastk: eng.matmul(psf(bo)[:, 0:130], pS[pi_][:, r, :], Vap, start=(i == 0), stop=lastk),
                                 reads=[pS_r[pi_][r], Vr], writes=[bankreg[bo]], sig=lastk)
                            if lastk:
                                coef(bo, 128, gsig[:, qt, br * 16 + hd:br * 16 + hd + 1], br)
                                P.op("dve", lambda eng, bo=bo, br=br, r=r: eng.scalar_tensor_tensor(
                                    out=acc4[:, r, :], in0=psf(bo)[:, 0:128], scalar=cf[:, br + 4:br + 5], in1=acc4[:, r, :], op0=ALU.mult, op1=ALU.add),
                                    reads=[bankreg[bo], cf_r, acc4_r[r]], writes=[acc4_r[r]])
                                if br == 1:
                                    P.op("act", lambda eng, r=r: eng.copy(ost[:, r, :], acc4[:, r, :]), reads=[acc4_r[r]], writes=[ost_r])

                    LAG = 2
                    pend = []
                    for task in tasks:
                        if task[0] == 1 and task[1] == 0:
                            P.op("pe", lambda eng: eng.transpose(psb(bt)[0:32, 0:128], selb, ident), reads=[selb_r, ident_r], writes=[bankreg[bt]])
                            for r in range(4):
                                if r % 2 == 0:
                                    P.op("act", lambda eng, r=r: eng.copy(R4[:, r, :], psb(bt)[0:32, 0:128]), reads=[bankreg[bt]], writes=[R4_r])
                                else:
                                    P.op("dve", lambda eng, r=r: eng.tensor_copy(R4[:, r, :], psb(bt)[0:32, 0:128]), reads=[bankreg[bt]], writes=[R4_r])
                        pend.append(emit_score(task))
                        if len(pend) > LAG:
                            emit_pv(pend.pop(0))
                    while pend:
                        emit_pv(pend.pop(0))
                    transpose_to(mst4[:, 0:4, qs], [ost[:, r, :] for r in range(4)], [ost_r], mst4_r, "dve")
                P.dma("sp", mixT_scr.rearrange("(a p) t -> p a t", p=128)[:, 16 + 4 * g:20 + 4 * g, :], mst4, reads=[mst4_r], writes=[mscr_r])
            P.barrier()
            SB.reset(mA)
            return mscr_r

        P.op("dve", lambda eng: eng.memset(small[:, 0:1], EPS), writes=[small_r])
        EPS_AP[0] = small[:, 0:1]
        mscr_r = None
        if cfg.get("phaseA", True):
            mscr_r = phase_a()
        mix_src = mix_in if not cfg.get("phaseA", True) else mixT_scr
        if not cfg.get("dbgA"):
            xattn_kv()
            for tb in range(cfg.get("ntb", 2)):
                token_block(tb, mix_src)
        P.final_wait("sp")
        P.replay()
    return nc


def _tables(hf):
    f64 = np.float64
    t = {}
    gam = 1.0 - 2.0 ** (-5.0 - np.arange(8, dtype=f64))
    p = np.arange(128, dtype=f64)
    n = np.arange(8, dtype=f64)
    t["ret_rs1"] = (gam[None, :, None] ** (1023.0 - (128.0 * n[None, None, :] + p[:, None, None])) / 16.0)
    dcs = p[None, :] - p[:, None]
    dec = np.where(dcs[:, None, :] >= 0, gam[None, :, None] ** np.maximum(dcs[:, None, :], 0.0), 0.0) / 16.0
    t["ret_decT"] = dec
    t["ret_gq"] = np.broadcast_to(gam[None, :, None] ** (p[None, None, :] + 1.0), (128, 8, 128))
    t["ret_rs2"] = gam[None, :] ** (127.0 - p[:, None]) / 16.0
    slopes = 2.0 ** (-8.0 * np.arange(1, 17, dtype=f64) / 16.0)
    vstart = 1024 * (1 - hf)
    qt = np.arange(8)
    bmid = 1024.0 + 128.0 * qt + 64.0
    pc_ = np.arange(64)
    cidx = np.stack([pc_, 63 + pc_], axis=1)
    cend = 16 * cidx + 31
    ctx_t = 1024 + 128 * qt[:, None] + np.arange(128)[None, :]
    valid = (cend[:, :, None, None] <= ctx_t[None, None]) & (16 * cidx[:, :, None, None] >= vstart)
    valid[63, 0] = False
    t["cmp_mask"] = np.where(valid, 0.0, NEGM)
    t["cmp_bias"] = slopes[None, None, :, None] * (cend[:, :, None, None] - bmid[None, None, None, :])
    s_ = np.arange(32)
    c0 = 16.0 * cidx[:, :, None]
    ov = np.clip(np.minimum(c0 + 32, 64.0 * s_ + 64) - np.maximum(c0, 64.0 * s_), 0, None) / 32.0
    vci = np.zeros((64, 4, 2, 162), f64)
    vci[:, :, :, 128:130] = 1.0
    vci[:, :, :, 130:162] = ov[:, None, :, :]
    t["vc_init"] = vci.astype(ml_dtypes.bfloat16)
    ctxp = 1024 + 128 * qt[None, :, None] + np.arange(128)[:, None, None]
    cur = ctxp // 64
    blk = np.arange(32)[None, None, :]
    blk0 = 16 * (1 - hf)
    forced = (blk == blk0) | (blk == cur) | (blk == cur - 1)
    dead = (blk > cur) | (blk < blk0)
    t["imp_keep"] = np.where(forced | dead, 0.0, 1.0)
    t["imp_add"] = np.where(forced, 1e9, np.where(dead, -1e9, 0.0))
    kt = np.arange(16)
    kpos = 128 * kt[None, None, :, None] + np.arange(128)[:, None, None, None]
    sb = slopes[None, :, None, None] * (kpos - bmid[None, None, None, :])
    t["slc_bias"] = np.where(kpos >= vstart, sb, -1e30)
    key = np.arange(128)
    t["e2"] = (np.arange(32)[:, None, None] == (2 * kt[None, :, None] + key[None, None, :] // 64)).astype(f64)
    tri = np.zeros((128, 2, 128), f64)
    tri[:, 0, :] = np.where(key[:, None] <= key[None, :], 0.0, NEGM)
    tri[:, 1, :] = np.where(key[:, None] > key[None, :], 0.0, NEGM)
    t["tri"] = tri
    t["tri4"] = np.broadcast_to(tri[:, :, None, :], (128, 2, 4, 128))
    return {k: (np.ascontiguousarray(v) if v.dtype == ml_dtypes.bfloat16 else np.ascontiguousarray(v, dtype=np.float32)) for k, v in t.items()}


def make_maps(inputs, cores, cfg):
    x = np.asarray(inputs["x"])
    maps = []
    tabs = [_tables(0), _tables(1)]
    ident = np.eye(128, dtype=np.float32)
    shared = {}
    if cfg.get("phaseA", True):
        wi = np.asarray(inputs["w_in"])[0]
        shared.update(w_in_t=np.ascontiguousarray(wi[:, :13312].reshape(NKC, 128, 104, 128).transpose(2, 1, 0, 3)).reshape(104, 128, NKC * 128),
                      w_gates=np.ascontiguousarray(wi[:, 13312:].reshape(NKC, 128, 48).transpose(1, 0, 2)), cmp_w1=np.asarray(inputs["cmp_w1"])[0], cmp_w2=np.asarray(inputs["cmp_w2"])[0],
                      cmp_posT=np.ascontiguousarray(np.asarray(inputs["cmp_pos"])[0].transpose(0, 2, 1)))
    if not cfg.get("dbgA"):
        def tl(w, n):
            k = w.shape[1] // n
            return np.ascontiguousarray(w.reshape(NKC, 128, k, n).transpose(2, 1, 0, 3)).reshape(k, 128, NKC * n)
        shared.update(w_out_t=tl(np.asarray(inputs["w_out"])[0], 512), xa_wq_t=tl(np.asarray(inputs["xa_wq"])[0], 128),
                      xa_wkv_t=tl(np.asarray(inputs["xa_wkv"])[0], 128),
                      xa_wo=np.asarray(inputs["xa_wo"])[0], w_ff1_t=tl(np.asarray(inputs["w_ff1"])[0], 128), w_ff2=np.asarray(inputs["w_ff2"])[0],
                      ln_g=np.asarray(inputs["ln_g"])[0], ln_b=np.asarray(inputs["ln_b"])[0])
    for c in cores:
        b, hf = c // 2, c % 2
        m = dict(shared)
        m["ident"] = ident
        own = x[b, hf * 1024:(hf + 1) * 1024]
        if cfg.get("phaseA", True):
            m["xT_own"] = np.ascontiguousarray(own.T)
            m["xT_prev"] = np.ascontiguousarray(x[b, 0:1024].T) if hf == 1 else np.zeros((D, 1024), np.float32)
            m.update(tabs[hf])
        if not cfg.get("dbgA"):
            m["x_own"] = np.ascontiguousarray(own)
            m["memT"] = np.ascontiguousarray(np.asarray(inputs["mem"])[b].T)
        maps.append(m)
    return maps


def kernel(**inputs):
    cfg = dict(phaseA=True)
    nc = build(cfg)
    cores = list(range(8))
    maps = make_maps(inputs, cores, cfg)
    res = run_bass_kernel_spmd(nc, maps, core_ids=cores)
    out = np.empty((4, 2048, D), np.float32)
    for c in cores:
        out[c // 2, (c % 2) * 1024:(c % 2 + 1) * 1024] = res.results[c]["y"]
    return out
```

```python
import contextlib
import math
import numpy as np
import ml_dtypes
import concourse.bass as bass
import concourse.mybir as mybir
from concourse.bass_utils import run_bass_kernel_spmd

F32, BF16 = mybir.dt.float32, mybir.dt.bfloat16
AF = mybir.ActivationFunctionType
ALU = mybir.AluOpType
AX = mybir.AxisListType

D = 4096
T_OWN = 1024
TB = 512
NKC = 32
ALPHA = 2.0 ** 0.25
EPS = 1e-5
IN_W = 13360
DFF = 16384
NEGM = -30000.0


class Reg:
    __slots__ = ("w", "r", "name")

    def __init__(self, name=""):
        self.w = None
        self.r = {}
        self.name = name


class Prog:
    ENG = ("pe", "act", "dve", "pool", "sp")

    def __init__(self, nc, stack, ndma=8):
        self.nc = nc
        self.q = {e: [] for e in self.ENG}
        self.sem = {}
        self.cnt = {}
        self.seen = {e: {} for e in self.ENG}
        self.fence = {e: {} for e in self.ENG}
        self.rr = {e: 0 for e in self.ENG}
        self.ndma = ndma
        for e in ("pe", "act", "dve", "pool"):
            self.sem[e] = stack.enter_context(nc.semaphore("s_" + e))
            self.cnt[e] = 0
        for e in ("sp", "pool", "act"):
            for i in range(ndma):
                k = ("d", e, i)
                self.sem[k] = stack.enter_context(nc.semaphore("d_%s_%d" % (e, i)))
                self.cnt[k] = 0

    def _waits(self, e, reads, writes):
        need = dict(self.fence[e])
        self.fence[e] = {}

        def add(k, v):
            if need.get(k, 0) < v:
                need[k] = v

        for r in reads:
            if r.w is not None:
                add(*r.w)
        for w in writes:
            if w.w is not None:
                add(*w.w)
            for k, v in w.r.items():
                add(k, v)
        out = []
        for k, v in need.items():
            if k == e and e == "pe":
                continue
            if self.seen[e].get(k, 0) >= v:
                continue
            self.seen[e][k] = v
            out.append((k, v))
        return out

    def _post(self, tok, reads, writes):
        for r in reads:
            if r.r.get(tok[0], 0) < tok[1]:
                r.r[tok[0]] = tok[1]
        for w in writes:
            w.w = tok
            w.r = {}

    def op(self, e, fn, reads=(), writes=(), sig=True):
        waits = self._waits(e, reads, writes)
        if sig:
            self.cnt[e] += 1
            tok = (e, self.cnt[e])
        else:
            tok = (e, self.cnt[e] + 1)
        sem = self.sem

        def run(eng):
            for k, v in waits:
                eng.wait_ge(sem[k], v)
            ins = fn(eng)
            if sig:
                ins.then_inc(sem[e], 1)

        self.q[e].append(run)
        self._post(tok, reads, writes)
        return tok

    def dma(self, e, out, in_, reads=(), writes=()):
        i = self.rr[e]
        self.rr[e] = (i + 1) % self.ndma
        key = ("d", e, i)
        waits = self._waits(e, reads, writes)
        prev = self.cnt[key]
        if prev > 0 and self.seen[e].get(key, 0) < prev:
            waits.append((key, prev))
            self.seen[e][key] = prev
        self.cnt[key] += 16
        tok = (key, self.cnt[key])
        sem = self.sem

        def run(eng):
            for k, v in waits:
                eng.wait_ge(sem[k], v)
            eng.dma_start(out=out, in_=in_).then_inc(sem[key], 16)

        self.q[e].append(run)
        self._post(tok, reads, writes)
        return tok

    def barrier(self):
        allt = {k: v for k, v in self.cnt.items() if v > 0}
        for e in self.ENG:
            for k, v in allt.items():
                if self.fence[e].get(k, 0) < v:
                    self.fence[e][k] = v

    def final_wait(self, e):
        self.barrier()
        waits = self._waits(e, (), ())
        sem = self.sem

        def run(eng):
            for k, v in waits:
                eng.wait_ge(sem[k], v)

        self.q[e].append(run)

    def replay(self):
        nc = self.nc
        q = self.q
        with nc.Block() as block:
            @block.tensor
            def _(eng):
                for f in q["pe"]:
                    f(eng)

            @block.scalar
            def _(eng):
                for f in q["act"]:
                    f(eng)

            @block.vector
            def _(eng):
                for f in q["dve"]:
                    f(eng)

            @block.gpsimd
            def _(eng):
                for f in q["pool"]:
                    f(eng)

            @block.sync
            def _(eng):
                for f in q["sp"]:
                    f(eng)


class SbAlloc:
    def __init__(self, nc, nbytes):
        self.words = nbytes // 4
        self.t = nc.alloc_sbuf_tensor("SB", [128, self.words], F32)
        self.off = 0

    def take(self, dims, dt, parts=128):
        n = int(np.prod(dims))
        sz = 2 if dt == BF16 else 4
        words = (n * sz + 3) // 4
        words = (words + 7) // 8 * 8
        assert self.off + words <= self.words, ("SBUF overflow", self.off, words, self.words)
        ap = self.t[0:parts, self.off:self.off + words]
        self.off += words
        if dt != F32:
            ap = ap.bitcast(dt)
        ap = ap[:, 0:n]
        if len(dims) == 2:
            ap = ap.rearrange("p (a b) -> p a b", a=dims[0])
        elif len(dims) == 3:
            ap = ap.rearrange("p (a b c) -> p a b c", a=dims[0], b=dims[1])
        elif len(dims) == 4:
            ap = ap.rearrange("p (a b c d) -> p a b c d", a=dims[0], b=dims[1], c=dims[2])
        return ap

    def mark(self):
        return self.off

    def reset(self, m):
        self.off = m


def build(cfg):
    nc = bass.Bass("TRN2", target_bir_lowering=False)
    dram = {}

    def din(name, shape, dt=F32):
        dram[name] = nc.dram_tensor(name, list(shape), dt, kind="ExternalInput").ap()
        return dram[name]

    if not cfg.get("dbgA"):
        x_own = din("x_own", [T_OWN, D])
        memT = din("memT", [D, 256])
        w_out_t = din("w_out_t", [8, 128, NKC * 512])
        xa_wq_t = din("xa_wq_t", [4, 128, NKC * 128])
        xa_wkv_t = din("xa_wkv_t", [8, 128, NKC * 128])
        xa_wo = din("xa_wo", [512, D])
        w_ff1_t = din("w_ff1_t", [DFF // 128, 128, NKC * 128])
        w_ff2 = din("w_ff2", [DFF, D])
        ln_g = din("ln_g", [3, D])
        ln_b = din("ln_b", [3, D])
    ident_d = din("ident", [128, 128])
    if cfg.get("phaseA", True):
        din("xT_prev", [D, 1024]); din("xT_own", [D, 1024])
        w_in_t = din("w_in_t", [104, 128, NKC * 128]); w_gates = din("w_gates", [128, NKC, 48])
        din("cmp_w1", [2, D, 128]); din("cmp_w2", [2, 128, 128]); din("cmp_posT", [2, 128, 32])
        din("vc_init", [64, 4, 2, 162], BF16)
        din("ret_rs1", [128, 8, 8]); din("ret_decT", [128, 8, 128]); din("ret_gq", [128, 8, 128]); din("ret_rs2", [128, 8])
        din("cmp_mask", [64, 2, 8, 128]); din("cmp_bias", [64, 2, 16, 8])
        din("imp_keep", [128, 8, 32]); din("imp_add", [128, 8, 32]); din("slc_bias", [128, 16, 16, 8])
        din("e2", [32, 16, 128]); din("tri", [128, 2, 128]); din("tri4", [128, 2, 4, 128])
    else:
        mix_in = din("mixT_in", [D, T_OWN], BF16)
    if not cfg.get("dbgA"):
        y = nc.dram_tensor("y", [T_OWN, D], F32, kind="ExternalOutput").ap()
    mixT_scr = nc.dram_tensor("mixT_scr", [D, T_OWN], BF16, kind=("ExternalOutput" if cfg.get("dbgA") else "Internal")).ap()

    stack = contextlib.ExitStack()
    with stack:
        P = Prog(nc, stack)
        SB = SbAlloc(nc, 206 * 1024)
        PS = nc.alloc_psum_tensor("PS", [128, 8, 512], F32)
        bankreg = [Reg("bank%d" % i) for i in range(8)]
        bank_rr = [0]

        def nextbank(lo=0, hi=8):
            b = lo + bank_rr[0] % (hi - lo)
            bank_rr[0] += 1
            return b

        def psf(b):
            return PS[:, b, :]

        def psb(b):
            return PS[:, b, :].bitcast(BF16)

        ident = SB.take([128], BF16)
        ident_r = Reg("ident")
        P.dma("pool", ident, ident_d, writes=[ident_r])
        kT = SB.take([4, 256], BF16)
        vaug = SB.take([2, 4, 130], BF16)
        kT_r, vaug_r = Reg("kT"), Reg("vaug")
        small = SB.take([64], F32)
        small_r = Reg("small")
        base_mark = SB.mark()

        def wview(ap, p=128):
            return ap.rearrange("(kc p) n -> p kc n", p=p)

        def mm_group(out_ap, pairs, reads, bank, extra_writes=()):
            n = len(pairs)
            for i, (l, r) in enumerate(pairs):
                P.op("pe", (lambda eng, l=l, r=r, i=i: eng.matmul(out_ap, l, r, start=(i == 0), stop=(i == n - 1))),
                     reads=reads, writes=[bankreg[bank]] + list(extra_writes), sig=(i == n - 1))

        def transpose_to(dst_aps, src_aps, src_regs, dst_reg, copy_eng):
            b = nextbank()
            n = len(src_aps)
            pb = psb(b)
            for j, s in enumerate(src_aps):
                P.op("pe", (lambda eng, s=s, j=j: eng.transpose(pb[:, j * 128:(j + 1) * 128], s, ident)),
                     reads=list(src_regs) + [ident_r], writes=[bankreg[b]], sig=(j == n - 1))
            src = pb[:, 0:n * 128].rearrange("p (a b) -> p a b", a=n)
            if copy_eng == "act":
                P.op("act", lambda eng: eng.copy(dst_aps, src), reads=[bankreg[b]], writes=[dst_reg])
            else:
                P.op("dve", lambda eng: eng.tensor_copy(dst_aps, src), reads=[bankreg[b]], writes=[dst_reg])

        def xattn_kv():
            m = SB.mark()
            memb = SB.take([NKC, 256], BF16)
            memb_r = Reg("memb")
            P.dma("pool", memb, wview(memT), writes=[memb_r])
            wr = [SB.take([NKC, 128], BF16) for _ in range(2)]
            wr_r = [Reg("wkvr0"), Reg("wkvr1")]
            P.op("dve", lambda eng: eng.memset(vaug, 1.0), writes=[vaug_r])
            for j in range(8):
                s = j % 2
                P.dma("pool", wr[s], xa_wkv_t[j].rearrange("p (k n) -> p k n", k=NKC), writes=[wr_r[s]])
                if j < 4:
                    b = nextbank()
                    mm_group(psf(b)[:, 0:256], [(wr[s][:, kc, :], memb[:, kc, :]) for kc in range(NKC)],
                             [wr_r[s], memb_r], b)
                    P.op("act", lambda eng, b=b, j=j: eng.copy(kT[:, j, :], psf(b)[:, 0:256]),
                         reads=[bankreg[b]], writes=[kT_r])
                else:
                    hd = j - 4
                    for mt in range(2):
                        b = nextbank()
                        mm_group(psf(b)[:, 0:128],
                                 [(memb[:, kc, mt * 128:(mt + 1) * 128], wr[s][:, kc, :]) for kc in range(NKC)],
                                 [wr_r[s], memb_r], b)
                        P.op("act", lambda eng, b=b, mt=mt, hd=hd: eng.copy(vaug[:, mt, hd, 0:128], psf(b)[:, 0:128]),
                             reads=[bankreg[b]], writes=[vaug_r])
            P.barrier()
            SB.reset(m)

        def layer_norm(H, Hr, HT, HT_r, GB, GB_r, HB, HB_r, stats, stats_r, do_T=True):
            for tt in range(4):
                h = H[:, tt, :]
                st = stats[:, tt, :]
                for c in range(8):
                    P.op("dve", lambda eng, c=c, h=h, st=st: eng.bn_stats(st[:, c * 6:(c + 1) * 6], h[:, c * 512:(c + 1) * 512]),
                         reads=[Hr[tt]], writes=[stats_r[tt]])
                P.op("dve", lambda eng, st=st: eng.bn_aggr(st[:, 48:50], st[:, 0:48]), reads=[stats_r[tt]], writes=[stats_r[tt]])
            for tt in range(4):
                st = stats[:, tt, :]
                P.op("act", lambda eng, st=st: eng.activation(st[:, 50:51], st[:, 49:50], AF.Sqrt, bias=EPS_AP[0], scale=1.0),
                     reads=[stats_r[tt], small_r], writes=[stats_r[tt]])
                P.op("dve", lambda eng, st=st: eng.reciprocal(st[:, 50:51], st[:, 50:51]), reads=[stats_r[tt]], writes=[stats_r[tt]])
                P.op("dve", lambda eng, st=st: eng.scalar_tensor_tensor(
                    out=st[:, 51:52], in0=st[:, 48:49], scalar=-1.0, in1=st[:, 50:51], op0=ALU.mult, op1=ALU.mult),
                    reads=[stats_r[tt]], writes=[stats_r[tt]])
            for tt in range(4):
                h = H[:, tt, :]
                st = stats[:, tt, :]
                P.op("act", lambda eng, h=h, st=st: eng.activation(h, h, AF.Identity, bias=st[:, 51:52], scale=st[:, 50:51]),
                     reads=[stats_r[tt], Hr[tt]], writes=[Hr[tt]])
            for tt in range(4):
                h = H[:, tt, :]
                P.op("dve", lambda eng, h=h: eng.tensor_tensor(h, h, GB[:, 0, :], ALU.mult), reads=[Hr[tt], GB_r], writes=[Hr[tt]])
                P.op("dve", lambda eng, h=h: eng.tensor_tensor(h, h, GB[:, 1, :], ALU.add), reads=[Hr[tt], GB_r], writes=[Hr[tt]])
                if do_T:
                    P.op("act", lambda eng, h=h: eng.copy(HB, h), reads=[Hr[tt]], writes=[HB_r])
                    for q4 in range(4):
                        transpose_to(HT[:, q4 * 8:(q4 + 1) * 8, tt * 128:(tt + 1) * 128],
                                     [HB[:, (q4 * 8 + j) * 128:(q4 * 8 + j + 1) * 128] for j in range(8)],
                                     [HB_r], HT_r, "act" if q4 % 2 else "dve")

        EPS_AP = [None]

        def load_gb(GB, GB_r, i):
            P.dma("sp", GB[:, 0, :], ln_g[i:i + 1, :].partition_broadcast(128), writes=[GB_r])
            P.dma("sp", GB[:, 1, :], ln_b[i:i + 1, :].partition_broadcast(128), writes=[GB_r])

        def token_block(tb, mix_src):
            m0 = SB.mark()
            H = SB.take([4, D], F32)
            Hr = [Reg("H%d" % i) for i in range(4)]
            HT = SB.take([NKC, TB], BF16)
            HT_r = Reg("HT")
            mX = SB.mark()

            def take_ln():
                return (SB.take([2, D], F32), Reg("GB"), SB.take([D], BF16), Reg("HB"), SB.take([4, 64], F32), [Reg("stats%d" % i) for i in range(4)])

            GB, GB_r, HB, HB_r, stats, stats_r = take_ln()
            t0 = tb * TB
            P.dma("sp", HT, wview(mix_src)[:, :, t0:t0 + TB], reads=([mscr_r] if mscr_r is not None else []), writes=[HT_r])
            for tt in range(4):
                P.dma("sp", H[:, tt, :], x_own[t0 + tt * 128:t0 + (tt + 1) * 128, :], writes=[Hr[tt]])
            load_gb(GB, GB_r, 0)
            wr = [SB.take([NKC, 512], BF16) for _ in range(2)]
            wr_r = [Reg("wo0"), Reg("wo1")]
            for nt in range(8):
                s = nt % 2
                P.dma("pool", wr[s], w_out_t[nt].rearrange("p (k n) -> p k n", k=NKC), writes=[wr_r[s]])
                for tt in range(4):
                    b = nextbank()
                    mm_group(psf(b), [(HT[:, kc, tt * 128:(tt + 1) * 128], wr[s][:, kc, :]) for kc in range(NKC)],
                             [HT_r, wr_r[s]], b)
                    hs = H[:, tt, nt * 512:(nt + 1) * 512]
                    P.op("dve", lambda eng, hs=hs, b=b: eng.scalar_tensor_tensor(
                        out=hs, in0=hs, scalar=ALPHA, in1=psf(b), op0=ALU.mult, op1=ALU.add),
                        reads=[bankreg[b], Hr[tt]], writes=[Hr[tt]])
            layer_norm(H, Hr, HT, HT_r, GB, GB_r, HB, HB_r, stats, stats_r)
            P.barrier()
            SB.reset(mX)
            GB, GB_r, HB, HB_r, stats, stats_r = take_ln()
            load_gb(GB, GB_r, 1)
            qT = SB.take([4, TB], BF16)
            qT_r = Reg("qT")
            pT = [SB.take([2, TB], BF16) for _ in range(2)]
            pT_r = [Reg("pT0"), Reg("pT1")]
            otok = SB.take([4, 512], BF16)
            otok_r = Reg("otok")
            oT = SB.take([4, TB], BF16)
            oT_r = Reg("oT")
            rc = SB.take([8], F32)
            rc_r = Reg("rc")
            wq = [SB.take([NKC, 128], BF16) for _ in range(2)]
            wq_r = [Reg("wq0"), Reg("wq1")]
            wo = [SB.take([4, 512], BF16) for _ in range(2)]
            wo_r = [Reg("wo0"), Reg("wo1")]
            for hd in range(4):
                s = hd % 2
                P.dma("pool", wq[s], xa_wq_t[hd].rearrange("p (k n) -> p k n", k=NKC), writes=[wq_r[s]])
                b = nextbank()
                mm_group(psf(b), [(wq[s][:, kc, :], HT[:, kc, :]) for kc in range(NKC)], [wq_r[s], HT_r], b)
                P.op("act", lambda eng, b=b, hd=hd: eng.activation(qT[:, hd, :], psf(b), AF.Copy, scale=128.0 ** -0.5),
                     reads=[bankreg[b]], writes=[qT_r])
            for hd in range(4):
                s = hd % 2
                for mt in range(2):
                    b = nextbank()
                    mm_group(psf(b), [(kT[:, hd, mt * 128:(mt + 1) * 128], qT[:, hd, :])], [kT_r, qT_r], b)
                    P.op("act", lambda eng, b=b, mt=mt, s=s: eng.activation(pT[s][:, mt, :], psf(b), AF.Exp),
                         reads=[bankreg[b]], writes=[pT_r[s]])
                for tt in range(4):
                    b = nextbank()
                    mm_group(psf(b)[:, 0:130],
                             [(pT[s][:, mt, tt * 128:(tt + 1) * 128], vaug[:, mt, hd, :]) for mt in range(2)],
                             [pT_r[s], vaug_r], b)
                    rcs = rc[:, tt:tt + 1]
                    P.op("dve", lambda eng, b=b, rcs=rcs: eng.reciprocal(rcs, psf(b)[:, 128:129]),
                         reads=[bankreg[b]], writes=[rc_r])
                    P.op("dve", lambda eng, b=b, rcs=rcs, tt=tt, hd=hd: eng.tensor_scalar(
                        otok[:, tt, hd * 128:(hd + 1) * 128], psf(b)[:, 0:128], rcs, None, ALU.mult),
                        reads=[bankreg[b], rc_r], writes=[otok_r])
            for tt in range(4):
                transpose_to(oT[:, 0:4, tt * 128:(tt + 1) * 128],
                             [otok[:, tt, hd * 128:(hd + 1) * 128] for hd in range(4)], [otok_r], oT_r, "act")
            for nt in range(8):
                s = nt % 2
                P.dma("pool", wo[s], xa_wo.rearrange("(h p) n -> p h n", p=128)[:, :, nt * 512:(nt + 1) * 512],
                      writes=[wo_r[s]])
                for tt in range(4):
                    b = nextbank()
                    mm_group(psf(b), [(oT[:, hd, tt * 128:(tt + 1) * 128], wo[s][:, hd, :]) for hd in range(4)],
                             [oT_r, wo_r[s]], b)
                    hs = H[:, tt, nt * 512:(nt + 1) * 512]
                    P.op("dve", lambda eng, hs=hs, b=b: eng.scalar_tensor_tensor(
                        out=hs, in0=hs, scalar=ALPHA, in1=psf(b), op0=ALU.mult, op1=ALU.add),
                        reads=[bankreg[b], Hr[tt]], writes=[Hr[tt]])
            layer_norm(H, Hr, HT, HT_r, GB, GB_r, HB, HB_r, stats, stats_r)
            P.barrier()
            SB.reset(mX)
            for tt in range(4):
                P.op("act", lambda eng, tt=tt: eng.mul(H[:, tt, :], H[:, tt, :], ALPHA), reads=[Hr[tt]], writes=[Hr[tt]])
            G = 4
            NW1, NW2 = 4, 6
            w1 = [SB.take([NKC, 128], BF16) for _ in range(NW1)]
            w1_r = [Reg("w1_%d" % i) for i in range(NW1)]
            w2 = [SB.take([D], BF16) for _ in range(NW2)]
            w2_r = [Reg("w2_%d" % i) for i in range(NW2)]
            hid = [SB.take([TB], BF16) for _ in range(NW2)]
            hid_r = [Reg("hid%d" % i) for i in range(NW2)]
            rl = [SB.take([TB], F32) for _ in range(2)]
            rl_r = [Reg("rl0"), Reg("rl1")]
            nfc = DFF // 128

            def ld_w1(g):
                for c in range(G):
                    fc = g * G + c
                    P.dma("pool", w1[fc % NW1], w_ff1_t[fc].rearrange("p (k n) -> p k n", k=NKC), writes=[w1_r[fc % NW1]])

            def ld_w2(g):
                for c in range(G):
                    fc = g * G + c
                    P.dma("pool", w2[fc % NW2], w_ff2[fc * 128:(fc + 1) * 128, :], writes=[w2_r[fc % NW2]])

            ld_w1(0)
            for g in range(nfc // G):
                ld_w2(g)
                for c in range(G):
                    fc = g * G + c
                    s1 = fc % NW1
                    s2 = fc % NW2
                    b = nextbank(0, 2)
                    mm_group(psf(b), [(w1[s1][:, kc, :], HT[:, kc, :]) for kc in range(NKC)], [w1_r[s1], HT_r], b)
                    k = fc % 2
                    P.op("act", lambda eng, b=b, k=k: eng.activation(rl[k], psf(b), AF.Relu), reads=[bankreg[b]], writes=[rl_r[k]])
                    P.op("act", lambda eng, k=k, s2=s2: eng.activation(hid[s2], rl[k], AF.Square), reads=[rl_r[k]], writes=[hid_r[s2]])
                if g + 1 < nfc // G:
                    ld_w1(g + 1)
                for tt in range(4):
                    for nt in range(8):
                        b = nextbank(2, 8)
                        prs = []
                        rds = []
                        for c in range(G):
                            s2 = (g * G + c) % NW2
                            prs.append((hid[s2][:, tt * 128:(tt + 1) * 128], w2[s2][:, nt * 512:(nt + 1) * 512]))
                            rds += [hid_r[s2], w2_r[s2]]
                        mm_group(psf(b), prs, rds, b)
                        hs = H[:, tt, nt * 512:(nt + 1) * 512]
                        P.op("dve", lambda eng, hs=hs, b=b: eng.tensor_tensor(hs, hs, psf(b), ALU.add),
                             reads=[bankreg[b], Hr[tt]], writes=[Hr[tt]])
            P.barrier()
            SB.reset(mX)
            GB, GB_r, HB, HB_r, stats, stats_r = take_ln()
            load_gb(GB, GB_r, 2)
            layer_norm(H, Hr, HT, HT_r, GB, GB_r, HB, HB_r, stats, stats_r, do_T=False)
            for tt in range(4):
                P.dma("sp", y[t0 + tt * 128:t0 + (tt + 1) * 128, :], H[:, tt, :], reads=[Hr[tt]])
            P.barrier()
            SB.reset(m0)

        def phase_a():
            mA = SB.mark()
            SL = SB.take([96], F32)
            XT = SB.take([NKC, 1024], BF16)
            XT_r = Reg("XT")
            WR = [SB.take([NKC, 256], BF16) for _ in range(2)]
            WR_r = [Reg("WR0"), Reg("WR1")]
            wslot = [0]

            def load_piece(col0, ncols=256):
                s = wslot[0] % 2
                wslot[0] += 1
                if ncols == 48:
                    P.dma("pool", WR[s][:, :, 0:48], w_gates, writes=[WR_r[s]])
                    return s
                for jj in range(ncols // 128):
                    P.dma("pool", WR[s][:, :, jj * 128:(jj + 1) * 128],
                          w_in_t[col0 // 128 + jj].rearrange("p (k n) -> p k n", k=NKC), writes=[WR_r[s]])
                return s

            ST = SB.take([8, 2, 256], F32)
            ST_r = [Reg("ST%d" % i) for i in range(8)]
            KSp = SB.take([4, 1024], BF16)
            VSp = SB.take([8, 4, 130], BF16)
            KWp = SB.take([4, 512], BF16)
            VWp = SB.take([4, 4, 130], BF16)
            KSp_r, VSp_r, KWp_r, VWp_r = Reg("KSp"), Reg("VSp"), Reg("KWp"), Reg("VWp")
            KC = SB.take([4, 2, 64], BF16)
            KC_r = Reg("KC")
            VC = SB.take([4, 2, 162], BF16, parts=64)
            VC_r = Reg("VC")
            ZT = SB.take([2, 4, 16], BF16)
            ZT_r = Reg("ZT")
            tabs_r = Reg("tabs")

            def ltab(name, dims, dt, parts=128):
                t = SB.take(dims, dt, parts=parts)
                P.dma("pool" if dt == BF16 else "sp", t, dram[name], writes=[tabs_r])
                return t

            P.dma("sp", VC, dram["vc_init"], writes=[VC_r])
            P.op("dve", lambda eng: eng.memset(VSp, 1.0), writes=[VSp_r])
            P.op("dve", lambda eng: eng.memset(VWp, 1.0), writes=[VWp_r])
            P.dma("pool", XT, wview(dram["xT_prev"]), writes=[XT_r])
            rs1 = ltab("ret_rs1", [8, 8], F32)
            mP1 = SB.mark()
            ktok = SB.take([8, 256], BF16)
            vtok = SB.take([8, 256], BF16)
            ktok_r, vtok_r = Reg("ktok"), Reg("vtok")
            ecnt = [0]

            def evac(dst, src, b, dst_reg, scale=None, func=None, extra_reads=()):
                ecnt[0] += 1
                if func is not None or scale is not None or ecnt[0] % 2 == 0:
                    f = func if func is not None else AF.Copy
                    if scale is None:
                        P.op("act", lambda eng: eng.activation(dst, src, f), reads=[bankreg[b]] + list(extra_reads), writes=[dst_reg])
                    else:
                        P.op("act", lambda eng: eng.activation(dst, src, f, scale=scale), reads=[bankreg[b]] + list(extra_reads), writes=[dst_reg])
                else:
                    P.op("dve", lambda eng: eng.tensor_copy(dst, src), reads=[bankreg[b]] + list(extra_reads), writes=[dst_reg])

            def proj_tm(s, col0, ncols, tile, xoff=0):
                b = nextbank()
                mm_group(psf(b)[:, 0:ncols],
                         [(XT[:, kc, tile * 128:(tile + 1) * 128], WR[s][:, kc, col0:col0 + ncols]) for kc in range(NKC)],
                         [XT_r, WR_r[s]], b)
                return b

            def proj_fm(s, col0, th):
                b = nextbank()
                mm_group(psf(b), [(WR[s][:, kc, col0:col0 + 128], XT[:, kc, th * 512:(th + 1) * 512]) for kc in range(NKC)],
                         [XT_r, WR_r[s]], b)
                return b

            for h in range(8):
                sk = load_piece(2048 + h * 256)
                sv = load_piece(4096 + h * 256)
                for n in range(8):
                    b = proj_tm(sk, 0, 256, n)
                    evac(ktok[:, n, :], psf(b)[:, 0:256], b, ktok_r, scale=rs1[:, h, n:n + 1], extra_reads=[tabs_r])
                for n in range(8):
                    b = proj_tm(sv, 0, 256, n)
                    evac(vtok[:, n, :], psf(b)[:, 0:256], b, vtok_r)
                for dh in range(2):
                    b = nextbank()
                    mm_group(psf(b)[:, 0:256], [(ktok[:, n, dh * 128:(dh + 1) * 128], vtok[:, n, :]) for n in range(8)],
                             [ktok_r, vtok_r], b)
                    evac(ST[:, h, dh, :], psf(b)[:, 0:256], b, ST_r[h])
            P.barrier()
            SB.reset(mP1)
            W1 = SB.take([2, 32, 128], BF16)
            W2 = SB.take([2, 128], BF16)
            posT = SB.take([2, 32], BF16)
            cw_r = Reg("cw")
            P.dma("pool", W1, dram["cmp_w1"].rearrange("k (j p) n -> p k j n", p=128), writes=[cw_r])
            P.dma("pool", W2, dram["cmp_w2"].rearrange("k p n -> p k n"), writes=[cw_r])
            P.dma("pool", posT, dram["cmp_posT"].rearrange("k p j -> p k j"), writes=[cw_r])
            cbias = SB.take([2], F32)
            cbias_r = Reg("cbias")
            for kv in range(2):
                b = nextbank()
                mm_group(psf(b)[:, 0:1], [(W1[:, kv, j, :], posT[:, kv, j:j + 1]) for j in range(32)], [cw_r], b)
                evac(cbias[:, kv:kv + 1], psf(b)[:, 0:1], b, cbias_r)
            mZ = SB.mark()
            zb = SB.take([2, 4, 1040], BF16)
            zb_r = Reg("zb")
            gel = SB.take([6, 64], F32)
            gel_r = Reg("gel")
            hidc = SB.take([64], BF16)
            hidc_r = Reg("hidc")

            def compress(kv, g, tile, NB):
                b = nextbank()
                zz = zb[:, kv, g, :]
                mm_group(psf(b)[:, 0:NB], [(W1[:, kv, j, :], zz[:, j:j + 16 * (NB - 1) + 1:16]) for j in range(32)], [cw_r, zb_r], b)
                u = gel[:, 0, 0:NB]
                t1 = gel[:, 1, 0:NB]
                t2 = gel[:, 2, 0:NB]
                sg = gel[:, 3, 0:NB]
                P.op("act", lambda eng: eng.activation(u, psf(b)[:, 0:NB], AF.Identity, bias=cbias[:, kv:kv + 1], scale=1.0),
                     reads=[bankreg[b], cbias_r], writes=[gel_r])
                P.op("dve", lambda eng: eng.tensor_tensor(t1, u, u, ALU.mult), reads=[gel_r], writes=[gel_r])
                P.op("dve", lambda eng: eng.tensor_scalar(t2, t1, 0.044715, 1.0, ALU.mult, ALU.add), reads=[gel_r], writes=[gel_r])
                P.op("dve", lambda eng: eng.tensor_tensor(t1, t2, u, ALU.mult), reads=[gel_r], writes=[gel_r])
                P.op("act", lambda eng: eng.activation(sg, t1, AF.Sigmoid, scale=1.5957691216057308), reads=[gel_r], writes=[gel_r])
                P.op("dve", lambda eng: eng.tensor_tensor(hidc[:, 0:NB], u, sg, ALU.mult), reads=[gel_r], writes=[hidc_r])
                b2 = nextbank()
                if kv == 0:
                    mm_group(psf(b2)[:, 0:NB], [(W2[:, 0, :], hidc[:, 0:NB])], [cw_r, hidc_r], b2)
                    evac(KC[:, g, tile, 0:NB], psf(b2)[:, 0:NB], b2, KC_r)
                else:
                    mm_group(psf(b2)[0:NB, 0:128], [(hidc[:, 0:NB], W2[:, 1, :])], [cw_r, hidc_r], b2)
                    evac(VC[0:NB, g, tile, 0:128], psf(b2)[0:NB, 0:128], b2, VC_r)

            def nsa_kv_pass(is_prev):
                for j in range(6):
                    if (not is_prev) and j >= 2:
                        break
                    for gp in range(2):
                        s = load_piece(10240 + j * 512 + gp * 256)
                        for gi in range(2):
                            g = gp * 2 + gi
                            if j in (0, 1):
                                for th in range(2):
                                    b = proj_fm(s, gi * 128, th)
                                    off = (0 if is_prev else 16) + th * 512
                                    evac(zb[:, j, g, off:off + 512], psf(b), b, zb_r)
                            elif j == 2:
                                for th in range(2):
                                    b = proj_fm(s, gi * 128, th)
                                    evac(KSp[:, g, th * 512:(th + 1) * 512], psf(b), b, KSp_r)
                            elif j == 4:
                                b = proj_fm(s, gi * 128, 1)
                                evac(KWp[:, g, :], psf(b), b, KWp_r)
                        if j == 3:
                            for n in range(8):
                                b = proj_tm(s, 0, 256, n)
                                evac(VSp[:, n, gp * 2:gp * 2 + 2, 0:128], psf(b)[:, 0:256].rearrange("p (g d) -> p g d", g=2), b, VSp_r)
                        elif j == 5:
                            for n in range(4, 8):
                                b = proj_tm(s, 0, 256, n)
                                evac(VWp[:, n - 4, gp * 2:gp * 2 + 2, 0:128], psf(b)[:, 0:256].rearrange("p (g d) -> p g d", g=2), b, VWp_r)

            nsa_kv_pass(True)
            for kv in range(2):
                for g in range(4):
                    compress(kv, g, 0, 63)
            P.op("dve", lambda eng: eng.tensor_copy(ZT, zb[:, :, :, 1008:1024]), reads=[zb_r], writes=[ZT_r])
            P.barrier()
            P.dma("pool", XT, wview(dram["xT_own"]), writes=[XT_r])
            P.op("dve", lambda eng: eng.tensor_copy(zb[:, :, :, 0:16], ZT), reads=[ZT_r], writes=[zb_r])
            nsa_kv_pass(False)
            for kv in range(2):
                for g in range(4):
                    compress(kv, g, 1, 64)
            P.barrier()
            SB.reset(mP1)
            mscr_r = Reg("mixscr")
            decT = ltab("ret_decT", [8, 128], F32)
            gq = ltab("ret_gq", [8, 128], F32)
            rs2 = ltab("ret_rs2", [8], F32)
            mR = SB.mark()
            class _B:
                pass
            RB = []
            for i in range(2):
                B_ = _B()
                B_.qT = SB.take([2, 1024], BF16)
                B_.kTt = SB.take([2, 1024], BF16)
                B_.vtok = SB.take([8, 256], BF16)
                B_.gs = SB.take([8, 256], BF16)
                B_.qT_r, B_.kTt_r, B_.vtok_r, B_.gs_r = (Reg(n_ + str(i)) for n_ in ("qT", "kTt", "vtok", "gs"))
                RB.append(B_)
            qhT = SB.take([2, 1024], BF16)
            ktok = SB.take([8, 256], BF16)
            Sbf = SB.take([2, 256], BF16)
            PT = [SB.take([128], BF16) for _ in range(2)]
            yn = SB.take([256], F32)
            rout = SB.take([8, 256], BF16)
            mst = SB.take([2, 1024], BF16)
            qhT_r, ktok_r, Sbf_r = (Reg(n) for n in ("qhT", "ktok", "Sbf"))
            PT_r = [Reg("PT0"), Reg("PT1")]
            yn_r, rout_r, mst_r, SL_r = Reg("yn"), Reg("rout"), Reg("mst"), Reg("SL")
            GAM = [1.0 - 2.0 ** (-5.0 - h) for h in range(8)]

            def proj_gen(h, B_):
                sq = load_piece(h * 256)
                for dh in range(2):
                    for th in range(2):
                        b = proj_fm(sq, dh * 128, th)
                        evac(B_.qT[:, dh, th * 512:(th + 1) * 512], psf(b), b, B_.qT_r)
                        yield
                sk = load_piece(2048 + h * 256)
                for dh in range(2):
                    for th in range(2):
                        b = proj_fm(sk, dh * 128, th)
                        evac(B_.kTt[:, dh, th * 512:(th + 1) * 512], psf(b), b, B_.kTt_r)
                        yield
                sv = load_piece(4096 + h * 256)
                for n in range(8):
                    b = proj_tm(sv, 0, 256, n)
                    evac(B_.vtok[:, n, :], psf(b)[:, 0:256], b, B_.vtok_r)
                    yield
                sg = load_piece(6144 + h * 256)
                for n in range(8):
                    b = proj_tm(sg, 0, 256, n)
                    evac(B_.gs[:, n, :], psf(b)[:, 0:256], b, B_.gs_r, func=AF.Silu)
                    yield

            for _ in proj_gen(0, RB[0]):
                pass
            for h in range(8):
                B_ = RB[h % 2]
                qT, kTt, vtok, gs = B_.qT, B_.kTt, B_.vtok, B_.gs
                qT_r, kTt_r, vtok_r, gs_r = B_.qT_r, B_.kTt_r, B_.vtok_r, B_.gs_r
                nxt = proj_gen(h + 1, RB[(h + 1) % 2]) if h < 7 else iter(())

                def pull(k, nxt=nxt):
                    for _ in range(k):
                        next(nxt, None)

                for dh in range(2):
                    for n in range(8):
                        P.op("dve", lambda eng, dh=dh, n=n, h=h, qT=qT: eng.tensor_tensor(
                            qhT[:, dh, n * 128:(n + 1) * 128], qT[:, dh, n * 128:(n + 1) * 128], gq[:, h, :], ALU.mult),
                            reads=[qT_r, tabs_r], writes=[qhT_r])
                for n in range(8):
                    b = nextbank()
                    pb = psb(b)
                    for dh in range(2):
                        P.op("pe", lambda eng, dh=dh, n=n, pb=pb, kTt=kTt: eng.transpose(pb[:, dh * 128:(dh + 1) * 128], kTt[:, dh, n * 128:(n + 1) * 128], ident),
                             reads=[kTt_r, ident_r], writes=[bankreg[b]], sig=(dh == 1))
                    evac(ktok[:, n, :], pb[:, 0:256], b, ktok_r, scale=rs2[:, h:h + 1], extra_reads=[tabs_r])
                P.op("act", lambda eng, h=h: eng.copy(Sbf, ST[:, h, :, :]), reads=[ST_r[h]], writes=[Sbf_r])
                for n in range(8):
                    cs = slice(n * 128, (n + 1) * 128)
                    b1 = nextbank()
                    mm_group(psf(b1)[:, 0:128], [(kTt[:, dh, cs], qT[:, dh, cs]) for dh in range(2)], [kTt_r, qT_r], b1)
                    p = n % 2
                    P.op("dve", lambda eng, b1=b1, p=p, h=h: eng.tensor_tensor(PT[p], psf(b1)[:, 0:128], decT[:, h, :], ALU.mult),
                         reads=[bankreg[b1], tabs_r], writes=[PT_r[p]])
                    pull(1)
                    b2 = nextbank()
                    mm_group(psf(b2)[:, 0:256], [(PT[p], vtok[:, n, :])] + [(qhT[:, dh, cs], Sbf[:, dh, :]) for dh in range(2)],
                             [PT_r[p], vtok_r, qhT_r, Sbf_r], b2)
                    o = psf(b2)[:, 0:256]
                    pull(1)
                    P.op("dve", lambda eng, o=o: eng.bn_stats(SL[:, 0:6], o), reads=[bankreg[b2]], writes=[SL_r])
                    P.op("dve", lambda eng: eng.bn_aggr(SL[:, 8:10], SL[:, 0:6]), reads=[SL_r], writes=[SL_r])
                    P.op("act", lambda eng: eng.activation(SL[:, 10:11], SL[:, 9:10], AF.Sqrt, bias=EPS_AP[0], scale=1.0),
                         reads=[SL_r, small_r], writes=[SL_r])
                    P.op("dve", lambda eng: eng.reciprocal(SL[:, 10:11], SL[:, 10:11]), reads=[SL_r], writes=[SL_r])
                    P.op("dve", lambda eng: eng.scalar_tensor_tensor(out=SL[:, 11:12], in0=SL[:, 8:9], scalar=-1.0, in1=SL[:, 10:11],
                                                                   op0=ALU.mult, op1=ALU.mult), reads=[SL_r], writes=[SL_r])
                    P.op("act", lambda eng, o=o: eng.activation(yn, o, AF.Identity, bias=SL[:, 11:12], scale=SL[:, 10:11]),
                         reads=[SL_r, bankreg[b2]], writes=[yn_r])
                    P.op("dve", lambda eng, n=n, gs=gs: eng.tensor_tensor(rout[:, n, :], yn, gs[:, n, :], ALU.mult),
                         reads=[yn_r, gs_r], writes=[rout_r])
                    pull(1)
                    if n < 7:
                        for dh in range(2):
                            b3 = nextbank()
                            mm_group(psf(b3)[:, 0:256], [(ktok[:, n, dh * 128:(dh + 1) * 128], vtok[:, n, :])], [ktok_r, vtok_r], b3)
                            P.op("dve", lambda eng, b3=b3, dh=dh, h=h: eng.scalar_tensor_tensor(
                                out=ST[:, h, dh, :], in0=ST[:, h, dh, :], scalar=GAM[h] ** 128, in1=psf(b3)[:, 0:256],
                                op0=ALU.mult, op1=ALU.add), reads=[bankreg[b3], ST_r[h]], writes=[ST_r[h]])
                        P.op("act", lambda eng, h=h: eng.copy(Sbf, ST[:, h, :, :]), reads=[ST_r[h]], writes=[Sbf_r])
                for _ in nxt:
                    pass
                for n in range(8):
                    transpose_to(mst[:, 0:2, n * 128:(n + 1) * 128], [rout[:, n, e * 128:(e + 1) * 128] for e in range(2)],
                                 [rout_r], mst_r, "act" if n % 2 else "dve")
                P.dma("sp", mixT_scr.rearrange("(a p) t -> p a t", p=128)[:, 2 * h:2 * h + 2, :], mst, reads=[mst_r], writes=[mscr_r])
            P.barrier()
            SB.reset(mP1)
            cmask = ltab("cmp_mask", [2, 8, 128], BF16, parts=64)
            cbi = ltab("cmp_bias", [2, 16, 8], F32, parts=64)
            ikeep = ltab("imp_keep", [8, 32], F32)
            iadd = ltab("imp_add", [8, 32], F32)
            sbias = ltab("slc_bias", [16, 16, 8], F32)
            E2 = ltab("e2", [16, 128], BF16, parts=32)
            tri = ltab("tri", [2, 128], BF16)
            gsig = SB.take([8, 48], F32)
            gsig_r = Reg("gsig")
            sgt = load_piece(13312, ncols=48)
            for qt in range(8):
                b = proj_tm(sgt, 0, 48, qt)
                evac(gsig[:, qt, :], psf(b)[:, 0:48], b, gsig_r, func=AF.Sigmoid)
            qn = SB.take([4, 1024], BF16)
            KSo = SB.take([1024], BF16)
            VSo = SB.take([8, 130], BF16)
            KWo = SB.take([1024], BF16)
            VWo = SB.take([8, 130], BF16)
            qn_r, KSo_r, VSo_r, KWo_r, VWo_r = (Reg(n) for n in ("qn", "KSo", "VSo", "KWo", "VWo"))
            pc = SB.take([2, 128], BF16, parts=64)
            pc_r = Reg("pc")
            pS = [SB.take([4, 128], BF16) for _ in range(5)]
            pS_r = [[Reg("pS%d_%d" % (i, r)) for r in range(4)] for i in range(5)]
            R4 = SB.take([4, 128], BF16, parts=32)
            R4_r = Reg("R4")
            tri4 = ltab("tri4", [2, 4, 128], BF16)
            SCB = (0, 1, 2, 7)
            scb = [0]
            psi = [0]
            imp = SB.take([4, 32], F32)
            imp_r = Reg("imp")
            m8 = SB.take([16], F32)
            selb = SB.take([32], BF16)
            selb_r = Reg("selb")
            Rm = SB.take([128], BF16, parts=32)
            Rm_r = Reg("Rm")
            acc4 = SB.take([4, 128], F32)
            acc4_r = [Reg("acc%d" % i) for i in range(4)]
            ost = SB.take([4, 128], BF16)
            ost_r = Reg("ost")
            mst4 = SB.take([4, 1024], BF16)
            mst4_r = Reg("mst4")
            cf = SB.take([16], F32)
            cf_r = Reg("cf")
            P.op("dve", lambda eng: eng.memset(VSo, 1.0), writes=[VSo_r])
            P.op("dve", lambda eng: eng.memset(VWo, 1.0), writes=[VWo_r])
            SC = 128.0 ** -0.5

            def coef(b, col, gcol, k):
                P.op("dve", lambda eng: eng.tensor_scalar(cf[:, k:k + 1], psf(b)[:, col:col + 1], 1e-30, None, ALU.add),
                     reads=[bankreg[b]], writes=[cf_r])
                P.op("dve", lambda eng: eng.reciprocal(cf[:, k:k + 1], cf[:, k:k + 1]), reads=[cf_r], writes=[cf_r])
                if gcol is not None:
                    P.op("dve", lambda eng: eng.tensor_tensor(cf[:, k + 4:k + 5], cf[:, k:k + 1], gcol, ALU.mult),
                         reads=[cf_r, gsig_r], writes=[cf_r])

            for g in range(4):
                for rp in range(2):
                    s = load_piece(8192 + g * 512 + rp * 256)
                    for ri in range(2):
                        for th in range(2):
                            b = proj_fm(s, ri * 128, th)
                            evac(qn[:, rp * 2 + ri, th * 512:(th + 1) * 512], psf(b), b, qn_r, scale=SC)
                for pi, (Ko, Ko_r, Vo, Vo_r) in enumerate(((KSo, KSo_r, VSo, VSo_r), (KWo, KWo_r, VWo, VWo_r))):
                    s = wslot[0] % 2
                    wslot[0] += 1
                    for jj in range(2):
                        c0_ = 11264 + (2 * pi + jj) * 512 + g * 128
                        P.dma("pool", WR[s][:, :, jj * 128:(jj + 1) * 128],
                              w_in_t[c0_ // 128].rearrange("p (k n) -> p k n", k=NKC), writes=[WR_r[s]])
                    for th in range(2):
                        b = proj_fm(s, 0, th)
                        evac(Ko[:, th * 512:(th + 1) * 512], psf(b), b, Ko_r)
                    for n in range(8):
                        b = proj_tm(s, 128, 128, n)
                        evac(Vo[:, n, 0:128], psf(b)[:, 0:128], b, Vo_r)
                for qt in range(8):
                    qs = slice(qt * 128, (qt + 1) * 128)
                    qtc = 8 + qt
                    for r in range(4):
                        hd = g * 4 + r
                        b = nextbank(0, 2)
                        for tile, NB in ((0, 63), (1, 64)):
                            oc = psf(b)[0:NB, tile * 128:(tile + 1) * 128]
                            P.op("pe", lambda eng, oc=oc, l_=KC[:, g, tile, 0:NB], r_=qn[:, r, qs]: eng.matmul(oc, l_, r_, start=True, stop=False),
                                 reads=[KC_r, qn_r], writes=[bankreg[b]], sig=False)
                            P.op("pe", lambda eng, oc=oc, l_=ident[0:NB, 0:NB], r_=cmask[0:NB, tile, qt, :]: eng.matmul(oc, l_, r_, start=False, stop=True),
                                 reads=[ident_r, tabs_r], writes=[bankreg[b]], sig=True)
                            P.op("act", lambda eng, oc=oc, o_=pc[0:NB, tile, :], b_=cbi[0:NB, tile, hd, qt:qt + 1]: eng.activation(
                                o_, oc, AF.Exp, bias=b_, scale=1.0),
                                reads=[bankreg[b], tabs_r], writes=[pc_r])
                        b2 = 2
                        mm_group(psf(b2)[:, 0:162], [(pc[0:NB, tile, :], VC[0:NB, g, tile, :]) for tile, NB in ((0, 63), (1, 64))],
                                 [pc_r, VC_r], b2)
                        coef(b2, 128, gsig[:, qt, hd:hd + 1], r)
                        if r == 0:
                            P.op("dve", lambda eng, r=r: eng.tensor_scalar(imp[:, 0, :], psf(2)[:, 130:162], cf[:, r:r + 1], None, ALU.mult),
                                 reads=[bankreg[2], cf_r], writes=[imp_r])
                        else:
                            P.op("dve", lambda eng, r=r: eng.scalar_tensor_tensor(out=imp[:, 0, :], in0=psf(2)[:, 130:162], scalar=cf[:, r:r + 1],
                                                                               in1=imp[:, 0, :], op0=ALU.mult, op1=ALU.add),
                                 reads=[bankreg[2], cf_r, imp_r], writes=[imp_r])
                        P.op("dve", lambda eng, r=r: eng.tensor_scalar(acc4[:, r, :], psf(2)[:, 0:128], cf[:, r + 4:r + 5], None, ALU.mult),
                             reads=[bankreg[2], cf_r], writes=[acc4_r[r]])
                    P.op("dve", lambda eng, k_=ikeep[:, qt, :]: eng.tensor_tensor(imp[:, 1, :], imp[:, 0, :], k_, ALU.mult), reads=[imp_r, tabs_r], writes=[imp_r])
                    P.op("dve", lambda eng, k_=iadd[:, qt, :]: eng.tensor_tensor(imp[:, 1, :], imp[:, 1, :], k_, ALU.add), reads=[imp_r, tabs_r], writes=[imp_r])
                    P.op("dve", lambda eng: eng.max(m8[:, 0:8], imp[:, 1, :]), reads=[imp_r], writes=[imp_r])
                    P.op("dve", lambda eng: eng.match_replace(imp[:, 2, :], m8[:, 0:8], imp[:, 1, :], -3.0e38), reads=[imp_r], writes=[imp_r])
                    P.op("dve", lambda eng: eng.max(m8[:, 8:16], imp[:, 2, :]), reads=[imp_r], writes=[imp_r])
                    P.op("dve", lambda eng: eng.tensor_scalar(imp[:, 3, :], imp[:, 1, :], m8[:, 15:16], None, ALU.is_ge), reads=[imp_r], writes=[imp_r])
                    P.op("dve", lambda eng: eng.tensor_scalar(selb, imp[:, 3, :], -NEGM, NEGM, ALU.mult, ALU.add), reads=[imp_r], writes=[selb_r])
                    bt = 7
                    tasks = []
                    for br, klist in ((2, list(range(qtc - 4, qtc + 1))), (1, list(range(0, qtc + 1)))):
                        for i, kt in enumerate(klist):
                            tasks.append((br, i, kt, i == len(klist) - 1))

                    def emit_score(task):
                        br, i, kt, lastk = task
                        b = SCB[scb[0] % len(SCB)]
                        scb[0] += 1
                        if br == 1:
                            Kap, Kr = (KSp[:, g, kt * 128:(kt + 1) * 128], KSp_r) if kt < 8 else (KSo[:, (kt - 8) * 128:(kt - 7) * 128], KSo_r)
                            Vap, Vr = (VSp[:, kt, g, :], VSp_r) if kt < 8 else (VSo[:, kt - 8, :], VSo_r)
                        else:
                            Kap, Kr = (KWp[:, g, (kt - 4) * 128:(kt - 3) * 128], KWp_r) if kt < 8 else (KWo[:, (kt - 8) * 128:(kt - 7) * 128], KWo_r)
                            Vap, Vr = (VWp[:, kt - 4, g, :], VWp_r) if kt < 8 else (VWo[:, kt - 8, :], VWo_r)
                        extra = []
                        if br == 1:
                            extra.append((E2[:, kt, :], R4, [tabs_r, R4_r]))
                        if kt == qtc:
                            extra.append((ident, tri4[:, 0, :, :], [ident_r, tabs_r]))
                        if br == 2 and kt == qtc - 4:
                            extra.append((ident, tri4[:, 1, :, :], [ident_r, tabs_r]))
                        sc = psf(b).rearrange("p (r t) -> p r t", r=4)
                        P.op("pe", lambda eng, sc=sc, Kap=Kap, q_=qn[:, 0:4, qs], ne=len(extra): eng.matmul(sc, Kap, q_, start=True, stop=(ne == 0)),
                             reads=[Kr, qn_r], writes=[bankreg[b]], sig=(len(extra) == 0))
                        for ei, (l_, r_, rg_) in enumerate(extra):
                            last = ei == len(extra) - 1
                            P.op("pe", lambda eng, sc=sc, l_=l_, r_=r_, last=last: eng.matmul(sc, l_, r_, start=False, stop=last),
                                 reads=rg_, writes=[bankreg[b]], sig=last)
                        pi_ = psi[0] % len(pS)
                        psi[0] += 1
                        for r in range(4):
                            P.op("act", lambda eng, b=b, r=r, pi_=pi_, b_=sbias[:, g * 4 + r, kt, qt:qt + 1]: eng.activation(
                                pS[pi_][:, r, :], psf(b)[:, r * 128:(r + 1) * 128], AF.Exp, bias=b_, scale=1.0),
                                reads=[bankreg[b], tabs_r], writes=[pS_r[pi_][r]])
                        return (br, i, lastk, pi_, Vap, Vr)

                    def emit_pv(info):
                        br, i, lastk, pi_, Vap, Vr = info
                        for r in range(4):
                            bo = 3 + r
                            hd = g * 4 + r
                            P.op("pe", lambda eng, bo=bo, r=r, pi_=pi_, Vap=Vap, i=i, lastk=lastk: eng.matmul(psf(bo)[:, 0:130], pS[pi_][:, r, :], Vap, start=(i == 0), stop=lastk),
                                 reads=[pS_r[pi_][r], Vr], writes=[bankreg[bo]], sig=lastk)
                            if lastk:
                                coef(bo, 128, gsig[:, qt, br * 16 + hd:br * 16 + hd + 1], br)
                                P.op("dve", lambda eng, bo=bo, br=br, r=r: eng.scalar_tensor_tensor(
                                    out=acc4[:, r, :], in0=psf(bo)[:, 0:128], scalar=cf[:, br + 4:br + 5], in1=acc4[:, r, :], op0=ALU.mult, op1=ALU.add),
                                    reads=[bankreg[bo], cf_r, acc4_r[r]], writes=[acc4_r[r]])
                                if br == 1:
                                    P.op("act", lambda eng, r=r: eng.copy(ost[:, r, :], acc4[:, r, :]), reads=[acc4_r[r]], writes=[ost_r])

                    LAG = 3
                    pend = []
                    for task in tasks:
                        if task[0] == 1 and task[1] == 0:
                            P.op("pe", lambda eng: eng.transpose(psb(bt)[0:32, 0:128], selb, ident), reads=[selb_r, ident_r], writes=[bankreg[bt]])
                            for r in range(4):
                                if r % 2 == 0:
                                    P.op("act", lambda eng, r=r: eng.copy(R4[:, r, :], psb(bt)[0:32, 0:128]), reads=[bankreg[bt]], writes=[R4_r])
                                else:
                                    P.op("dve", lambda eng, r=r: eng.tensor_copy(R4[:, r, :], psb(bt)[0:32, 0:128]), reads=[bankreg[bt]], writes=[R4_r])
                        pend.append(emit_score(task))
                        if len(pend) > LAG:
                            emit_pv(pend.pop(0))
                    while pend:
                        emit_pv(pend.pop(0))
                    transpose_to(mst4[:, 0:4, qs], [ost[:, r, :] for r in range(4)], [ost_r], mst4_r, "dve")
                P.dma("sp", mixT_scr.rearrange("(a p) t -> p a t", p=128)[:, 16 + 4 * g:20 + 4 * g, :], mst4, reads=[mst4_r], writes=[mscr_r])
            P.barrier()
            SB.reset(mA)
            return mscr_r

        P.op("dve", lambda eng: eng.memset(small[:, 0:1], EPS), writes=[small_r])
        EPS_AP[0] = small[:, 0:1]
        mscr_r = None
        if cfg.get("phaseA", True):
            mscr_r = phase_a()
        mix_src = mix_in if not cfg.get("phaseA", True) else mixT_scr
        if not cfg.get("dbgA"):
            xattn_kv()
            for tb in range(cfg.get("ntb", 2)):
                token_block(tb, mix_src)
        P.final_wait("sp")
        P.replay()
    return nc


def _tables(hf):
    f64 = np.float64
    t = {}
    gam = 1.0 - 2.0 ** (-5.0 - np.arange(8, dtype=f64))
    p = np.arange(128, dtype=f64)
    n = np.arange(8, dtype=f64)
    t["ret_rs1"] = (gam[None, :, None] ** (1023.0 - (128.0 * n[None, None, :] + p[:, None, None])) / 16.0)
    dcs = p[None, :] - p[:, None]
    dec = np.where(dcs[:, None, :] >= 0, gam[None, :, None] ** np.maximum(dcs[:, None, :], 0.0), 0.0) / 16.0
    t["ret_decT"] = dec
    t["ret_gq"] = np.broadcast_to(gam[None, :, None] ** (p[None, None, :] + 1.0), (128, 8, 128))
    t["ret_rs2"] = gam[None, :] ** (127.0 - p[:, None]) / 16.0
    slopes = 2.0 ** (-8.0 * np.arange(1, 17, dtype=f64) / 16.0)
    vstart = 1024 * (1 - hf)
    qt = np.arange(8)
    bmid = 1024.0 + 128.0 * qt + 64.0
    pc_ = np.arange(64)
    cidx = np.stack([pc_, 63 + pc_], axis=1)
    cend = 16 * cidx + 31
    ctx_t = 1024 + 128 * qt[:, None] + np.arange(128)[None, :]
    valid = (cend[:, :, None, None] <= ctx_t[None, None]) & (16 * cidx[:, :, None, None] >= vstart)
    valid[63, 0] = False
    t["cmp_mask"] = np.where(valid, 0.0, NEGM)
    t["cmp_bias"] = slopes[None, None, :, None] * (cend[:, :, None, None] - bmid[None, None, None, :])
    s_ = np.arange(32)
    c0 = 16.0 * cidx[:, :, None]
    ov = np.clip(np.minimum(c0 + 32, 64.0 * s_ + 64) - np.maximum(c0, 64.0 * s_), 0, None) / 32.0
    vci = np.zeros((64, 4, 2, 162), f64)
    vci[:, :, :, 128:130] = 1.0
    vci[:, :, :, 130:162] = ov[:, None, :, :]
    t["vc_init"] = vci.astype(ml_dtypes.bfloat16)
    ctxp = 1024 + 128 * qt[None, :, None] + np.arange(128)[:, None, None]
    cur = ctxp // 64
    blk = np.arange(32)[None, None, :]
    blk0 = 16 * (1 - hf)
    forced = (blk == blk0) | (blk == cur) | (blk == cur - 1)
    dead = (blk > cur) | (blk < blk0)
    t["imp_keep"] = np.where(forced | dead, 0.0, 1.0)
    t["imp_add"] = np.where(forced, 1e9, np.where(dead, -1e9, 0.0))
    kt = np.arange(16)
    kpos = 128 * kt[None, None, :, None] + np.arange(128)[:, None, None, None]
    sb = slopes[None, :, None, None] * (kpos - bmid[None, None, None, :])
    t["slc_bias"] = np.where(kpos >= vstart, sb, -1e30)
    key = np.arange(128)
    t["e2"] = (np.arange(32)[:, None, None] == (2 * kt[None, :, None] + key[None, None, :] // 64)).astype(f64)
    tri = np.zeros((128, 2, 128), f64)
    tri[:, 0, :] = np.where(key[:, None] <= key[None, :], 0.0, NEGM)
    tri[:, 1, :] = np.where(key[:, None] > key[None, :], 0.0, NEGM)
    t["tri"] = tri
    t["tri4"] = np.broadcast_to(tri[:, :, None, :], (128, 2, 4, 128))
    return {k: (np.ascontiguousarray(v) if v.dtype == ml_dtypes.bfloat16 else np.ascontiguousarray(v, dtype=np.float32)) for k, v in t.items()}


def make_maps(inputs, cores, cfg):
    x = np.asarray(inputs["x"])
    maps = []
    tabs = [_tables(0), _tables(1)]
    ident = np.eye(128, dtype=np.float32)
    shared = {}
    if cfg.get("phaseA", True):
        wi = np.asarray(inputs["w_in"])[0]
        shared.update(w_in_t=np.ascontiguousarray(wi[:, :13312].reshape(NKC, 128, 104, 128).transpose(2, 1, 0, 3)).reshape(104, 128, NKC * 128),
                      w_gates=np.ascontiguousarray(wi[:, 13312:].reshape(NKC, 128, 48).transpose(1, 0, 2)), cmp_w1=np.asarray(inputs["cmp_w1"])[0], cmp_w2=np.asarray(inputs["cmp_w2"])[0],
                      cmp_posT=np.ascontiguousarray(np.asarray(inputs["cmp_pos"])[0].transpose(0, 2, 1)))
    if not cfg.get("dbgA"):
        def tl(w, n):
            k = w.shape[1] // n
            return np.ascontiguousarray(w.reshape(NKC, 128, k, n).transpose(2, 1, 0, 3)).reshape(k, 128, NKC * n)
        shared.update(w_out_t=tl(np.asarray(inputs["w_out"])[0], 512), xa_wq_t=tl(np.asarray(inputs["xa_wq"])[0], 128),
                      xa_wkv_t=tl(np.asarray(inputs["xa_wkv"])[0], 128),
                      xa_wo=np.asarray(inputs["xa_wo"])[0], w_ff1_t=tl(np.asarray(inputs["w_ff1"])[0], 128), w_ff2=np.asarray(inputs["w_ff2"])[0],
                      ln_g=np.asarray(inputs["ln_g"])[0], ln_b=np.asarray(inputs["ln_b"])[0])
    for c in cores:
        b, hf = c // 2, c % 2
        m = dict(shared)
        m["ident"] = ident
        own = x[b, hf * 1024:(hf + 1) * 1024]
        if cfg.get("phaseA", True):
            m["xT_own"] = np.ascontiguousarray(own.T)
            m["xT_prev"] = np.ascontiguousarray(x[b, 0:1024].T) if hf == 1 else np.zeros((D, 1024), np.float32)
            m.update(tabs[hf])
        if not cfg.get("dbgA"):
            m["x_own"] = np.ascontiguousarray(own)
            m["memT"] = np.ascontiguousarray(np.asarray(inputs["mem"])[b].T)
        maps.append(m)
    return maps


def kernel(**inputs):
    cfg = dict(phaseA=True)
    nc = build(cfg)
    cores = list(range(8))
    maps = make_maps(inputs, cores, cfg)
    res = run_bass_kernel_spmd(nc, maps, core_ids=cores)
    out = np.empty((4, 2048, D), np.float32)
    for c in cores:
        out[c // 2, (c % 2) * 1024:(c % 2 + 1) * 1024] = res.results[c]["y"]
    return out
```

```python
import contextlib
import math
import numpy as np
import ml_dtypes
import concourse.bass as bass
import concourse.mybir as mybir
from concourse.bass_utils import run_bass_kernel_spmd

F32, BF16 = mybir.dt.float32, mybir.dt.bfloat16
AF = mybir.ActivationFunctionType
ALU = mybir.AluOpType
AX = mybir.AxisListType

D = 4096
T_OWN = 1024
TB = 512
NKC = 32
ALPHA = 2.0 ** 0.25
EPS = 1e-5
IN_W = 13360
DFF = 16384
NEGM = -30000.0


class Reg:
    __slots__ = ("w", "r", "name")

    def __init__(self, name=""):
        self.w = None
        self.r = {}
        self.name = name


class Prog:
    ENG = ("pe", "act", "dve", "pool", "sp")

    def __init__(self, nc, stack, ndma=8):
        self.nc = nc
        self.q = {e: [] for e in self.ENG}
        self.sem = {}
        self.cnt = {}
        self.seen = {e: {} for e in self.ENG}
        self.fence = {e: {} for e in self.ENG}
        self.rr = {e: 0 for e in self.ENG}
        self.ndma = ndma
        for e in ("pe", "act", "dve", "pool"):
            self.sem[e] = stack.enter_context(nc.semaphore("s_" + e))
            self.cnt[e] = 0
        for e in ("sp", "pool", "act"):
            for i in range(ndma):
                k = ("d", e, i)
                self.sem[k] = stack.enter_context(nc.semaphore("d_%s_%d" % (e, i)))
                self.cnt[k] = 0

    def _waits(self, e, reads, writes):
        need = dict(self.fence[e])
        self.fence[e] = {}

        def add(k, v):
            if need.get(k, 0) < v:
                need[k] = v

        for r in reads:
            if r.w is not None:
                add(*r.w)
        for w in writes:
            if w.w is not None:
                add(*w.w)
            for k, v in w.r.items():
                add(k, v)
        out = []
        for k, v in need.items():
            if k == e and e == "pe":
                continue
            if self.seen[e].get(k, 0) >= v:
                continue
            self.seen[e][k] = v
            out.append((k, v))
        return out

    def _post(self, tok, reads, writes):
        for r in reads:
            if r.r.get(tok[0], 0) < tok[1]:
                r.r[tok[0]] = tok[1]
        for w in writes:
            w.w = tok
            w.r = {}

    def op(self, e, fn, reads=(), writes=(), sig=True):
        waits = self._waits(e, reads, writes)
        if sig:
            self.cnt[e] += 1
            tok = (e, self.cnt[e])
        else:
            tok = (e, self.cnt[e] + 1)
        sem = self.sem

        def run(eng):
            for k, v in waits:
                eng.wait_ge(sem[k], v)
            ins = fn(eng)
            if sig:
                ins.then_inc(sem[e], 1)

        self.q[e].append(run)
        self._post(tok, reads, writes)
        return tok

    def dma(self, e, out, in_, reads=(), writes=()):
        i = self.rr[e]
        self.rr[e] = (i + 1) % self.ndma
        key = ("d", e, i)
        waits = self._waits(e, reads, writes)
        prev = self.cnt[key]
        if prev > 0 and self.seen[e].get(key, 0) < prev:
            waits.append((key, prev))
            self.seen[e][key] = prev
        self.cnt[key] += 16
        tok = (key, self.cnt[key])
        sem = self.sem

        def run(eng):
            for k, v in waits:
                eng.wait_ge(sem[k], v)
            eng.dma_start(out=out, in_=in_).then_inc(sem[key], 16)

        self.q[e].append(run)
        self._post(tok, reads, writes)
        return tok

    def barrier(self):
        allt = {k: v for k, v in self.cnt.items() if v > 0}
        for e in self.ENG:
            for k, v in allt.items():
                if self.fence[e].get(k, 0) < v:
                    self.fence[e][k] = v

    def final_wait(self, e):
        self.barrier()
        waits = self._waits(e, (), ())
        sem = self.sem

        def run(eng):
            for k, v in waits:
                eng.wait_ge(sem[k], v)

        self.q[e].append(run)

    def replay(self):
        nc = self.nc
        q = self.q
        with nc.Block() as block:
            @block.tensor
            def _(eng):
                for f in q["pe"]:
                    f(eng)

            @block.scalar
            def _(eng):
                for f in q["act"]:
                    f(eng)

            @block.vector
            def _(eng):
                for f in q["dve"]:
                    f(eng)

            @block.gpsimd
            def _(eng):
                for f in q["pool"]:
                    f(eng)

            @block.sync
            def _(eng):
                for f in q["sp"]:
                    f(eng)


class SbAlloc:
    def __init__(self, nc, nbytes):
        self.words = nbytes // 4
        self.t = nc.alloc_sbuf_tensor("SB", [128, self.words], F32)
        self.off = 0

    def take(self, dims, dt, parts=128):
        n = int(np.prod(dims))
        sz = 2 if dt == BF16 else 4
        words = (n * sz + 3) // 4
        words = (words + 7) // 8 * 8
        assert self.off + words <= self.words, ("SBUF overflow", self.off, words, self.words)
        ap = self.t[0:parts, self.off:self.off + words]
        self.off += words
        if dt != F32:
            ap = ap.bitcast(dt)
        ap = ap[:, 0:n]
        if len(dims) == 2:
            ap = ap.rearrange("p (a b) -> p a b", a=dims[0])
        elif len(dims) == 3:
            ap = ap.rearrange("p (a b c) -> p a b c", a=dims[0], b=dims[1])
        elif len(dims) == 4:
            ap = ap.rearrange("p (a b c d) -> p a b c d", a=dims[0], b=dims[1], c=dims[2])
        return ap

    def mark(self):
        return self.off

    def reset(self, m):
        self.off = m


def build(cfg):
    nc = bass.Bass("TRN2", target_bir_lowering=False)
    dram = {}

    def din(name, shape, dt=F32):
        dram[name] = nc.dram_tensor(name, list(shape), dt, kind="ExternalInput").ap()
        return dram[name]

    if not cfg.get("dbgA"):
        x_own = din("x_own", [T_OWN, D])
        memT = din("memT", [D, 256])
        w_out_t = din("w_out_t", [8, 128, NKC * 512])
        xa_wq_t = din("xa_wq_t", [4, 128, NKC * 128])
        xa_wkv_t = din("xa_wkv_t", [8, 128, NKC * 128])
        xa_wo = din("xa_wo", [512, D])
        w_ff1_t = din("w_ff1_t", [DFF // 128, 128, NKC * 128])
        w_ff2 = din("w_ff2", [DFF, D])
        ln_g = din("ln_g", [3, D])
        ln_b = din("ln_b", [3, D])
    ident_d = din("ident", [128, 128])
    if cfg.get("phaseA", True):
        din("xT_prev", [D, 1024]); din("xT_own", [D, 1024])
        w_in_t = din("w_in_t", [104, 128, NKC * 128]); w_gates = din("w_gates", [128, NKC, 48])
        din("cmp_w1", [2, D, 128]); din("cmp_w2", [2, 128, 128]); din("cmp_posT", [2, 128, 32])
        din("vc_init", [64, 4, 2, 162], BF16)
        din("ret_rs1", [128, 8, 8]); din("ret_decT", [128, 8, 128]); din("ret_gq", [128, 8, 128]); din("ret_rs2", [128, 8])
        din("cmp_mask", [64, 2, 8, 128]); din("cmp_bias", [64, 2, 16, 8])
        din("imp_keep", [128, 8, 32]); din("imp_add", [128, 8, 32]); din("slc_bias", [128, 16, 16, 8])
        din("e2", [32, 16, 128]); din("tri", [128, 2, 128]); din("tri4", [128, 2, 4, 128])
    else:
        mix_in = din("mixT_in", [D, T_OWN], BF16)
    if not cfg.get("dbgA"):
        y = nc.dram_tensor("y", [T_OWN, D], F32, kind="ExternalOutput").ap()
    mixT_scr = nc.dram_tensor("mixT_scr", [D, T_OWN], BF16, kind=("ExternalOutput" if cfg.get("dbgA") else "Internal")).ap()

    stack = contextlib.ExitStack()
    with stack:
        P = Prog(nc, stack)
        SB = SbAlloc(nc, 206 * 1024)
        PS = nc.alloc_psum_tensor("PS", [128, 8, 512], F32)
        bankreg = [Reg("bank%d" % i) for i in range(8)]
        bank_rr = [0]

        def nextbank(lo=0, hi=8):
            b = lo + bank_rr[0] % (hi - lo)
            bank_rr[0] += 1
            return b

        def psf(b):
            return PS[:, b, :]

        def psb(b):
            return PS[:, b, :].bitcast(BF16)

        ident = SB.take([128], BF16)
        ident_r = Reg("ident")
        P.dma("pool", ident, ident_d, writes=[ident_r])
        kT = SB.take([4, 256], BF16)
        vaug = SB.take([2, 4, 130], BF16)
        kT_r, vaug_r = Reg("kT"), Reg("vaug")
        small = SB.take([64], F32)
        small_r = Reg("small")
        base_mark = SB.mark()

        def wview(ap, p=128):
            return ap.rearrange("(kc p) n -> p kc n", p=p)

        def mm_group(out_ap, pairs, reads, bank, extra_writes=()):
            n = len(pairs)
            for i, (l, r) in enumerate(pairs):
                P.op("pe", (lambda eng, l=l, r=r, i=i: eng.matmul(out_ap, l, r, start=(i == 0), stop=(i == n - 1))),
                     reads=reads, writes=[bankreg[bank]] + list(extra_writes), sig=(i == n - 1))

        def transpose_to(dst_aps, src_aps, src_regs, dst_reg, copy_eng):
            b = nextbank()
            n = len(src_aps)
            pb = psb(b)
            for j, s in enumerate(src_aps):
                P.op("pe", (lambda eng, s=s, j=j: eng.transpose(pb[:, j * 128:(j + 1) * 128], s, ident)),
                     reads=list(src_regs) + [ident_r], writes=[bankreg[b]], sig=(j == n - 1))
            src = pb[:, 0:n * 128].rearrange("p (a b) -> p a b", a=n)
            if copy_eng == "act":
                P.op("act", lambda eng: eng.copy(dst_aps, src), reads=[bankreg[b]], writes=[dst_reg])
            else:
                P.op("dve", lambda eng: eng.tensor_copy(dst_aps, src), reads=[bankreg[b]], writes=[dst_reg])

        def xattn_kv():
            m = SB.mark()
            memb = SB.take([NKC, 256], BF16)
            memb_r = Reg("memb")
            P.dma("pool", memb, wview(memT), writes=[memb_r])
            wr = [SB.take([NKC, 128], BF16) for _ in range(2)]
            wr_r = [Reg("wkvr0"), Reg("wkvr1")]
            P.op("dve", lambda eng: eng.memset(vaug, 1.0), writes=[vaug_r])
            for j in range(8):
                s = j % 2
                P.dma("pool", wr[s], xa_wkv_t[j].rearrange("p (k n) -> p k n", k=NKC), writes=[wr_r[s]])
                if j < 4:
                    b = nextbank()
                    mm_group(psf(b)[:, 0:256], [(wr[s][:, kc, :], memb[:, kc, :]) for kc in range(NKC)],
                             [wr_r[s], memb_r], b)
                    P.op("act", lambda eng, b=b, j=j: eng.copy(kT[:, j, :], psf(b)[:, 0:256]),
                         reads=[bankreg[b]], writes=[kT_r])
                else:
                    hd = j - 4
                    for mt in range(2):
                        b = nextbank()
                        mm_group(psf(b)[:, 0:128],
                                 [(memb[:, kc, mt * 128:(mt + 1) * 128], wr[s][:, kc, :]) for kc in range(NKC)],
                                 [wr_r[s], memb_r], b)
                        P.op("act", lambda eng, b=b, mt=mt, hd=hd: eng.copy(vaug[:, mt, hd, 0:128], psf(b)[:, 0:128]),
                             reads=[bankreg[b]], writes=[vaug_r])
            P.barrier()
            SB.reset(m)

        def layer_norm(H, Hr, HT, HT_r, GB, GB_r, HB, HB_r, stats, stats_r, do_T=True):
            for tt in range(4):
                h = H[:, tt, :]
                st = stats[:, tt, :]
                for c in range(8):
                    P.op("dve", lambda eng, c=c, h=h, st=st: eng.bn_stats(st[:, c * 6:(c + 1) * 6], h[:, c * 512:(c + 1) * 512]),
                         reads=[Hr[tt]], writes=[stats_r[tt]])
                P.op("dve", lambda eng, st=st: eng.bn_aggr(st[:, 48:50], st[:, 0:48]), reads=[stats_r[tt]], writes=[stats_r[tt]])
            for tt in range(4):
                st = stats[:, tt, :]
                P.op("act", lambda eng, st=st: eng.activation(st[:, 50:51], st[:, 49:50], AF.Sqrt, bias=EPS_AP[0], scale=1.0),
                     reads=[stats_r[tt], small_r], writes=[stats_r[tt]])
                P.op("dve", lambda eng, st=st: eng.reciprocal(st[:, 50:51], st[:, 50:51]), reads=[stats_r[tt]], writes=[stats_r[tt]])
                P.op("dve", lambda eng, st=st: eng.scalar_tensor_tensor(
                    out=st[:, 51:52], in0=st[:, 48:49], scalar=-1.0, in1=st[:, 50:51], op0=ALU.mult, op1=ALU.mult),
                    reads=[stats_r[tt]], writes=[stats_r[tt]])
            for tt in range(4):
                h = H[:, tt, :]
                st = stats[:, tt, :]
                P.op("act", lambda eng, h=h, st=st: eng.activation(h, h, AF.Identity, bias=st[:, 51:52], scale=st[:, 50:51]),
                     reads=[stats_r[tt], Hr[tt]], writes=[Hr[tt]])
            for tt in range(4):
                h = H[:, tt, :]
                P.op("dve", lambda eng, h=h: eng.tensor_tensor(h, h, GB[:, 0, :], ALU.mult), reads=[Hr[tt], GB_r], writes=[Hr[tt]])
                P.op("dve", lambda eng, h=h: eng.tensor_tensor(h, h, GB[:, 1, :], ALU.add), reads=[Hr[tt], GB_r], writes=[Hr[tt]])
                if do_T:
                    P.op("act", lambda eng, h=h: eng.copy(HB, h), reads=[Hr[tt]], writes=[HB_r])
                    for q4 in range(4):
                        transpose_to(HT[:, q4 * 8:(q4 + 1) * 8, tt * 128:(tt + 1) * 128],
                                     [HB[:, (q4 * 8 + j) * 128:(q4 * 8 + j + 1) * 128] for j in range(8)],
                                     [HB_r], HT_r, "act" if q4 % 2 else "dve")

        EPS_AP = [None]

        def load_gb(GB, GB_r, i):
            P.dma("sp", GB[:, 0, :], ln_g[i:i + 1, :].partition_broadcast(128), writes=[GB_r])
            P.dma("sp", GB[:, 1, :], ln_b[i:i + 1, :].partition_broadcast(128), writes=[GB_r])

        def token_block(tb, mix_src):
            m0 = SB.mark()
            H = SB.take([4, D], F32)
            Hr = [Reg("H%d" % i) for i in range(4)]
            HT = SB.take([NKC, TB], BF16)
            HT_r = Reg("HT")
            mX = SB.mark()

            def take_ln():
                return (SB.take([2, D], F32), Reg("GB"), SB.take([D], BF16), Reg("HB"), SB.take([4, 64], F32), [Reg("stats%d" % i) for i in range(4)])

            GB, GB_r, HB, HB_r, stats, stats_r = take_ln()
            t0 = tb * TB
            P.dma("sp", HT, wview(mix_src)[:, :, t0:t0 + TB], reads=([mscr_r] if mscr_r is not None else []), writes=[HT_r])
            for tt in range(4):
                P.dma("sp", H[:, tt, :], x_own[t0 + tt * 128:t0 + (tt + 1) * 128, :], writes=[Hr[tt]])
            load_gb(GB, GB_r, 0)
            wr = [SB.take([NKC, 512], BF16) for _ in range(2)]
            wr_r = [Reg("wo0"), Reg("wo1")]
            for nt in range(8):
                s = nt % 2
                P.dma("pool", wr[s], w_out_t[nt].rearrange("p (k n) -> p k n", k=NKC), writes=[wr_r[s]])
                for tt in range(4):
                    b = nextbank()
                    mm_group(psf(b), [(HT[:, kc, tt * 128:(tt + 1) * 128], wr[s][:, kc, :]) for kc in range(NKC)],
                             [HT_r, wr_r[s]], b)
                    hs = H[:, tt, nt * 512:(nt + 1) * 512]
                    P.op("dve", lambda eng, hs=hs, b=b: eng.scalar_tensor_tensor(
                        out=hs, in0=hs, scalar=ALPHA, in1=psf(b), op0=ALU.mult, op1=ALU.add),
                        reads=[bankreg[b], Hr[tt]], writes=[Hr[tt]])
            layer_norm(H, Hr, HT, HT_r, GB, GB_r, HB, HB_r, stats, stats_r)
            P.barrier()
            SB.reset(mX)
            GB, GB_r, HB, HB_r, stats, stats_r = take_ln()
            load_gb(GB, GB_r, 1)
            qT = SB.take([4, TB], BF16)
            qT_r = Reg("qT")
            pT = [SB.take([2, TB], BF16) for _ in range(2)]
            pT_r = [Reg("pT0"), Reg("pT1")]
            otok = SB.take([4, 512], BF16)
            otok_r = Reg("otok")
            oT = SB.take([4, TB], BF16)
            oT_r = Reg("oT")
            rc = SB.take([8], F32)
            rc_r = Reg("rc")
            wq = [SB.take([NKC, 128], BF16) for _ in range(2)]
            wq_r = [Reg("wq0"), Reg("wq1")]
            wo = [SB.take([4, 512], BF16) for _ in range(2)]
            wo_r = [Reg("wo0"), Reg("wo1")]
            for hd in range(4):
                s = hd % 2
                P.dma("pool", wq[s], xa_wq_t[hd].rearrange("p (k n) -> p k n", k=NKC), writes=[wq_r[s]])
                b = nextbank()
                mm_group(psf(b), [(wq[s][:, kc, :], HT[:, kc, :]) for kc in range(NKC)], [wq_r[s], HT_r], b)
                P.op("act", lambda eng, b=b, hd=hd: eng.activation(qT[:, hd, :], psf(b), AF.Copy, scale=128.0 ** -0.5),
                     reads=[bankreg[b]], writes=[qT_r])
            for hd in range(4):
                s = hd % 2
                for mt in range(2):
                    b = nextbank()
                    mm_group(psf(b), [(kT[:, hd, mt * 128:(mt + 1) * 128], qT[:, hd, :])], [kT_r, qT_r], b)
                    P.op("act", lambda eng, b=b, mt=mt, s=s: eng.activation(pT[s][:, mt, :], psf(b), AF.Exp),
                         reads=[bankreg[b]], writes=[pT_r[s]])
                for tt in range(4):
                    b = nextbank()
                    mm_group(psf(b)[:, 0:130],
                             [(pT[s][:, mt, tt * 128:(tt + 1) * 128], vaug[:, mt, hd, :]) for mt in range(2)],
                             [pT_r[s], vaug_r], b)
                    rcs = rc[:, tt:tt + 1]
                    P.op("dve", lambda eng, b=b, rcs=rcs: eng.reciprocal(rcs, psf(b)[:, 128:129]),
                         reads=[bankreg[b]], writes=[rc_r])
                    P.op("dve", lambda eng, b=b, rcs=rcs, tt=tt, hd=hd: eng.tensor_scalar(
                        otok[:, tt, hd * 128:(hd + 1) * 128], psf(b)[:, 0:128], rcs, None, ALU.mult),
                        reads=[bankreg[b], rc_r], writes=[otok_r])
            for tt in range(4):
                transpose_to(oT[:, 0:4, tt * 128:(tt + 1) * 128],
                             [otok[:, tt, hd * 128:(hd + 1) * 128] for hd in range(4)], [otok_r], oT_r, "act")
            for nt in range(8):
                s = nt % 2
                P.dma("pool", wo[s], xa_wo.rearrange("(h p) n -> p h n", p=128)[:, :, nt * 512:(nt + 1) * 512],
                      writes=[wo_r[s]])
                for tt in range(4):
                    b = nextbank()
                    mm_group(psf(b), [(oT[:, hd, tt * 128:(tt + 1) * 128], wo[s][:, hd, :]) for hd in range(4)],
                             [oT_r, wo_r[s]], b)
                    hs = H[:, tt, nt * 512:(nt + 1) * 512]
                    P.op("dve", lambda eng, hs=hs, b=b: eng.scalar_tensor_tensor(
                        out=hs, in0=hs, scalar=ALPHA, in1=psf(b), op0=ALU.mult, op1=ALU.add),
                        reads=[bankreg[b], Hr[tt]], writes=[Hr[tt]])
            layer_norm(H, Hr, HT, HT_r, GB, GB_r, HB, HB_r, stats, stats_r)
            P.barrier()
            SB.reset(mX)
            for tt in range(4):
                P.op("act", lambda eng, tt=tt: eng.mul(H[:, tt, :], H[:, tt, :], ALPHA), reads=[Hr[tt]], writes=[Hr[tt]])
            G = 4
            NW1, NW2 = 4, 6
            w1 = [SB.take([NKC, 128], BF16) for _ in range(NW1)]
            w1_r = [Reg("w1_%d" % i) for i in range(NW1)]
            w2 = [SB.take([D], BF16) for _ in range(NW2)]
            w2_r = [Reg("w2_%d" % i) for i in range(NW2)]
            hid = [SB.take([TB], BF16) for _ in range(NW2)]
            hid_r = [Reg("hid%d" % i) for i in range(NW2)]
            rl = [SB.take([TB], F32) for _ in range(2)]
            rl_r = [Reg("rl0"), Reg("rl1")]
            nfc = DFF // 128

            def ld_w1(g):
                for c in range(G):
                    fc = g * G + c
                    P.dma("pool", w1[fc % NW1], w_ff1_t[fc].rearrange("p (k n) -> p k n", k=NKC), writes=[w1_r[fc % NW1]])

            def ld_w2(g):
                for c in range(G):
                    fc = g * G + c
                    P.dma("pool", w2[fc % NW2], w_ff2[fc * 128:(fc + 1) * 128, :], writes=[w2_r[fc % NW2]])

            ld_w1(0)
            for g in range(nfc // G):
                ld_w2(g)
                for c in range(G):
                    fc = g * G + c
                    s1 = fc % NW1
                    s2 = fc % NW2
                    b = nextbank(0, 2)
                    mm_group(psf(b), [(w1[s1][:, kc, :], HT[:, kc, :]) for kc in range(NKC)], [w1_r[s1], HT_r], b)
                    k = fc % 2
                    P.op("act", lambda eng, b=b, k=k: eng.activation(rl[k], psf(b), AF.Relu), reads=[bankreg[b]], writes=[rl_r[k]])
                    P.op("act", lambda eng, k=k, s2=s2: eng.activation(hid[s2], rl[k], AF.Square), reads=[rl_r[k]], writes=[hid_r[s2]])
                if g + 1 < nfc // G:
                    ld_w1(g + 1)
                for tt in range(4):
                    for nt in range(8):
                        b = nextbank(2, 8)
                        prs = []
                        rds = []
                        for c in range(G):
                            s2 = (g * G + c) % NW2
                            prs.append((hid[s2][:, tt * 128:(tt + 1) * 128], w2[s2][:, nt * 512:(nt + 1) * 512]))
                            rds += [hid_r[s2], w2_r[s2]]
                        mm_group(psf(b), prs, rds, b)
                        hs = H[:, tt, nt * 512:(nt + 1) * 512]
                        P.op("dve", lambda eng, hs=hs, b=b: eng.tensor_tensor(hs, hs, psf(b), ALU.add),
                             reads=[bankreg[b], Hr[tt]], writes=[Hr[tt]])
            P.barrier()
            SB.reset(mX)
            GB, GB_r, HB, HB_r, stats, stats_r = take_ln()
            load_gb(GB, GB_r, 2)
            layer_norm(H, Hr, HT, HT_r, GB, GB_r, HB, HB_r, stats, stats_r, do_T=False)
            for tt in range(4):
                P.dma("sp", y[t0 + tt * 128:t0 + (tt + 1) * 128, :], H[:, tt, :], reads=[Hr[tt]])
            P.barrier()
            SB.reset(m0)

        def phase_a():
            mA = SB.mark()
            SL = SB.take([96], F32)
            XT = SB.take([NKC, 1024], BF16)
            XT_r = Reg("XT")
            WR = [SB.take([NKC, 256], BF16) for _ in range(2)]
            WR_r = [Reg("WR0"), Reg("WR1")]
            wslot = [0]

            def load_piece(col0, ncols=256):
                s = wslot[0] % 2
                wslot[0] += 1
                if ncols == 48:
                    P.dma("pool", WR[s][:, :, 0:48], w_gates, writes=[WR_r[s]])
                    return s
                for jj in range(ncols // 128):
                    P.dma("pool", WR[s][:, :, jj * 128:(jj + 1) * 128],
                          w_in_t[col0 // 128 + jj].rearrange("p (k n) -> p k n", k=NKC), writes=[WR_r[s]])
                return s

            ST = SB.take([8, 2, 256], F32)
            ST_r = [Reg("ST%d" % i) for i in range(8)]
            KSp = SB.take([4, 1024], BF16)
            VSp = SB.take([8, 4, 130], BF16)
            KWp = SB.take([4, 512], BF16)
            VWp = SB.take([4, 4, 130], BF16)
            KSp_r, VSp_r, KWp_r, VWp_r = Reg("KSp"), Reg("VSp"), Reg("KWp"), Reg("VWp")
            KC = SB.take([4, 2, 64], BF16)
            KC_r = Reg("KC")
            VC = SB.take([4, 2, 162], BF16, parts=64)
            VC_r = Reg("VC")
            ZT = SB.take([2, 4, 16], BF16)
            ZT_r = Reg("ZT")
            tabs_r = Reg("tabs")

            def ltab(name, dims, dt, parts=128):
                t = SB.take(dims, dt, parts=parts)
                P.dma("pool" if dt == BF16 else "sp", t, dram[name], writes=[tabs_r])
                return t

            P.dma("sp", VC, dram["vc_init"], writes=[VC_r])
            P.op("dve", lambda eng: eng.memset(VSp, 1.0), writes=[VSp_r])
            P.op("dve", lambda eng: eng.memset(VWp, 1.0), writes=[VWp_r])
            P.dma("pool", XT, wview(dram["xT_prev"]), writes=[XT_r])
            rs1 = ltab("ret_rs1", [8, 8], F32)
            mP1 = SB.mark()
            ktok = SB.take([8, 256], BF16)
            vtok = SB.take([8, 256], BF16)
            ktok_r, vtok_r = Reg("ktok"), Reg("vtok")
            ecnt = [0]

            def evac(dst, src, b, dst_reg, scale=None, func=None, extra_reads=()):
                ecnt[0] += 1
                if func is not None or scale is not None or ecnt[0] % 2 == 0:
                    f = func if func is not None else AF.Copy
                    if scale is None:
                        P.op("act", lambda eng: eng.activation(dst, src, f), reads=[bankreg[b]] + list(extra_reads), writes=[dst_reg])
                    else:
                        P.op("act", lambda eng: eng.activation(dst, src, f, scale=scale), reads=[bankreg[b]] + list(extra_reads), writes=[dst_reg])
                else:
                    P.op("dve", lambda eng: eng.tensor_copy(dst, src), reads=[bankreg[b]] + list(extra_reads), writes=[dst_reg])

            def proj_tm(s, col0, ncols, tile, xoff=0):
                b = nextbank()
                mm_group(psf(b)[:, 0:ncols],
                         [(XT[:, kc, tile * 128:(tile + 1) * 128], WR[s][:, kc, col0:col0 + ncols]) for kc in range(NKC)],
                         [XT_r, WR_r[s]], b)
                return b

            def proj_fm(s, col0, th):
                b = nextbank()
                mm_group(psf(b), [(WR[s][:, kc, col0:col0 + 128], XT[:, kc, th * 512:(th + 1) * 512]) for kc in range(NKC)],
                         [XT_r, WR_r[s]], b)
                return b

            for h in range(8):
                sk = load_piece(2048 + h * 256)
                sv = load_piece(4096 + h * 256)
                for n in range(8):
                    b = proj_tm(sk, 0, 256, n)
                    evac(ktok[:, n, :], psf(b)[:, 0:256], b, ktok_r, scale=rs1[:, h, n:n + 1], extra_reads=[tabs_r])
                for n in range(8):
                    b = proj_tm(sv, 0, 256, n)
                    evac(vtok[:, n, :], psf(b)[:, 0:256], b, vtok_r)
                for dh in range(2):
                    b = nextbank()
                    mm_group(psf(b)[:, 0:256], [(ktok[:, n, dh * 128:(dh + 1) * 128], vtok[:, n, :]) for n in range(8)],
                             [ktok_r, vtok_r], b)
                    evac(ST[:, h, dh, :], psf(b)[:, 0:256], b, ST_r[h])
            P.barrier()
            SB.reset(mP1)
            W1 = SB.take([2, 32, 128], BF16)
            W2 = SB.take([2, 128], BF16)
            posT = SB.take([2, 32], BF16)
            cw_r = Reg("cw")
            P.dma("pool", W1, dram["cmp_w1"].rearrange("k (j p) n -> p k j n", p=128), writes=[cw_r])
            P.dma("pool", W2, dram["cmp_w2"].rearrange("k p n -> p k n"), writes=[cw_r])
            P.dma("pool", posT, dram["cmp_posT"].rearrange("k p j -> p k j"), writes=[cw_r])
            cbias = SB.take([2], F32)
            cbias_r = Reg("cbias")
            for kv in range(2):
                b = nextbank()
                mm_group(psf(b)[:, 0:1], [(W1[:, kv, j, :], posT[:, kv, j:j + 1]) for j in range(32)], [cw_r], b)
                evac(cbias[:, kv:kv + 1], psf(b)[:, 0:1], b, cbias_r)
            mZ = SB.mark()
            zb = SB.take([2, 4, 1040], BF16)
            zb_r = Reg("zb")
            gel = SB.take([6, 64], F32)
            gel_r = Reg("gel")
            hidc = SB.take([64], BF16)
            hidc_r = Reg("hidc")

            def compress(kv, g, tile, NB):
                b = nextbank()
                zz = zb[:, kv, g, :]
                mm_group(psf(b)[:, 0:NB], [(W1[:, kv, j, :], zz[:, j:j + 16 * (NB - 1) + 1:16]) for j in range(32)], [cw_r, zb_r], b)
                u = gel[:, 0, 0:NB]
                t1 = gel[:, 1, 0:NB]
                t2 = gel[:, 2, 0:NB]
                sg = gel[:, 3, 0:NB]
                P.op("act", lambda eng: eng.activation(u, psf(b)[:, 0:NB], AF.Identity, bias=cbias[:, kv:kv + 1], scale=1.0),
                     reads=[bankreg[b], cbias_r], writes=[gel_r])
                P.op("dve", lambda eng: eng.tensor_tensor(t1, u, u, ALU.mult), reads=[gel_r], writes=[gel_r])
                P.op("dve", lambda eng: eng.tensor_scalar(t2, t1, 0.044715, 1.0, ALU.mult, ALU.add), reads=[gel_r], writes=[gel_r])
                P.op("dve", lambda eng: eng.tensor_tensor(t1, t2, u, ALU.mult), reads=[gel_r], writes=[gel_r])
                P.op("act", lambda eng: eng.activation(sg, t1, AF.Sigmoid, scale=1.5957691216057308), reads=[gel_r], writes=[gel_r])
                P.op("dve", lambda eng: eng.tensor_tensor(hidc[:, 0:NB], u, sg, ALU.mult), reads=[gel_r], writes=[hidc_r])
                b2 = nextbank()
                if kv == 0:
                    mm_group(psf(b2)[:, 0:NB], [(W2[:, 0, :], hidc[:, 0:NB])], [cw_r, hidc_r], b2)
                    evac(KC[:, g, tile, 0:NB], psf(b2)[:, 0:NB], b2, KC_r)
                else:
                    mm_group(psf(b2)[0:NB, 0:128], [(hidc[:, 0:NB], W2[:, 1, :])], [cw_r, hidc_r], b2)
                    evac(VC[0:NB, g, tile, 0:128], psf(b2)[0:NB, 0:128], b2, VC_r)

            def nsa_kv_pass(is_prev):
                for j in range(6):
                    if (not is_prev) and j >= 2:
                        break
                    for gp in range(2):
                        s = load_piece(10240 + j * 512 + gp * 256)
                        for gi in range(2):
                            g = gp * 2 + gi
                            if j in (0, 1):
                                for th in range(2):
                                    b = proj_fm(s, gi * 128, th)
                                    off = (0 if is_prev else 16) + th * 512
                                    evac(zb[:, j, g, off:off + 512], psf(b), b, zb_r)
                            elif j == 2:
                                for th in range(2):
                                    b = proj_fm(s, gi * 128, th)
                                    evac(KSp[:, g, th * 512:(th + 1) * 512], psf(b), b, KSp_r)
                            elif j == 4:
                                b = proj_fm(s, gi * 128, 1)
                                evac(KWp[:, g, :], psf(b), b, KWp_r)
                        if j == 3:
                            for n in range(8):
                                b = proj_tm(s, 0, 256, n)
                                evac(VSp[:, n, gp * 2:gp * 2 + 2, 0:128], psf(b)[:, 0:256].rearrange("p (g d) -> p g d", g=2), b, VSp_r)
                        elif j == 5:
                            for n in range(4, 8):
                                b = proj_tm(s, 0, 256, n)
                                evac(VWp[:, n - 4, gp * 2:gp * 2 + 2, 0:128], psf(b)[:, 0:256].rearrange("p (g d) -> p g d", g=2), b, VWp_r)

            nsa_kv_pass(True)
            for kv in range(2):
                for g in range(4):
                    compress(kv, g, 0, 63)
            P.op("dve", lambda eng: eng.tensor_copy(ZT, zb[:, :, :, 1008:1024]), reads=[zb_r], writes=[ZT_r])
            P.barrier()
            P.dma("pool", XT, wview(dram["xT_own"]), writes=[XT_r])
            P.op("dve", lambda eng: eng.tensor_copy(zb[:, :, :, 0:16], ZT), reads=[ZT_r], writes=[zb_r])
            nsa_kv_pass(False)
            for kv in range(2):
                for g in range(4):
                    compress(kv, g, 1, 64)
            P.barrier()
            SB.reset(mP1)
            mscr_r = Reg("mixscr")
            decT = ltab("ret_decT", [8, 128], F32)
            gq = ltab("ret_gq", [8, 128], F32)
            rs2 = ltab("ret_rs2", [8], F32)
            mR = SB.mark()
            class _B:
                pass
            RB = []
            for i in range(2):
                B_ = _B()
                B_.qT = SB.take([2, 1024], BF16)
                B_.kTt = SB.take([2, 1024], BF16)
                B_.vtok = SB.take([8, 256], BF16)
                B_.gs = SB.take([8, 256], BF16)
                B_.qT_r, B_.kTt_r, B_.vtok_r, B_.gs_r = (Reg(n_ + str(i)) for n_ in ("qT", "kTt", "vtok", "gs"))
                RB.append(B_)
            qhT = SB.take([2, 1024], BF16)
            ktok = SB.take([8, 256], BF16)
            Sbf = SB.take([2, 256], BF16)
            PT = [SB.take([128], BF16) for _ in range(2)]
            yn = SB.take([256], F32)
            rout = SB.take([8, 256], BF16)
            mst = SB.take([2, 1024], BF16)
            qhT_r, ktok_r, Sbf_r = (Reg(n) for n in ("qhT", "ktok", "Sbf"))
            PT_r = [Reg("PT0"), Reg("PT1")]
            yn_r, rout_r, mst_r, SL_r = Reg("yn"), Reg("rout"), Reg("mst"), Reg("SL")
            GAM = [1.0 - 2.0 ** (-5.0 - h) for h in range(8)]

            def proj_gen(h, B_):
                sq = load_piece(h * 256)
                for dh in range(2):
                    for th in range(2):
                        b = proj_fm(sq, dh * 128, th)
                        evac(B_.qT[:, dh, th * 512:(th + 1) * 512], psf(b), b, B_.qT_r)
                        yield
                sk = load_piece(2048 + h * 256)
                for dh in range(2):
                    for th in range(2):
                        b = proj_fm(sk, dh * 128, th)
                        evac(B_.kTt[:, dh, th * 512:(th + 1) * 512], psf(b), b, B_.kTt_r)
                        yield
                sv = load_piece(4096 + h * 256)
                for n in range(8):
                    b = proj_tm(sv, 0, 256, n)
                    evac(B_.vtok[:, n, :], psf(b)[:, 0:256], b, B_.vtok_r)
                    yield
                sg = load_piece(6144 + h * 256)
                for n in range(8):
                    b = proj_tm(sg, 0, 256, n)
                    evac(B_.gs[:, n, :], psf(b)[:, 0:256], b, B_.gs_r, func=AF.Silu)
                    yield

            for _ in proj_gen(0, RB[0]):
                pass
            for h in range(8):
                B_ = RB[h % 2]
                qT, kTt, vtok, gs = B_.qT, B_.kTt, B_.vtok, B_.gs
                qT_r, kTt_r, vtok_r, gs_r = B_.qT_r, B_.kTt_r, B_.vtok_r, B_.gs_r
                nxt = proj_gen(h + 1, RB[(h + 1) % 2]) if h < 7 else iter(())

                def pull(k, nxt=nxt):
                    for _ in range(k):
                        next(nxt, None)

                for dh in range(2):
                    for n in range(8):
                        P.op("dve", lambda eng, dh=dh, n=n, h=h, qT=qT: eng.tensor_tensor(
                            qhT[:, dh, n * 128:(n + 1) * 128], qT[:, dh, n * 128:(n + 1) * 128], gq[:, h, :], ALU.mult),
                            reads=[qT_r, tabs_r], writes=[qhT_r])
                for n in range(8):
                    b = nextbank()
                    pb = psb(b)
                    for dh in range(2):
                        P.op("pe", lambda eng, dh=dh, n=n, pb=pb, kTt=kTt: eng.transpose(pb[:, dh * 128:(dh + 1) * 128], kTt[:, dh, n * 128:(n + 1) * 128], ident),
                             reads=[kTt_r, ident_r], writes=[bankreg[b]], sig=(dh == 1))
                    evac(ktok[:, n, :], pb[:, 0:256], b, ktok_r, scale=rs2[:, h:h + 1], extra_reads=[tabs_r])
                P.op("act", lambda eng, h=h: eng.copy(Sbf, ST[:, h, :, :]), reads=[ST_r[h]], writes=[Sbf_r])
                for n in range(8):
                    cs = slice(n * 128, (n + 1) * 128)
                    b1 = nextbank()
                    mm_group(psf(b1)[:, 0:128], [(kTt[:, dh, cs], qT[:, dh, cs]) for dh in range(2)], [kTt_r, qT_r], b1)
                    p = n % 2
                    P.op("dve", lambda eng, b1=b1, p=p, h=h: eng.tensor_tensor(PT[p], psf(b1)[:, 0:128], decT[:, h, :], ALU.mult),
                         reads=[bankreg[b1], tabs_r], writes=[PT_r[p]])
                    pull(1)
                    b2 = nextbank()
                    mm_group(psf(b2)[:, 0:256], [(PT[p], vtok[:, n, :])] + [(qhT[:, dh, cs], Sbf[:, dh, :]) for dh in range(2)],
                             [PT_r[p], vtok_r, qhT_r, Sbf_r], b2)
                    o = psf(b2)[:, 0:256]
                    pull(1)
                    P.op("dve", lambda eng, o=o: eng.bn_stats(SL[:, 0:6], o), reads=[bankreg[b2]], writes=[SL_r])
                    P.op("dve", lambda eng: eng.bn_aggr(SL[:, 8:10], SL[:, 0:6]), reads=[SL_r], writes=[SL_r])
                    P.op("act", lambda eng: eng.activation(SL[:, 10:11], SL[:, 9:10], AF.Sqrt, bias=EPS_AP[0], scale=1.0),
                         reads=[SL_r, small_r], writes=[SL_r])
                    P.op("dve", lambda eng: eng.reciprocal(SL[:, 10:11], SL[:, 10:11]), reads=[SL_r], writes=[SL_r])
                    P.op("dve", lambda eng: eng.scalar_tensor_tensor(out=SL[:, 11:12], in0=SL[:, 8:9], scalar=-1.0, in1=SL[:, 10:11],
                                                                   op0=ALU.mult, op1=ALU.mult), reads=[SL_r], writes=[SL_r])
                    P.op("act", lambda eng, o=o: eng.activation(yn, o, AF.Identity, bias=SL[:, 11:12], scale=SL[:, 10:11]),
                         reads=[SL_r, bankreg[b2]], writes=[yn_r])
                    P.op("dve", lambda eng, n=n, gs=gs: eng.tensor_tensor(rout[:, n, :], yn, gs[:, n, :], ALU.mult),
                         reads=[yn_r, gs_r], writes=[rout_r])
                    pull(1)
                    if n < 7:
                        for dh in range(2):
                            b3 = nextbank()
                            mm_group(psf(b3)[:, 0:256], [(ktok[:, n, dh * 128:(dh + 1) * 128], vtok[:, n, :])], [ktok_r, vtok_r], b3)
                            P.op("dve", lambda eng, b3=b3, dh=dh, h=h: eng.scalar_tensor_tensor(
                                out=ST[:, h, dh, :], in0=ST[:, h, dh, :], scalar=GAM[h] ** 128, in1=psf(b3)[:, 0:256],
                                op0=ALU.mult, op1=ALU.add), reads=[bankreg[b3], ST_r[h]], writes=[ST_r[h]])
                        P.op("act", lambda eng, h=h: eng.copy(Sbf, ST[:, h, :, :]), reads=[ST_r[h]], writes=[Sbf_r])
                for _ in nxt:
                    pass
                for n in range(8):
                    transpose_to(mst[:, 0:2, n * 128:(n + 1) * 128], [rout[:, n, e * 128:(e + 1) * 128] for e in range(2)],
                                 [rout_r], mst_r, "act" if n % 2 else "dve")
                P.dma("sp", mixT_scr.rearrange("(a p) t -> p a t", p=128)[:, 2 * h:2 * h + 2, :], mst, reads=[mst_r], writes=[mscr_r])
            P.barrier()
            SB.reset(mP1)
            cmask = ltab("cmp_mask", [2, 8, 128], BF16, parts=64)
            cbi = ltab("cmp_bias", [2, 16, 8], F32, parts=64)
            ikeep = ltab("imp_keep", [8, 32], F32)
            iadd = ltab("imp_add", [8, 32], F32)
            sbias = ltab("slc_bias", [16, 16, 8], F32)
            E2 = ltab("e2", [16, 128], BF16, parts=32)
            tri = ltab("tri", [2, 128], BF16)
            gsig = SB.take([8, 48], F32)
            gsig_r = Reg("gsig")
            sgt = load_piece(13312, ncols=48)
            for qt in range(8):
                b = proj_tm(sgt, 0, 48, qt)
                evac(gsig[:, qt, :], psf(b)[:, 0:48], b, gsig_r, func=AF.Sigmoid)
            qn = SB.take([4, 1024], BF16)
            KSo = SB.take([1024], BF16)
            VSo = SB.take([8, 130], BF16)
            KWo = SB.take([1024], BF16)
            VWo = SB.take([8, 130], BF16)
            qn_r, KSo_r, VSo_r, KWo_r, VWo_r = (Reg(n) for n in ("qn", "KSo", "VSo", "KWo", "VWo"))
            pc = [SB.take([2, 128], BF16, parts=64) for _ in range(2)]
            pc_r = [Reg("pc0"), Reg("pc1")]
            pS = [SB.take([4, 128], BF16) for _ in range(5)]
            pS_r = [[Reg("pS%d_%d" % (i, r)) for r in range(4)] for i in range(5)]
            R4 = SB.take([4, 128], BF16, parts=32)
            R4_r = Reg("R4")
            tri4 = ltab("tri4", [2, 4, 128], BF16)
            SCB = (0, 1, 2, 7)
            scb = [0]
            psi = [0]
            imp = SB.take([4, 32], F32)
            imp_r = Reg("imp")
            m8 = SB.take([16], F32)
            selb = SB.take([32], BF16)
            selb_r = Reg("selb")
            Rm = SB.take([128], BF16, parts=32)
            Rm_r = Reg("Rm")
            acc4 = SB.take([4, 128], F32)
            acc4_r = [Reg("acc%d" % i) for i in range(4)]
            ost = SB.take([4, 128], BF16)
            ost_r = Reg("ost")
            mst4 = SB.take([4, 1024], BF16)
            mst4_r = Reg("mst4")
            cf = SB.take([16], F32)
            cf_r = Reg("cf")
            P.op("dve", lambda eng: eng.memset(VSo, 1.0), writes=[VSo_r])
            P.op("dve", lambda eng: eng.memset(VWo, 1.0), writes=[VWo_r])
            SC = 128.0 ** -0.5

            def coef(b, col, gcol, k):
                P.op("dve", lambda eng: eng.tensor_scalar(cf[:, k:k + 1], psf(b)[:, col:col + 1], 1e-30, None, ALU.add),
                     reads=[bankreg[b]], writes=[cf_r])
                P.op("dve", lambda eng: eng.reciprocal(cf[:, k:k + 1], cf[:, k:k + 1]), reads=[cf_r], writes=[cf_r])
                if gcol is not None:
                    P.op("dve", lambda eng: eng.tensor_tensor(cf[:, k + 4:k + 5], cf[:, k:k + 1], gcol, ALU.mult),
                         reads=[cf_r, gsig_r], writes=[cf_r])

            for g in range(4):
                for rp in range(2):
                    s = load_piece(8192 + g * 512 + rp * 256)
                    for ri in range(2):
                        for th in range(2):
                            b = proj_fm(s, ri * 128, th)
                            evac(qn[:, rp * 2 + ri, th * 512:(th + 1) * 512], psf(b), b, qn_r, scale=SC)
                for pi, (Ko, Ko_r, Vo, Vo_r) in enumerate(((KSo, KSo_r, VSo, VSo_r), (KWo, KWo_r, VWo, VWo_r))):
                    s = wslot[0] % 2
                    wslot[0] += 1
                    for jj in range(2):
                        c0_ = 11264 + (2 * pi + jj) * 512 + g * 128
                        P.dma("pool", WR[s][:, :, jj * 128:(jj + 1) * 128],
                              w_in_t[c0_ // 128].rearrange("p (k n) -> p k n", k=NKC), writes=[WR_r[s]])
                    for th in range(2):
                        b = proj_fm(s, 0, th)
                        evac(Ko[:, th * 512:(th + 1) * 512], psf(b), b, Ko_r)
                    for n in range(8):
                        b = proj_tm(s, 128, 128, n)
                        evac(Vo[:, n, 0:128], psf(b)[:, 0:128], b, Vo_r)
                for qt in range(8):
                    qs = slice(qt * 128, (qt + 1) * 128)
                    qtc = 8 + qt
                    for r in range(4):
                        hd = g * 4 + r
                        b = nextbank(0, 2)
                        for tile, NB in ((0, 63), (1, 64)):
                            oc = psf(b)[0:NB, tile * 128:(tile + 1) * 128]
                            P.op("pe", lambda eng, oc=oc, l_=KC[:, g, tile, 0:NB], r_=qn[:, r, qs]: eng.matmul(oc, l_, r_, start=True, stop=False),
                                 reads=[KC_r, qn_r], writes=[bankreg[b]], sig=False)
                            P.op("pe", lambda eng, oc=oc, l_=ident[0:NB, 0:NB], r_=cmask[0:NB, tile, qt, :]: eng.matmul(oc, l_, r_, start=False, stop=True),
                                 reads=[ident_r, tabs_r], writes=[bankreg[b]], sig=True)
                            P.op("act", lambda eng, oc=oc, o_=pc[r % 2][0:NB, tile, :], b_=cbi[0:NB, tile, hd, qt:qt + 1]: eng.activation(
                                o_, oc, AF.Exp, bias=b_, scale=1.0),
                                reads=[bankreg[b], tabs_r], writes=[pc_r[r % 2]])
                        b2 = 2
                        mm_group(psf(b2)[:, 0:162], [(pc[r % 2][0:NB, tile, :], VC[0:NB, g, tile, :]) for tile, NB in ((0, 63), (1, 64))],
                                 [pc_r[r % 2], VC_r], b2)
                        coef(b2, 128, gsig[:, qt, hd:hd + 1], r)
                        if r == 0:
                            P.op("dve", lambda eng, r=r: eng.tensor_scalar(imp[:, 0, :], psf(2)[:, 130:162], cf[:, r:r + 1], None, ALU.mult),
                                 reads=[bankreg[2], cf_r], writes=[imp_r])
                        else:
                            P.op("dve", lambda eng, r=r: eng.scalar_tensor_tensor(out=imp[:, 0, :], in0=psf(2)[:, 130:162], scalar=cf[:, r:r + 1],
                                                                               in1=imp[:, 0, :], op0=ALU.mult, op1=ALU.add),
                                 reads=[bankreg[2], cf_r, imp_r], writes=[imp_r])
                        P.op("dve", lambda eng, r=r: eng.tensor_scalar(acc4[:, r, :], psf(2)[:, 0:128], cf[:, r + 4:r + 5], None, ALU.mult),
                             reads=[bankreg[2], cf_r], writes=[acc4_r[r]])
                    P.op("dve", lambda eng, k_=ikeep[:, qt, :]: eng.tensor_tensor(imp[:, 1, :], imp[:, 0, :], k_, ALU.mult), reads=[imp_r, tabs_r], writes=[imp_r])
                    P.op("dve", lambda eng, k_=iadd[:, qt, :]: eng.tensor_tensor(imp[:, 1, :], imp[:, 1, :], k_, ALU.add), reads=[imp_r, tabs_r], writes=[imp_r])
                    P.op("dve", lambda eng: eng.max(m8[:, 0:8], imp[:, 1, :]), reads=[imp_r], writes=[imp_r])
                    P.op("dve", lambda eng: eng.match_replace(imp[:, 2, :], m8[:, 0:8], imp[:, 1, :], -3.0e38), reads=[imp_r], writes=[imp_r])
                    P.op("dve", lambda eng: eng.max(m8[:, 8:16], imp[:, 2, :]), reads=[imp_r], writes=[imp_r])
                    P.op("dve", lambda eng: eng.tensor_scalar(imp[:, 3, :], imp[:, 1, :], m8[:, 15:16], None, ALU.is_ge), reads=[imp_r], writes=[imp_r])
                    P.op("dve", lambda eng: eng.tensor_scalar(selb, imp[:, 3, :], -NEGM, NEGM, ALU.mult, ALU.add), reads=[imp_r], writes=[selb_r])
                    bt = 7
                    tasks = []
                    for br, klist in ((2, list(range(qtc - 4, qtc + 1))), (1, list(range(0, qtc + 1)))):
                        for i, kt in enumerate(klist):
                            tasks.append((br, i, kt, i == len(klist) - 1))

                    def emit_score(task):
                        br, i, kt, lastk = task
                        b = SCB[scb[0] % len(SCB)]
                        scb[0] += 1
                        if br == 1:
                            Kap, Kr = (KSp[:, g, kt * 128:(kt + 1) * 128], KSp_r) if kt < 8 else (KSo[:, (kt - 8) * 128:(kt - 7) * 128], KSo_r)
                            Vap, Vr = (VSp[:, kt, g, :], VSp_r) if kt < 8 else (VSo[:, kt - 8, :], VSo_r)
                        else:
                            Kap, Kr = (KWp[:, g, (kt - 4) * 128:(kt - 3) * 128], KWp_r) if kt < 8 else (KWo[:, (kt - 8) * 128:(kt - 7) * 128], KWo_r)
                            Vap, Vr = (VWp[:, kt - 4, g, :], VWp_r) if kt < 8 else (VWo[:, kt - 8, :], VWo_r)
                        extra = []
                        if br == 1:
                            extra.append((E2[:, kt, :], R4, [tabs_r, R4_r]))
                        if kt == qtc:
                            extra.append((ident, tri4[:, 0, :, :], [ident_r, tabs_r]))
                        if br == 2 and kt == qtc - 4:
                            extra.append((ident, tri4[:, 1, :, :], [ident_r, tabs_r]))
                        sc = psf(b).rearrange("p (r t) -> p r t", r=4)
                        P.op("pe", lambda eng, sc=sc, Kap=Kap, q_=qn[:, 0:4, qs], ne=len(extra): eng.matmul(sc, Kap, q_, start=True, stop=(ne == 0)),
                             reads=[Kr, qn_r], writes=[bankreg[b]], sig=(len(extra) == 0))
                        for ei, (l_, r_, rg_) in enumerate(extra):
                            last = ei == len(extra) - 1
                            P.op("pe", lambda eng, sc=sc, l_=l_, r_=r_, last=last: eng.matmul(sc, l_, r_, start=False, stop=last),
                                 reads=rg_, writes=[bankreg[b]], sig=last)
                        pi_ = psi[0] % len(pS)
                        psi[0] += 1
                        for r in range(4):
                            P.op("act", lambda eng, b=b, r=r, pi_=pi_, b_=sbias[:, g * 4 + r, kt, qt:qt + 1]: eng.activation(
                                pS[pi_][:, r, :], psf(b)[:, r * 128:(r + 1) * 128], AF.Exp, bias=b_, scale=1.0),
                                reads=[bankreg[b], tabs_r], writes=[pS_r[pi_][r]])
                        return (br, i, lastk, pi_, Vap, Vr)

                    def emit_pv(info):
                        br, i, lastk, pi_, Vap, Vr = info
                        for r in range(4):
                            bo = 3 + r
                            hd = g * 4 + r
                            P.op("pe", lambda eng, bo=bo, r=r, pi_=pi_, Vap=Vap, i=i, lastk=lastk: eng.matmul(psf(bo)[:, 0:130], pS[pi_][:, r, :], Vap, start=(i == 0), stop=lastk),
                                 reads=[pS_r[pi_][r], Vr], writes=[bankreg[bo]], sig=lastk)
                            if lastk:
                                coef(bo, 128, gsig[:, qt, br * 16 + hd:br * 16 + hd + 1], br)
                                P.op("dve", lambda eng, bo=bo, br=br, r=r: eng.scalar_tensor_tensor(
                                    out=acc4[:, r, :], in0=psf(bo)[:, 0:128], scalar=cf[:, br + 4:br + 5], in1=acc4[:, r, :], op0=ALU.mult, op1=ALU.add),
                                    reads=[bankreg[bo], cf_r, acc4_r[r]], writes=[acc4_r[r]])
                                if br == 1:
                                    P.op("act", lambda eng, r=r: eng.copy(ost[:, r, :], acc4[:, r, :]), reads=[acc4_r[r]], writes=[ost_r])

                    LAG = 3
                    pend = []
                    for task in tasks:
                        if task[0] == 1 and task[1] == 0:
                            P.op("pe", lambda eng: eng.transpose(psb(bt)[0:32, 0:128], selb, ident), reads=[selb_r, ident_r], writes=[bankreg[bt]])
                            for r in range(4):
                                if r % 2 == 0:
                                    P.op("act", lambda eng, r=r: eng.copy(R4[:, r, :], psb(bt)[0:32, 0:128]), reads=[bankreg[bt]], writes=[R4_r])
                                else:
                                    P.op("dve", lambda eng, r=r: eng.tensor_copy(R4[:, r, :], psb(bt)[0:32, 0:128]), reads=[bankreg[bt]], writes=[R4_r])
                        pend.append(emit_score(task))
                        if len(pend) > LAG:
                            emit_pv(pend.pop(0))
                    while pend:
                        emit_pv(pend.pop(0))
                    transpose_to(mst4[:, 0:4, qs], [ost[:, r, :] for r in range(4)], [ost_r], mst4_r, "dve")
                P.dma("sp", mixT_scr.rearrange("(a p) t -> p a t", p=128)[:, 16 + 4 * g:20 + 4 * g, :], mst4, reads=[mst4_r], writes=[mscr_r])
            P.barrier()
            SB.reset(mA)
            return mscr_r

        P.op("dve", lambda eng: eng.memset(small[:, 0:1], EPS), writes=[small_r])
        EPS_AP[0] = small[:, 0:1]
        mscr_r = None
        if cfg.get("phaseA", True):
            mscr_r = phase_a()
        mix_src = mix_in if not cfg.get("phaseA", True) else mixT_scr
        if not cfg.get("dbgA"):
            xattn_kv()
            for tb in range(cfg.get("ntb", 2)):
                token_block(tb, mix_src)
        P.final_wait("sp")
        P.replay()
    return nc


def _tables(hf):
    f64 = np.float64
    t = {}
    gam = 1.0 - 2.0 ** (-5.0 - np.arange(8, dtype=f64))
    p = np.arange(128, dtype=f64)
    n = np.arange(8, dtype=f64)
    t["ret_rs1"] = (gam[None, :, None] ** (1023.0 - (128.0 * n[None, None, :] + p[:, None, None])) / 16.0)
    dcs = p[None, :] - p[:, None]
    dec = np.where(dcs[:, None, :] >= 0, gam[None, :, None] ** np.maximum(dcs[:, None, :], 0.0), 0.0) / 16.0
    t["ret_decT"] = dec
    t["ret_gq"] = np.broadcast_to(gam[None, :, None] ** (p[None, None, :] + 1.0), (128, 8, 128))
    t["ret_rs2"] = gam[None, :] ** (127.0 - p[:, None]) / 16.0
    slopes = 2.0 ** (-8.0 * np.arange(1, 17, dtype=f64) / 16.0)
    vstart = 1024 * (1 - hf)
    qt = np.arange(8)
    bmid = 1024.0 + 128.0 * qt + 64.0
    pc_ = np.arange(64)
    cidx = np.stack([pc_, 63 + pc_], axis=1)
    cend = 16 * cidx + 31
    ctx_t = 1024 + 128 * qt[:, None] + np.arange(128)[None, :]
    valid = (cend[:, :, None, None] <= ctx_t[None, None]) & (16 * cidx[:, :, None, None] >= vstart)
    valid[63, 0] = False
    t["cmp_mask"] = np.where(valid, 0.0, NEGM)
    t["cmp_bias"] = slopes[None, None, :, None] * (cend[:, :, None, None] - bmid[None, None, None, :])
    s_ = np.arange(32)
    c0 = 16.0 * cidx[:, :, None]
    ov = np.clip(np.minimum(c0 + 32, 64.0 * s_ + 64) - np.maximum(c0, 64.0 * s_), 0, None) / 32.0
    vci = np.zeros((64, 4, 2, 162), f64)
    vci[:, :, :, 128:130] = 1.0
    vci[:, :, :, 130:162] = ov[:, None, :, :]
    t["vc_init"] = vci.astype(ml_dtypes.bfloat16)
    ctxp = 1024 + 128 * qt[None, :, None] + np.arange(128)[:, None, None]
    cur = ctxp // 64
    blk = np.arange(32)[None, None, :]
    blk0 = 16 * (1 - hf)
    forced = (blk == blk0) | (blk == cur) | (blk == cur - 1)
    dead = (blk > cur) | (blk < blk0)
    t["imp_keep"] = np.where(forced | dead, 0.0, 1.0)
    t["imp_add"] = np.where(forced, 1e9, np.where(dead, -1e9, 0.0))
    kt = np.arange(16)
    kpos = 128 * kt[None, None, :, None] + np.arange(128)[:, None, None, None]
    sb = slopes[None, :, None, None] * (kpos - bmid[None, None, None, :])
    t["slc_bias"] = np.where(kpos >= vstart, sb, -1e30)
    key = np.arange(128)
    t["e2"] = (np.arange(32)[:, None, None] == (2 * kt[None, :, None] + key[None, None, :] // 64)).astype(f64)
    tri = np.zeros((128, 2, 128), f64)
    tri[:, 0, :] = np.where(key[:, None] <= key[None, :], 0.0, NEGM)
    tri[:, 1, :] = np.where(key[:, None] > key[None, :], 0.0, NEGM)
    t["tri"] = tri
    t["tri4"] = np.broadcast_to(tri[:, :, None, :], (128, 2, 4, 128))
    return {k: (np.ascontiguousarray(v) if v.dtype == ml_dtypes.bfloat16 else np.ascontiguousarray(v, dtype=np.float32)) for k, v in t.items()}


def make_maps(inputs, cores, cfg):
    x = np.asarray(inputs["x"])
    maps = []
    tabs = [_tables(0), _tables(1)]
    ident = np.eye(128, dtype=np.float32)
    shared = {}
    if cfg.get("phaseA", True):
        wi = np.asarray(inputs["w_in"])[0]
        shared.update(w_in_t=np.ascontiguousarray(wi[:, :13312].reshape(NKC, 128, 104, 128).transpose(2, 1, 0, 3)).reshape(104, 128, NKC * 128),
                      w_gates=np.ascontiguousarray(wi[:, 13312:].reshape(NKC, 128, 48).transpose(1, 0, 2)), cmp_w1=np.asarray(inputs["cmp_w1"])[0], cmp_w2=np.asarray(inputs["cmp_w2"])[0],
                      cmp_posT=np.ascontiguousarray(np.asarray(inputs["cmp_pos"])[0].transpose(0, 2, 1)))
    if not cfg.get("dbgA"):
        def tl(w, n):
            k = w.shape[1] // n
            return np.ascontiguousarray(w.reshape(NKC, 128, k, n).transpose(2, 1, 0, 3)).reshape(k, 128, NKC * n)
        shared.update(w_out_t=tl(np.asarray(inputs["w_out"])[0], 512), xa_wq_t=tl(np.asarray(inputs["xa_wq"])[0], 128),
                      xa_wkv_t=tl(np.asarray(inputs["xa_wkv"])[0], 128),
                      xa_wo=np.asarray(inputs["xa_wo"])[0], w_ff1_t=tl(np.asarray(inputs["w_ff1"])[0], 128), w_ff2=np.asarray(inputs["w_ff2"])[0],
                      ln_g=np.asarray(inputs["ln_g"])[0], ln_b=np.asarray(inputs["ln_b"])[0])
    for c in cores:
        b, hf = c // 2, c % 2
        m = dict(shared)
        m["ident"] = ident
        own = x[b, hf * 1024:(hf + 1) * 1024]
        if cfg.get("phaseA", True):
            m["xT_own"] = np.ascontiguousarray(own.T)
            m["xT_prev"] = np.ascontiguousarray(x[b, 0:1024].T) if hf == 1 else np.zeros((D, 1024), np.float32)
            m.update(tabs[hf])
        if not cfg.get("dbgA"):
            m["x_own"] = np.ascontiguousarray(own)
            m["memT"] = np.ascontiguousarray(np.asarray(inputs["mem"])[b].T)
        maps.append(m)
    return maps


def kernel(**inputs):
    cfg = dict(phaseA=True)
    nc = build(cfg)
    cores = list(range(8))
    maps = make_maps(inputs, cores, cfg)
    res = run_bass_kernel_spmd(nc, maps, core_ids=cores)
    out = np.empty((4, 2048, D), np.float32)
    for c in cores:
        out[c // 2, (c % 2) * 1024:(c % 2 + 1) * 1024] = res.results[c]["y"]
    return out
```
